# Optimizing a Trainium2 kernel written in Bass

```python
import math
import jax, jax.numpy as jnp
from jax import lax
import numpy as np

D_MODEL = 1024
BATCH = 8
SEQ = 2048
DEPTH = 1

PLE_DIM = 256
DIFF_HEADS = 4
DIFF_QK_DIM = 64
DIFF_V_DIM = 2 * DIFF_QK_DIM
DIFF_QK_WIDTH = DIFF_HEADS * 2 * DIFF_QK_DIM
DIFF_WIDTH = DIFF_HEADS * DIFF_V_DIM
SSM_WIDTH = D_MODEL - DIFF_WIDTH
SSM_GROUP = 16
SSM_GROUPS = SSM_WIDTH // SSM_GROUP
SSM_STATE = 64
DT_MIN = 1e-3
DT_MAX = 1e-1
MIX_WIDTH = DIFF_WIDTH + SSM_WIDTH
IN_COLS = 2 * DIFF_QK_WIDTH + DIFF_WIDTH + SSM_WIDTH
D_FF = 2816
CONV_WIDTH = 3
Q_BLOCK = 128
LN_EPS = 1e-5
DEEPNORM_ALPHA = (2 * DEPTH) ** 0.25
DEEPNORM_BETA = (8 * DEPTH) ** -0.25

kernel_name = "hybrid_diffattn_s5_convffn_deepnorm"


def layer_norm(x, g, b):
    xf = x.astype(jnp.float32)
    mu = jnp.mean(xf, axis=-1, keepdims=True)
    var = jnp.mean(jnp.square(xf - mu), axis=-1, keepdims=True)
    return ((xf - mu) * lax.rsqrt(var + LN_EPS) * g + b).astype(x.dtype)


def rms_norm(x, g):
    xf = x.astype(jnp.float32)
    ms = jnp.mean(jnp.square(xf), axis=-1, keepdims=True)
    return (xf * lax.rsqrt(ms + LN_EPS) * g).astype(x.dtype)


def diff_attention(q, k, v, lam, subln_g, lam_init):
    B, L, H, _, dq = q.shape
    E = v.shape[-1]
    nb = L // Q_BLOCK
    qb = q.reshape(B, nb, Q_BLOCK, H, 2, dq).transpose(1, 0, 2, 3, 4, 5)
    k_pos = jnp.arange(L)
    scale = DIFF_QK_DIM ** -0.5

    def block(args):
        q_blk, start = args
        s = jnp.einsum('bqhcd,bkhcd->bhcqk', q_blk, k).astype(jnp.float32) * scale
        q_pos = start + jnp.arange(Q_BLOCK)
        mask = k_pos[None, :] <= q_pos[:, None]
        s = jnp.where(mask, s, -jnp.inf)
        pr = jax.nn.softmax(s, axis=-1)
        a = pr[:, :, 0] - lam * pr[:, :, 1]
        return jnp.einsum('bhqk,bkhe->bqhe', a.astype(v.dtype), v)

    starts = jnp.arange(nb) * Q_BLOCK
    o = lax.map(block, (qb, starts))
    o = o.transpose(1, 0, 2, 3, 4).reshape(B, L, H, E)
    o = rms_norm(o, subln_g) * (1.0 - lam_init)
    return o.reshape(B, L, H * E)


def s5_mixer(u, lam_re, lam_im, log_dt, b_re, b_im, c_re, c_im, d_skip, w_glu, b_glu):
    Bsz, L, W = u.shape
    f32 = jnp.float32
    uf = u.astype(f32).reshape(Bsz, L, SSM_GROUPS, SSM_GROUP)
    dt = jnp.exp(log_dt.astype(f32))[:, None]
    lr = lam_re.astype(f32)
    li = lam_im.astype(f32)
    mag = jnp.exp(lr * dt)
    ab_re = mag * jnp.cos(li * dt)
    ab_im = mag * jnp.sin(li * dt)
    den = lr * lr + li * li
    zr = ab_re - 1.0
    zi = ab_im
    fr = (zr * lr + zi * li) / den
    fi = (zi * lr - zr * li) / den
    br = b_re.astype(f32)
    bi = b_im.astype(f32)
    bb_re = fr[..., None] * br - fi[..., None] * bi
    bb_im = fr[..., None] * bi + fi[..., None] * br
    bu_re = jnp.einsum('blgh,gph->blgp', uf, bb_re)
    bu_im = jnp.einsum('blgh,gph->blgp', uf, bb_im)
    a_re = jnp.broadcast_to(ab_re, bu_re.shape)
    a_im = jnp.broadcast_to(ab_im, bu_im.shape)

    def combine(e1, e2):
        a1r, a1i, b1r, b1i = e1
        a2r, a2i, b2r, b2i = e2
        return (a1r * a2r - a1i * a2i,
                a1r * a2i + a1i * a2r,
                a2r * b1r - a2i * b1i + b2r,
                a2r * b1i + a2i * b1r + b2i)

    _, _, s_re, s_im = lax.associative_scan(combine, (a_re, a_im, bu_re, bu_im), axis=1)
    y = (jnp.einsum('blgp,ghp->blgh', s_re, c_re.astype(f32))
         - jnp.einsum('blgp,ghp->blgh', s_im, c_im.astype(f32)))
    y = (y + d_skip.astype(f32) * uf).reshape(Bsz, L, W)
    y = jax.nn.gelu(y)
    y = y * jax.nn.sigmoid(y @ w_glu.astype(f32) + b_glu.astype(f32))
    return y.astype(u.dtype)


def conv_ffn(x, w_up, conv_w, conv_b, w_down):
    L = x.shape[1]
    h = x @ w_up
    hp = jnp.pad(h, ((0, 0), (CONV_WIDTH - 1, 0), (0, 0)))
    hc = conv_b + sum(hp[:, j:j + L] * conv_w[j] for j in range(CONV_WIDTH))
    g, val = jnp.split(hc, 2, axis=-1)
    return (jax.nn.silu(g) * val) @ w_down


def _normal(key, shape, std):
    return jax.random.normal(key, shape, jnp.float32) * std


def setup_inputs(seed: int = 0) -> dict:
    key = jax.random.key(seed)
    ks = jax.random.split(key, 32)
    N = DEPTH
    G, P, H = SSM_GROUPS, SSM_STATE, SSM_GROUP
    x = _normal(ks[0], (BATCH, SEQ, D_MODEL), 1.0)
    p = _normal(ks[1], (DEPTH, BATCH, SEQ, PLE_DIM), 1.0)
    w_in = _normal(ks[2], (N, D_MODEL, IN_COLS), D_MODEL ** -0.5)
    diff_lambda_q1 = _normal(ks[3], (N, DIFF_QK_DIM), 0.1)
    diff_lambda_k1 = _normal(ks[4], (N, DIFF_QK_DIM), 0.1)
    diff_lambda_q2 = _normal(ks[5], (N, DIFF_QK_DIM), 0.1)
    diff_lambda_k2 = _normal(ks[6], (N, DIFF_QK_DIM), 0.1)
    diff_subln_g = 1.0 + _normal(ks[7], (N, DIFF_V_DIM), 0.01)
    ssm_lambda_re = -0.5 + _normal(ks[8], (N, G, P), 0.01)
    ssm_lambda_im = jnp.pi * jnp.arange(P, dtype=jnp.float32) + _normal(ks[9], (N, G, P), 0.01)
    ssm_log_dt = jax.random.uniform(ks[10], (N, G), jnp.float32, math.log(DT_MIN), math.log(DT_MAX))
    ssm_b_re = _normal(ks[11], (N, G, P, H), (2.0 * H) ** -0.5)
    ssm_b_im = _normal(ks[12], (N, G, P, H), (2.0 * H) ** -0.5)
    ssm_c_re = _normal(ks[13], (N, G, H, P), 0.5)
    ssm_c_im = _normal(ks[14], (N, G, H, P), 0.5)
    ssm_d = _normal(ks[15], (N, G, H), 1.0)
    ssm_w_glu = _normal(ks[16], (N, SSM_WIDTH, SSM_WIDTH), SSM_WIDTH ** -0.5)
    ssm_b_glu = _normal(ks[17], (N, SSM_WIDTH), 0.01)
    w_o = _normal(ks[18], (N, MIX_WIDTH, D_MODEL), MIX_WIDTH ** -0.5 * DEEPNORM_BETA)
    ln1_g = 1.0 + _normal(ks[19], (N, D_MODEL), 0.01)
    ln1_b = _normal(ks[20], (N, D_MODEL), 0.01)
    ffn_w_up = _normal(ks[21], (N, D_MODEL, 2 * D_FF), D_MODEL ** -0.5)
    ffn_conv_w = _normal(ks[22], (N, CONV_WIDTH, 2 * D_FF), CONV_WIDTH ** -0.5)
    ffn_conv_b = _normal(ks[23], (N, 2 * D_FF), 0.01)
    ffn_w_down = _normal(ks[24], (N, D_FF, D_MODEL), D_FF ** -0.5 * DEEPNORM_BETA)
    w_ple = _normal(ks[25], (N, PLE_DIM, D_MODEL), PLE_DIM ** -0.5)
    w_ple_gate = _normal(ks[26], (N, D_MODEL, D_MODEL), D_MODEL ** -0.5)
    ln2_g = 1.0 + _normal(ks[27], (N, D_MODEL), 0.01)
    ln2_b = _normal(ks[28], (N, D_MODEL), 0.01)
    return {"x": x, "p": p, "w_in": w_in,
            "diff_lambda_q1": diff_lambda_q1, "diff_lambda_k1": diff_lambda_k1,
            "diff_lambda_q2": diff_lambda_q2, "diff_lambda_k2": diff_lambda_k2,
            "diff_subln_g": diff_subln_g,
            "ssm_lambda_re": ssm_lambda_re, "ssm_lambda_im": ssm_lambda_im, "ssm_log_dt": ssm_log_dt,
            "ssm_b_re": ssm_b_re, "ssm_b_im": ssm_b_im, "ssm_c_re": ssm_c_re, "ssm_c_im": ssm_c_im,
            "ssm_d": ssm_d, "ssm_w_glu": ssm_w_glu, "ssm_b_glu": ssm_b_glu,
            "w_o": w_o, "ln1_g": ln1_g, "ln1_b": ln1_b,
            "ffn_w_up": ffn_w_up, "ffn_conv_w": ffn_conv_w, "ffn_conv_b": ffn_conv_b, "ffn_w_down": ffn_w_down,
            "w_ple": w_ple, "w_ple_gate": w_ple_gate, "ln2_g": ln2_g, "ln2_b": ln2_b}


def reference(x, p, w_in, diff_lambda_q1, diff_lambda_k1, diff_lambda_q2, diff_lambda_k2, diff_subln_g,
              ssm_lambda_re, ssm_lambda_im, ssm_log_dt, ssm_b_re, ssm_b_im, ssm_c_re, ssm_c_im,
              ssm_d, ssm_w_glu, ssm_b_glu, w_o, ln1_g, ln1_b,
              ffn_w_up, ffn_conv_w, ffn_conv_b, ffn_w_down, w_ple, w_ple_gate, ln2_g, ln2_b):
    B, L, _ = x.shape
    for i in range(DEPTH):
        lam_init = 0.8 - 0.6 * math.exp(-0.3 * i)
        h = x @ w_in[i]
        q = h[..., :DIFF_QK_WIDTH].reshape(B, L, DIFF_HEADS, 2, DIFF_QK_DIM)
        k = h[..., DIFF_QK_WIDTH:2 * DIFF_QK_WIDTH].reshape(B, L, DIFF_HEADS, 2, DIFF_QK_DIM)
        v = h[..., 2 * DIFF_QK_WIDTH:2 * DIFF_QK_WIDTH + DIFF_WIDTH].reshape(B, L, DIFF_HEADS, DIFF_V_DIM)
        u = h[..., 2 * DIFF_QK_WIDTH + DIFF_WIDTH:]
        lam = (jnp.exp(jnp.sum(diff_lambda_q1[i].astype(jnp.float32) * diff_lambda_k1[i].astype(jnp.float32)))
               - jnp.exp(jnp.sum(diff_lambda_q2[i].astype(jnp.float32) * diff_lambda_k2[i].astype(jnp.float32)))
               + lam_init)
        attn = diff_attention(q, k, v, lam, diff_subln_g[i], lam_init)
        ssm = s5_mixer(u, ssm_lambda_re[i], ssm_lambda_im[i], ssm_log_dt[i], ssm_b_re[i], ssm_b_im[i],
                       ssm_c_re[i], ssm_c_im[i], ssm_d[i], ssm_w_glu[i], ssm_b_glu[i])
        mix = jnp.concatenate([attn, ssm.astype(attn.dtype)], axis=-1) @ w_o[i]
        x = layer_norm(DEEPNORM_ALPHA * x + mix, ln1_g[i], ln1_b[i])
        f = conv_ffn(x, ffn_w_up[i], ffn_conv_w[i], ffn_conv_b[i], ffn_w_down[i])
        ple = (p[i] @ w_ple[i]) * jax.nn.sigmoid(x @ w_ple_gate[i])
        x = layer_norm(DEEPNORM_ALPHA * x + f + ple, ln2_g[i], ln2_b[i])
    return x
```

```python
import math
from contextlib import ExitStack
import numpy as np
import concourse.bass as bass
import concourse.mybir as mybir
from concourse.bass_utils import run_bass_kernel_spmd
from concourse.alu_op_type import AluOpType as ALU

F32 = mybir.dt.float32
BF16 = mybir.dt.bfloat16
I32 = mybir.dt.int32
AF = mybir.ActivationFunctionType

L = 2048
D = 1024
DFF = 2816
NPAIR = DFF // 128
LN_EPS = 1e-5
ALPHA = 2.0 ** 0.25
LAM_INIT = 0.8 - 0.6 * math.exp(0.0)
TWO_PI_LO = 6.28318

_c = {}
_off = 0
for _n, _w in (("lamq", 256), ("gcol", 1), ("lr", 16), ("li", 16), ("ldt", 16), ("bre", 256), ("bim", 256),
               ("cre", 256), ("cim", 256), ("dl", 32), ("bglu", 4), ("cw", 132), ("cb", 44),
               ("ident", 128), ("maskb", 128), ("tmask", 128), ("kvec", 25), ("cidx", 128), ("i2", 64), ("g1c", 8), ("b1c", 8)):
    _c[_n] = (_off, _off + _w)
    _off += _w
NCST = _off
KV = [0, -1, -2, -3, -4, -5, -6, -7] + list(range(0, 17))


class Buf:
    __slots__ = ("w", "r", "name")

    def __init__(self, name=""):
        self.w = None
        self.r = {}
        self.name = name


class Eng:
    def __init__(self, name, handle, sem):
        self.name = name
        self.h = handle
        self.sem = sem
        self.count = 0
        self.waited = {}


class FW:
    def __init__(self, nc, stack):
        self.nc = nc
        self.stack = stack
        self.engs = {}
        for name, h in (("pe", nc.tensor), ("act", nc.scalar), ("dve", nc.vector),
                        ("pool", nc.gpsimd), ("sp", nc.sync)):
            sem = stack.enter_context(nc.semaphore("sem_" + name))
            self.engs[name] = Eng(name, h, sem)
        self.dma_sems = {}

    def dma_slot(self, name):
        if name not in self.dma_sems:
            sem = self.stack.enter_context(self.nc.semaphore("dsem_" + name))
            self.dma_sems[name] = [sem, 0]
        return self.dma_sems[name]

    def _wait(self, eng, tok):
        key, sem, val = tok
        if eng.waited.get(key, 0) >= val:
            return
        eng.h.wait_ge(sem, val)
        eng.waited[key] = val

    def _deps(self, eng, reads, writes):
        best = {}

        def add(t):
            if t is None:
                return
            if eng.name == "pe" and t[0] == "pe":
                return
            if t[0] not in best or best[t[0]][2] < t[2]:
                best[t[0]] = t
        for b in reads:
            add(b.w)
        for b in writes:
            add(b.w)
            for k, t in b.r.items():
                add(t)
        for t in best.values():
            self._wait(eng, t)

    def _commit(self, tok, reads, writes):
        for b in reads:
            b.r[tok[0]] = tok
        for b in writes:
            b.w = tok
            b.r = {}

    def op(self, engname, fn, reads=(), writes=(), acc=False):
        eng = self.engs[engname]
        if engname == "pe" and not acc:
            for b in writes:
                if b.w is not None and b.w[0] == "pe" and not b.r:
                    raise RuntimeError(f"PSUM clobber: PE overwrites {b.name} before anyone read the previous PE result")
        self._deps(eng, reads, writes)
        ins = fn(eng.h)
        ins.then_inc(eng.sem, 1)
        eng.count += 1
        tok = (eng.name, eng.sem, eng.count)
        self._commit(tok, reads, writes)
        return tok

    def dma(self, qname, slot, out, in_, reads=(), writes=(), **kw):
        eng = self.engs[qname]
        self._deps(eng, reads, writes)
        s = self.dma_slot(slot)
        ins = eng.h.dma_start(out=out, in_=in_, **kw)
        ins.then_inc(s[0], 16)
        s[1] += 16
        tok = ("dma_" + slot, s[0], s[1])
        self._commit(tok, reads, writes)
        return tok

    def barrier(self):
        toks = [(e.name, e.sem, e.count) for e in self.engs.values() if e.count > 0]
        toks += [("dma_" + k, s[0], s[1]) for k, s in self.dma_sems.items() if s[1] > 0]
        for e in self.engs.values():
            for t in toks:
                if t[0] != e.name:
                    self._wait(e, t)


class Arena:
    def __init__(self, tensor, nbytes):
        self.t = tensor
        self.n = nbytes
        self.off = 0
        self.top = nbytes
        self.marks = []

    def alloc_top(self, cols, dt):
        esz = 4 if dt in (F32, I32) else 2
        nb = (cols * esz + 31) // 32 * 32
        assert self.top - nb >= self.off, ("SBUF arena overflow (top)", self.off, nb, self.top)
        self.top -= nb
        a = self.t[:, self.top // 4:(self.top + nb) // 4]
        if dt != F32:
            a = a.bitcast(dt)
        return a[:, 0:cols]

    def alloc(self, cols, dt):
        esz = 4 if dt in (F32, I32) else 2
        nb = (cols * esz + 31) // 32 * 32
        assert self.off + nb <= self.top, ("SBUF arena overflow", self.off, nb, self.top)
        a = self.t[:, self.off // 4:(self.off + nb) // 4]
        self.off += nb
        if dt != F32:
            a = a.bitcast(dt)
        return a[:, 0:cols]

    def mark(self):
        return self.off

    def reset(self, m):
        self.off = m


def build_nc(debug=None):
    nc = bass.Bass("TRN2", target_bir_lowering=False)
    dram = lambda n, s, k="ExternalInput": nc.dram_tensor(n, s, F32, kind=k).ap()
    xT_d = dram("xT", [D, L])
    x_d = dram("x", [L, D])
    pT_d = dram("pT", [256, L])
    win_d = dram("w_in", [D, 2048])
    wo_d = dram("w_o", [D, D])
    wglu_d = dram("w_glu", [512, 512])
    wup_d = dram("w_up", [D, 2 * DFF])
    wdn_d = dram("w_down", [DFF, D])
    wple_d = dram("w_ple", [256, D])
    wg_d = dram("w_gate", [D, D])
    cst_d = dram("cst", [128, NCST])
    lnc_d = dram("lnc", [128, 4 * D])
    out_d = dram("out", [L, D], "ExternalOutput")
    dbg_d = dram("dbg", [D, L], "ExternalOutput") if debug else None

    with ExitStack() as st:
        fw = FW(nc, st)
        ARENA_BYTES = 212736
        arena_t = st.enter_context(nc.sbuf_tensor("arena", [128, ARENA_BYTES // 4], F32))
        ar = Arena(arena_t, ARENA_BYTES)
        banks = [st.enter_context(nc.psum_tensor(f"bank{i}", [128, 512], F32)) for i in range(8)]
        PB = [Buf(f"bank{i}") for i in range(8)]
        pbf = lambda i: banks[i][:, :].bitcast(BF16)

        def A(cols, dt, name=""):
            return ar.alloc(cols, dt), Buf(name)

        NCB = 192
        cstB, BcstB = A(NCB, F32, "cstB")
        mixT, _ = A(8 * L, BF16, "mixT")
        mixT = mixT.rearrange("p (k t) -> p k t", k=8)
        Bmix = [[Buf(f"mix{k}_{g}") for g in range(4)] for k in range(8)]
        identb, Bidb = A(128, BF16)
        maskb, Bmkb = A(128, BF16)
        fw.dma("sp", "cstB", cstB[:, 0:176], cst_d[:, _c["cw"][0]:_c["cb"][1]], writes=[BcstB])
        fw.dma("sp", "cstB", cstB[:, 176:192], cst_d[:, _c["g1c"][0]:_c["b1c"][1]], writes=[BcstB])
        _cb = {"cw": (0, 132), "cb": (132, 176), "g1c": (176, 184), "b1c": (184, 192)}
        CB = lambda n: cstB[:, _cb[n][0]:_cb[n][1]]
        PM = ar.mark()
        cst, Bcst = A(NCST, F32, "cst")
        fw.dma("sp", "cst", cst, cst_d, writes=[Bcst])
        C = lambda n: cst[:, _c[n][0]:_c[n][1]]
        fw.op("dve", lambda e: e.tensor_copy(out=identb, in_=C("ident")), reads=[Bcst], writes=[Bidb])
        fw.op("dve", lambda e: e.tensor_copy(out=maskb, in_=C("maskb")), reads=[Bcst], writes=[Bmkb])

        xT, _ = A(8 * L, BF16, "xT")
        xT = xT.rearrange("p (k t) -> p k t", k=8)
        BxT = Buf("xT")
        for kb in range(8):
            fw.dma("pool", "xT", xT[:, kb, :], xT_d[kb * 128:(kb + 1) * 128, :], writes=[BxT], max_dma_last_dim=4096)
        wslot = []
        for i in range(2):
            t, b = A(8 * 512, BF16, f"wslot{i}")
            wslot.append((t.rearrange("p (k n) -> p k n", k=8), b))
        wglu, Bwglu = A(4 * 512, BF16, "wglu")
        wglu = wglu.rearrange("p (k n) -> p k n", k=4)
        fw.dma("pool", "wglu", wglu, wglu_d.rearrange("(k p) n -> p k n", p=128), writes=[Bwglu])
        def load_w(slot_i, src_cols_list):
            t, b = wslot[slot_i]
            o = 0
            for (c0, c1) in src_cols_list:
                fw.dma("pool", f"ws{slot_i}", t[:, :, o:o + (c1 - c0)],
                       win_d[:, c0:c1].rearrange("(k p) n -> p k n", p=128), writes=[b])
                o += c1 - c0
            return t, b

        head_cols = lambda h: [(128 * h, 128 * h + 128), (512 + 128 * h, 512 + 128 * h + 128), (1024 + 128 * h, 1024 + 128 * h + 128)]
        wtU, BwU = load_w(0, [(1536, 2048)])
        head_w = {0: load_w(1, head_cols(0))}

        PM_A = ar.mark()
        def T_(cols, name=""):
            return A(cols, F32, name)
        s_dt, Bs = T_(16)
        s_a, _ = T_(16); s_th, _ = T_(16); s_q, _ = T_(16); s_fq, _ = T_(16)
        s_i, _ = A(16, I32); s_n, _ = T_(16)
        V = lambda e: e
        dve = lambda fn, r, w: fw.op("dve", fn, reads=r, writes=w)
        act = lambda fn, r, w: fw.op("act", fn, reads=r, writes=w)
        act(lambda e: e.activation(out=s_dt, in_=C("ldt"), func=AF.Exp), [Bcst], [Bs])
        dve(lambda e: e.tensor_tensor(out=s_a, in0=C("lr"), in1=s_dt, op=ALU.mult), [Bcst, Bs], [Bs])
        dve(lambda e: e.tensor_tensor(out=s_th, in0=C("li"), in1=s_dt, op=ALU.mult), [Bcst, Bs], [Bs])
        dve(lambda e: e.tensor_scalar(out=s_q, in0=s_th, scalar1=1.0 / (2 * math.pi), scalar2=None, op0=ALU.mult), [Bs], [Bs])

        def frac(dst, src, itmp, ftmp, shape_reads):
            dve(lambda e: e.tensor_copy(out=itmp, in_=src), shape_reads, shape_reads)
            dve(lambda e: e.tensor_copy(out=ftmp, in_=itmp), shape_reads, shape_reads)
            dve(lambda e: e.tensor_tensor(out=dst, in0=src, in1=ftmp, op=ALU.subtract), shape_reads, shape_reads)
        frac(s_fq, s_q, s_i, s_n, [Bs])
        NK = 25
        pwr, _ = T_(16 * NK); pwi, _ = T_(16 * NK); ph, _ = T_(16 * NK); ph2, _ = T_(16 * NK)
        phi_, _ = A(16 * NK, I32); ex, _ = T_(16 * NK)
        v3 = lambda t: t.rearrange("p (g k) -> p g k", g=16)
        kv3 = C("kvec").unsqueeze(1).to_broadcast([128, 16, NK])
        bc3 = lambda t, n: t.unsqueeze(2).to_broadcast([128, 16, n])
        dve(lambda e: e.tensor_tensor(out=v3(ph), in0=kv3, in1=bc3(s_fq, NK), op=ALU.mult), [Bcst, Bs], [Bs])
        frac(ph, ph, phi_, ph2, [Bs])
        act(lambda e: e.activation(out=pwi, in_=ph, func=AF.Sin, scale=TWO_PI_LO), [Bs], [Bs])
        dve(lambda e: e.tensor_scalar(out=ph, in0=ph, scalar1=0.25, scalar2=None, op0=ALU.add), [Bs], [Bs])
        frac(ph, ph, phi_, ph2, [Bs])
        act(lambda e: e.activation(out=pwr, in_=ph, func=AF.Sin, scale=TWO_PI_LO), [Bs], [Bs])
        dve(lambda e: e.tensor_tensor(out=v3(ph2), in0=kv3, in1=bc3(s_a, NK), op=ALU.mult), [Bcst, Bs], [Bs])
        act(lambda e: e.activation(out=ex, in_=ph2, func=AF.Exp), [Bs], [Bs])
        dve(lambda e: e.tensor_tensor(out=pwr, in0=pwr, in1=ex, op=ALU.mult), [Bs], [Bs])
        dve(lambda e: e.tensor_tensor(out=pwi, in0=pwi, in1=ex, op=ALU.mult), [Bs], [Bs])
        pwr3, pwi3 = v3(pwr), v3(pwi)
        IDX1 = 9
        den, _ = T_(16); t1, _ = T_(16); t2, _ = T_(16); zr, _ = T_(16); fr, _ = T_(16); fi, _ = T_(16)
        lam_r, lam_i = pwr3[:, :, IDX1], pwi3[:, :, IDX1]
        dve(lambda e: e.tensor_tensor(out=den, in0=C("lr"), in1=C("lr"), op=ALU.mult), [Bcst, Bs], [Bs])
        dve(lambda e: e.tensor_tensor(out=t1, in0=C("li"), in1=C("li"), op=ALU.mult), [Bcst, Bs], [Bs])
        dve(lambda e: e.tensor_tensor(out=den, in0=den, in1=t1, op=ALU.add), [Bs], [Bs])
        dve(lambda e: e.reciprocal(out=den, in_=den), [Bs], [Bs])
        dve(lambda e: e.tensor_scalar(out=zr, in0=lam_r, scalar1=-1.0, scalar2=None, op0=ALU.add), [Bs], [Bs])
        dve(lambda e: e.tensor_tensor(out=t1, in0=zr, in1=C("lr"), op=ALU.mult), [Bcst, Bs], [Bs])
        dve(lambda e: e.tensor_tensor(out=t2, in0=lam_i, in1=C("li"), op=ALU.mult), [Bcst, Bs], [Bs])
        dve(lambda e: e.tensor_tensor(out=t1, in0=t1, in1=t2, op=ALU.add), [Bs], [Bs])
        dve(lambda e: e.tensor_tensor(out=fr, in0=t1, in1=den, op=ALU.mult), [Bs], [Bs])
        dve(lambda e: e.tensor_tensor(out=t1, in0=lam_i, in1=C("lr"), op=ALU.mult), [Bcst, Bs], [Bs])
        dve(lambda e: e.tensor_tensor(out=t2, in0=zr, in1=C("li"), op=ALU.mult), [Bcst, Bs], [Bs])
        dve(lambda e: e.tensor_tensor(out=t1, in0=t1, in1=t2, op=ALU.subtract), [Bs], [Bs])
        dve(lambda e: e.tensor_tensor(out=fi, in0=t1, in1=den, op=ALU.mult), [Bs], [Bs])
        bbr, _ = T_(256); bbi, _ = T_(256); tq1, _ = T_(256)
        g16 = lambda t: t.rearrange("p (g h) -> p g h", g=16)
        dve(lambda e: e.tensor_tensor(out=g16(bbr), in0=g16(C("bre")), in1=bc3(fr, 16), op=ALU.mult), [Bcst, Bs], [Bs])
        dve(lambda e: e.tensor_tensor(out=g16(tq1), in0=g16(C("bim")), in1=bc3(fi, 16), op=ALU.mult), [Bcst, Bs], [Bs])
        dve(lambda e: e.tensor_tensor(out=bbr, in0=bbr, in1=tq1, op=ALU.subtract), [Bs], [Bs])
        dve(lambda e: e.tensor_tensor(out=g16(bbi), in0=g16(C("bim")), in1=bc3(fr, 16), op=ALU.mult), [Bcst, Bs], [Bs])
        dve(lambda e: e.tensor_tensor(out=g16(tq1), in0=g16(C("bre")), in1=bc3(fi, 16), op=ALU.mult), [Bcst, Bs], [Bs])
        dve(lambda e: e.tensor_tensor(out=bbi, in0=bbi, in1=tq1, op=ALU.add), [Bs], [Bs])
        Rcol, _ = T_(16); f16, _ = T_(16)
        act(lambda e: e.activation(out=Rcol, in_=s_a, func=AF.Exp, scale=16.0), [Bs], [Bs])
        dve(lambda e: e.tensor_scalar(out=f16, in0=s_fq, scalar1=16.0, scalar2=None, op0=ALU.mult), [Bs], [Bs])
        frac(f16, f16, s_i, s_n, [Bs])
        zero1, _ = T_(1)
        dve(lambda e: e.memset(zero1, 0.0), [], [Bs])

        NG = 4
        NGR = 2 * NG
        A0r_s = [A(NG * 128, BF16)[0] for _ in range(2)]; A0i_s = [A(NG * 128, BF16)[0] for _ in range(2)]
        Bmr_s = [A(NG * 272, BF16)[0] for _ in range(2)]; Bmi_s = [A(NG * 272, BF16)[0] for _ in range(2)]
        BA0_s = [Buf("A0a"), Buf("A0b")]; BBm_s = [Buf("Bma"), Buf("Bmb")]
        tA, BtA = T_(NG * 272); tB, _ = T_(NG * 272)
        Dg_s = [A(NG * 2 * 3 * 64, BF16)[0] for _ in range(2)]
        BDg_s = [Buf("Dga"), Buf("Dgb")]
        TT, _ = A(NGR * 256, BF16)
        GmR, _ = A(NGR * 2 * 64, BF16); GmI, _ = A(NGR * 2 * 64, BF16)
        cosA, _ = T_(16 * 128); sinA, _ = T_(16 * 128)
        rph, _ = T_(4 * 128); rph2, _ = T_(4 * 128)
        rphi, _ = A(4 * 128, I32)
        SRsh, _ = A(NG * 128, BF16); SIsh, _ = A(NG * 128, BF16)
        l2a, _ = T_(NG * 128); l2b, _ = T_(NG * 128); l2c, _ = T_(NG * 128); l2d, _ = T_(NG * 128)
        Utok, _ = A(NGR * 256, BF16)
        UT, _ = A(NGR * 2 * 128, BF16)
        ygT, _ = A(4 * L, BF16)
        ygT = ygT.rearrange("p (k t) -> p k t", k=4)
        gtmp, _ = T_(512); gtmp2, _ = T_(512)
        BDg, BTT, BGm, Bl2x, BS, Bl2, BU, BUT, Bg = (Buf(n) for n in "Dg TT Gm l2x S l2 U UT g".split())
        BYt = BU
        Bg2 = Buf("g2")
        Byg = [Buf(f"yg{k}") for k in range(4)]
        tA4 = tA.rearrange("p (g t h) -> p g t h", g=NG, t=17)
        tB4 = tB.rearrange("p (g t h) -> p g t h", g=NG, t=17)
        Dg5_s = [d.rearrange("p (g j k n) -> p g j k n", g=NG, j=2, k=3) for d in Dg_s]
        TT3 = TT.rearrange("p (g n) -> p g n", g=NGR)
        GmR4 = GmR.rearrange("p (g j n) -> p g j n", g=NGR, j=2)
        GmI4 = GmI.rearrange("p (g j n) -> p g j n", g=NGR, j=2)
        Brot = Buf("rot")
        for hh in range(4):
            gsl = slice(4 * hh, 4 * hh + 4)
            csl = slice(4 * hh * 128, (4 * hh + 4) * 128)
            cid3 = C("cidx").unsqueeze(1).to_broadcast([128, 4, 128])
            dve(lambda e, gsl=gsl: e.tensor_tensor(out=rph.rearrange("p (g c) -> p g c", g=4), in0=cid3,
                                                   in1=f16[:, gsl].unsqueeze(2).to_broadcast([128, 4, 128]), op=ALU.mult), [Bs, Bcst], [Bl2x])
            frac(rph, rph, rphi, rph2, [Bl2x])
            act(lambda e, csl=csl: e.activation(out=sinA[:, csl], in_=rph, func=AF.Sin, scale=TWO_PI_LO), [Bl2x], [Brot])
            dve(lambda e: e.tensor_scalar(out=rph, in0=rph, scalar1=0.25, scalar2=None, op0=ALU.add), [Bl2x], [Bl2x])
            frac(rph, rph, rphi, rph2, [Bl2x])
            act(lambda e, csl=csl: e.activation(out=cosA[:, csl], in_=rph, func=AF.Sin, scale=TWO_PI_LO), [Bl2x], [Brot])
        SR3 = SRsh.rearrange("p (g c) -> p g c", g=NG)
        SI3 = SIsh.rearrange("p (g c) -> p g c", g=NG)
        Ut4 = Utok.rearrange("p (g t h) -> p g t h", g=NGR, t=16)
        Yt4 = Utok.rearrange("p (t g h) -> p t g h", t=16, g=NGR)
        UT4 = UT.rearrange("p (g j c) -> p g j c", g=NGR, j=2)
        dve(lambda e: e.memset(SRsh, 0.0), [], [BS])
        dve(lambda e: e.memset(SIsh, 0.0), [], [BS])

        def emit_tables(qd):
            gp0 = qd * NG
            gps = slice(gp0, gp0 + NG)
            bcj = lambda t, idx0, n, m: t[:, gps, idx0:idx0 + n].unsqueeze(3).to_broadcast([128, NG, n, m])
            bch = lambda t, n: g16(t)[:, gps, :].unsqueeze(2).to_broadcast([128, NG, n, 16])
            A0r4 = A0r_s[qd % 2].rearrange("p (g j h) -> p g j h", g=NG, j=8)
            A0i4 = A0i_s[qd % 2].rearrange("p (g j h) -> p g j h", g=NG, j=8)
            Bmr4 = Bmr_s[qd % 2].rearrange("p (g t h) -> p g t h", g=NG, t=17)
            Bmi4 = Bmi_s[qd % 2].rearrange("p (g t h) -> p g t h", g=NG, t=17)
            BA0, BBm = BA0_s[qd % 2], BBm_s[qd % 2]
            Dg5 = Dg5_s[qd % 2]
            BDg = BDg_s[qd % 2]
            pl = lambda fn, r, w: fw.op("dve", fn, reads=r, writes=w)
            tA8 = tA[:, 0:NG * 128].rearrange("p (g j h) -> p g j h", g=NG, j=8)
            tB8 = tB[:, 0:NG * 128].rearrange("p (g j h) -> p g j h", g=NG, j=8)
            pl(lambda e: e.tensor_tensor(out=tA8, in0=bcj(pwr3, 0, 8, 16), in1=bch(bbr, 8), op=ALU.mult), [Bs], [BtA])
            pl(lambda e: e.tensor_tensor(out=tB8, in0=bcj(pwi3, 0, 8, 16), in1=bch(bbi, 8), op=ALU.mult), [Bs], [BtA])
            pl(lambda e: e.tensor_tensor(out=A0r4, in0=tA8, in1=tB8, op=ALU.subtract), [BtA], [BA0])
            pl(lambda e: e.tensor_tensor(out=tA8, in0=bcj(pwr3, 0, 8, 16), in1=bch(bbi, 8), op=ALU.mult), [Bs], [BtA])
            pl(lambda e: e.tensor_tensor(out=tB8, in0=bcj(pwi3, 0, 8, 16), in1=bch(bbr, 8), op=ALU.mult), [Bs], [BtA])
            pl(lambda e: e.tensor_tensor(out=A0i4, in0=tA8, in1=tB8, op=ALU.add), [BtA], [BA0])
            cch = lambda n: g16(C(n))[:, gps, :].unsqueeze(2).to_broadcast([128, NG, 17, 16])
            pl(lambda e: e.tensor_tensor(out=tA4, in0=bcj(pwr3, 8, 17, 16), in1=cch("cre"), op=ALU.mult), [Bs, Bcst], [BtA])
            pl(lambda e: e.tensor_tensor(out=tB4, in0=bcj(pwi3, 8, 17, 16), in1=cch("cim"), op=ALU.mult), [Bs, Bcst], [BtA])
            pl(lambda e: e.tensor_tensor(out=Bmr4, in0=tA4, in1=tB4, op=ALU.subtract), [BtA], [BBm])
            pl(lambda e: e.tensor_tensor(out=tA4, in0=bcj(pwr3, 8, 17, 16), in1=cch("cim"), op=ALU.mult), [Bs, Bcst], [BtA])
            pl(lambda e: e.tensor_tensor(out=tB4, in0=bcj(pwi3, 8, 17, 16), in1=cch("cre"), op=ALU.mult), [Bs, Bcst], [BtA])
            pl(lambda e: e.tensor_tensor(out=tA4, in0=tA4, in1=tB4, op=ALU.add), [BtA], [BtA])
            pl(lambda e: e.tensor_scalar(out=Bmi4, in0=tA4, scalar1=-1.0, scalar2=None, op0=ALU.mult), [BtA], [BBm])
            for gl in range(NG):
                for jb in range(2):
                    idx = 8 + 15 - 8 * jb
                    for kind in range(3):
                        src = (pwr3, pwi3, pwi3)[kind][:, gp0 + gl, idx:idx + 1]
                        sgn = (1.0, 1.0, -1.0)[kind]
                        dve(lambda e, gl=gl, jb=jb, kind=kind, src=src, sgn=sgn: e.tensor_scalar(
                            out=Dg5[:, gl, jb, kind, :], in0=C("i2"), scalar1=src, scalar2=sgn, op0=ALU.mult, op1=ALU.mult),
                            [Bs, Bcst], [BDg])
        UCOL0 = 1536
        emit_tables(0)
        for qd in range(16 // NG):
            gp0 = qd * NG
            gps = slice(gp0, gp0 + NG)
            wt, Bw = wtU, BwU
            A0r4 = A0r_s[qd % 2].rearrange("p (g j h) -> p g j h", g=NG, j=8)
            A0i4 = A0i_s[qd % 2].rearrange("p (g j h) -> p g j h", g=NG, j=8)
            Bmr4 = Bmr_s[qd % 2].rearrange("p (g t h) -> p g t h", g=NG, t=17)
            Bmi4 = Bmi_s[qd % 2].rearrange("p (g t h) -> p g t h", g=NG, t=17)
            BA0, BBm = BA0_s[qd % 2], BBm_s[qd % 2]
            Dg5 = Dg5_s[qd % 2]
            BDg = BDg_s[qd % 2]
            cosT = cosA[:, gp0 * 128:(gp0 + NG) * 128]
            sinT = sinA[:, gp0 * 128:(gp0 + NG) * 128]

            for tb4 in range(4):
                bk = 4 + tb4 % 2
                ps = banks[bk]

                def mm_u(e, tb4=tb4, ps=ps):
                    last = None
                    for ti in range(4):
                        tau = tb4 * 4 + ti
                        for kb in range(8):
                            last = e.matmul(ps[:, ti * 128:(ti + 1) * 128], lhsT=xT[:, kb, tau:L:16], rhs=wt[:, kb, 128 * qd:128 * qd + 128],
                                            start=(kb == 0 and ti == 0), stop=(kb == 7), skip_group_check=True)
                    return last
                fw.op("pe", mm_u, [BxT, Bw], [PB[bk]])
                src = ps[:, :].rearrange("p (t g h) -> p t g h", t=4, g=NGR)
                dst = Ut4[:, :, tb4 * 4:tb4 * 4 + 4, :].rearrange("p g t h -> p t g h")
                act(lambda e, src=src, dst=dst: e.activation(out=dst, in_=src, func=AF.Copy), [PB[bk]], [BU])
            for half in range(2):
                bk = 6 + half
                psb = pbf(bk)

                def tr_u(e, half=half, psb=psb):
                    last = None
                    for i in range(8):
                        gl, jb = (half * 8 + i) // 2, (half * 8 + i) % 2
                        last = e.transpose(psb[:, i * 128:(i + 1) * 128],
                                           Ut4[:, gl, jb * 8:(jb + 1) * 8, :].rearrange("p j h -> p (j h)"), identb)
                    return last
                fw.op("pe", tr_u, [BU, Bidb], [PB[bk]])
                dst = UT[:, half * 1024:(half + 1) * 1024]
                act(lambda e, dst=dst, psb=psb: e.activation(out=dst, in_=psb, func=AF.Copy), [PB[bk]], [BUT])
            for pb in range(NGR // 2):
                def mm_toep(e, pb=pb):
                    last = None
                    for k in range(2):
                        gl = 4 * (pb // 2) + (pb % 2) + 2 * k
                        gpl, P0 = gl // 2, 64 * (gl % 2)
                        o = banks[pb][:, k * 256:(k + 1) * 256]
                        e.matmul(o, lhsT=A0r4[P0:P0 + 64, gpl].rearrange("p j h -> p (j h)"),
                                 rhs=Bmr4[P0:P0 + 64, gpl, 0:16].rearrange("p t h -> p (t h)"), start=(k == 0), stop=False, skip_group_check=True)
                        last = e.matmul(o, lhsT=A0i4[P0:P0 + 64, gpl].rearrange("p j h -> p (j h)"),
                                        rhs=Bmi4[P0:P0 + 64, gpl, 0:16].rearrange("p t h -> p (t h)"), start=False, stop=True, skip_group_check=True)
                    return last
                fw.op("pe", mm_toep, [BA0, BBm], [PB[pb]])
            for pb in range(NGR // 2):
                def mm_gm(e, pb=pb):
                    last = None
                    for k in range(2):
                        gl = 4 * (pb // 2) + (pb % 2) + 2 * k
                        gpl, P0 = gl // 2, 64 * (gl % 2)
                        for jb in range(2):
                            l_r = A0r4[P0:P0 + 64, gpl].rearrange("p j h -> p (j h)")
                            l_i = A0i4[P0:P0 + 64, gpl].rearrange("p j h -> p (j h)")
                            d_re = Dg5[P0:P0 + 64, gpl, jb, 0, :]
                            d_im = Dg5[P0:P0 + 64, gpl, jb, 1, :]
                            d_in = Dg5[P0:P0 + 64, gpl, jb, 2, :]
                            c0 = k * 256 + jb * 128
                            o_r = banks[4 + pb][:, c0:c0 + 64]
                            o_i = banks[4 + pb][:, c0 + 64:c0 + 128]
                            first = (jb == 0 and k == 0)
                            e.matmul(o_r, lhsT=l_r, rhs=d_re, start=first, stop=False, skip_group_check=True)
                            e.matmul(o_r, lhsT=l_i, rhs=d_in, start=False, stop=True, skip_group_check=True)
                            e.matmul(o_i, lhsT=l_r, rhs=d_im, start=False, stop=False, skip_group_check=True)
                            last = e.matmul(o_i, lhsT=l_i, rhs=d_re, start=False, stop=True, skip_group_check=True)
                    return last
                fw.op("pe", mm_gm, [BA0, BDg], [PB[4 + pb]])
            for pb in range(NGR // 2):
                ps3 = banks[pb][:, :].rearrange("p (g n) -> p g n", g=2)
                gt_ = (gtmp, gtmp2)[pb % 2]
                Bgt_ = (Bg, Bg2)[pb % 2]
                gt3 = gt_[:, 0:256].rearrange("p (g n) -> p g n", g=2)
                dve(lambda e, ps3=ps3, gt3=gt3: e.tensor_tensor(out=gt3, in0=ps3[:, :, 0:128],
                                                                 in1=C("tmask").unsqueeze(1).to_broadcast([128, 2, 128]), op=ALU.mult),
                    [PB[pb], Bcst], [Bgt_])
                gl0 = 4 * (pb // 2) + (pb % 2)
                for k in range(2):
                    gl = gl0 + 2 * k
                    g = 2 * gp0 + gl
                    dve(lambda e, gl=gl, g=g, k=k, gt3=gt3: e.scalar_tensor_tensor(out=TT3[:, gl, 0:128], in0=C("ident"),
                                                                                   scalar=C("dl")[:, g:g + 1], in1=gt3[:, k, :],
                                                                                   op0=ALU.mult, op1=ALU.add), [Bgt_, Bcst], [BTT])
                dve(lambda e, pb=pb, ps3=ps3: e.tensor_copy(out=TT3[:, gl0:gl0 + 3:2, 128:256], in_=ps3[:, :, 128:256]),
                    [PB[pb]], [BTT])
            for pb in range(NGR // 2):
                gl0 = 4 * (pb // 2) + (pb % 2)
                v = banks[4 + pb][:, :].rearrange("p (g j r n) -> p g j r n", g=2, j=2, r=2)
                act(lambda e, pb=pb, v=v: e.activation(out=GmR4[:, gl0:gl0 + 3:2], in_=v[:, :, :, 0, :], func=AF.Copy), [PB[4 + pb]], [BGm])
                act(lambda e, pb=pb, v=v: e.activation(out=GmI4[:, gl0:gl0 + 3:2], in_=v[:, :, :, 1, :], func=AF.Copy), [PB[4 + pb]], [BGm])

            def mm_G(e):
                last = None
                for ri, (bk, Gm4) in enumerate(((0, GmR4), (1, GmI4))):
                    for gpl in range(NG):
                        for g2 in range(2):
                            gl = 2 * gpl + g2
                            for jb in range(2):
                                last = e.matmul(banks[bk][64 * g2:64 * g2 + 64, gpl * 128:(gpl + 1) * 128],
                                                lhsT=Gm4[:, gl, jb, :], rhs=UT4[:, gl, jb, :],
                                                start=(jb == 0 and gpl == 0), stop=(jb == 1), skip_group_check=True)
                return last
            fw.op("pe", mm_G, [BGm, BUT], [PB[0], PB[1]])
            GRp, GIp = banks[0][:, :], banks[1][:, :]
            dve(lambda e: e.tensor_tensor(out=l2a, in0=GRp, in1=cosT, op=ALU.mult), [PB[0], Brot], [Bl2])
            dve(lambda e: e.tensor_tensor(out=l2b, in0=GIp, in1=sinT, op=ALU.mult), [PB[1], Brot], [Bl2])
            dve(lambda e: e.tensor_tensor(out=l2a, in0=l2a, in1=l2b, op=ALU.add), [Bl2], [Bl2])
            dve(lambda e: e.tensor_tensor(out=l2b, in0=GIp, in1=cosT, op=ALU.mult), [PB[1], Brot], [Bl2])
            dve(lambda e: e.tensor_tensor(out=l2c, in0=GRp, in1=sinT, op=ALU.mult), [PB[0], Brot], [Bl2])
            dve(lambda e: e.tensor_tensor(out=l2b, in0=l2b, in1=l2c, op=ALU.subtract), [Bl2], [Bl2])
            for gpl in range(NG):
                sl = slice(gpl * 128, (gpl + 1) * 128)
                rc = Rcol[:, gp0 + gpl:gp0 + gpl + 1].to_broadcast([128, 128])
                dve(lambda e, sl=sl, rc=rc: e.tensor_tensor_scan(out=l2c[:, sl], data0=rc, data1=l2a[:, sl], initial=0.0,
                                                                 op0=ALU.mult, op1=ALU.add), [Bl2, Bs], [Bl2])
                dve(lambda e, sl=sl, rc=rc: e.tensor_tensor_scan(out=l2d[:, sl], data0=rc, data1=l2b[:, sl], initial=0.0,
                                                                 op0=ALU.mult, op1=ALU.add), [Bl2, Bs], [Bl2])
            dve(lambda e: e.tensor_tensor(out=l2a, in0=l2c, in1=cosT, op=ALU.mult), [Bl2, Brot], [Bl2])
            dve(lambda e: e.tensor_tensor(out=l2b, in0=l2d, in1=sinT, op=ALU.mult), [Bl2, Brot], [Bl2])
            l2a3 = l2a.rearrange("p (g c) -> p g c", g=NG)
            l2b3 = l2b.rearrange("p (g c) -> p g c", g=NG)
            dve(lambda e: e.tensor_tensor(out=SR3[:, :, 1:128], in0=l2a3[:, :, 0:127], in1=l2b3[:, :, 0:127], op=ALU.subtract),
                [Bl2], [BS])
            dve(lambda e: e.tensor_tensor(out=l2a, in0=l2c, in1=sinT, op=ALU.mult), [Bl2, Brot], [Bl2])
            dve(lambda e: e.tensor_tensor(out=l2b, in0=l2d, in1=cosT, op=ALU.mult), [Bl2, Brot], [Bl2])
            dve(lambda e: e.tensor_tensor(out=SI3[:, :, 1:128], in0=l2a3[:, :, 0:127], in1=l2b3[:, :, 0:127], op=ALU.add),
                [Bl2], [BS])
            for pr in range(NGR // 2):
                bk = 2 + pr % 2
                ps = banks[bk]

                def mm_y(e, pr=pr, ps=ps):
                    last = None
                    for k in range(2):
                        gl = 4 * (pr // 2) + (pr % 2) + 2 * k
                        gpl, g2 = gl // 2, gl % 2
                        P0 = 64 * g2
                        o = ps[:, k * 256:(k + 1) * 256]
                        e.matmul(o, lhsT=UT4[:, gl, 0, :], rhs=TT3[:, gl, 0:256], start=(k == 0), stop=False, skip_group_check=True)
                        e.matmul(o[:, 128:256], lhsT=UT4[:, gl, 1, :], rhs=TT3[:, gl, 0:128], start=False, stop=False, skip_group_check=True)
                        e.matmul(o, lhsT=SR3[P0:P0 + 64, gpl, :], rhs=Bmr4[P0:P0 + 64, gpl, 1:17].rearrange("p t h -> p (t h)"),
                                 start=False, stop=False, skip_group_check=True)
                        last = e.matmul(o, lhsT=SI3[P0:P0 + 64, gpl, :], rhs=Bmi4[P0:P0 + 64, gpl, 1:17].rearrange("p t h -> p (t h)"),
                                        start=False, stop=True, skip_group_check=True)
                    return last
                fw.op("pe", mm_y, [BUT, BTT, BS, BBm], [PB[bk]])
                gl0 = 4 * (pr // 2) + (pr % 2)
                dst = Yt4[:, :, gl0:gl0 + 3:2, :].rearrange("p t g h -> p g t h")
                act(lambda e, ps=ps, dst=dst: e.activation(out=dst, in_=ps[:, :].rearrange("p (g t h) -> p g t h", g=2, t=16),
                                                           func=AF.Gelu_apprx_tanh), [PB[bk]], [BYt])
            if qd + 1 < 16 // NG:
                emit_tables(qd + 1)
            for half in range(2):
                bk = 6 + half
                psb = pbf(bk)

                def tr_y(e, half=half, psb=psb):
                    last = None
                    for i in range(8):
                        tau = half * 8 + i
                        last = e.transpose(psb[:, i * 128:(i + 1) * 128], Yt4[:, tau].rearrange("p g h -> p (g h)"), identb)
                    return last
                fw.op("pe", tr_y, [BYt, Bidb], [PB[bk]])
                dst = ygT[:, qd, :].rearrange("p (c t) -> p t c", t=16)[:, half * 8:half * 8 + 8, :]
                src = psb.rearrange("p (t c) -> p t c", t=8)
                act(lambda e, dst=dst, src=src: e.activation(out=dst, in_=src, func=AF.Copy), [PB[bk]], [Byg[qd]])

        hb, Bhb = A(4, F32)
        dve(lambda e: e.tensor_scalar(out=hb, in0=C("bglu"), scalar1=0.5, scalar2=None, op0=ALU.mult), [Bcst], [Bhb])
        gth, Bgth = A(512, BF16)
        for cb in range(4):
            for tg in range(4):
                bk = (cb * 4 + tg) % 2
                ps = banks[bk]

                def mm_glu(e, cb=cb, tg=tg, ps=ps):
                    last = None
                    for kb in range(4):
                        last = e.matmul(ps[:, :], lhsT=wglu[:, kb, cb * 128:(cb + 1) * 128], rhs=ygT[:, kb, tg * 512:(tg + 1) * 512],
                                        start=(kb == 0), stop=(kb == 3))
                    return last
                fw.op("pe", mm_glu, [Bwglu] + Byg, [PB[bk]])
                act(lambda e, cb=cb, ps=ps: e.activation(out=gth, in_=ps[:, :], func=AF.Sigmoid, bias=C("bglu")[:, cb:cb + 1]),
                    [PB[bk], Bcst], [Bgth])
                dve(lambda e, cb=cb, tg=tg: e.tensor_tensor(out=mixT[:, 4 + cb, tg * 512:(tg + 1) * 512], in0=gth,
                                                            in1=ygT[:, cb, tg * 512:(tg + 1) * 512], op=ALU.mult),
                    [Bgth, Byg[cb]], [Bmix[4 + cb][tg]])

        fw.barrier()
        ar.reset(PM_A)
        AT = lambda cols, dt: (ar.alloc_top(cols, dt), Buf())
        lnc, Blnc = AT(4 * D, F32)
        WoG, BWoG = AT(8 * D, BF16)
        WoG = WoG.rearrange("p (k n) -> p k n", k=8)
        Wple, BWple = AT(2 * D, BF16)
        Wple = Wple.rearrange("p (k n) -> p k n", k=2)
        NWUP = 3
        NCH = NPAIR // 2
        wup = []
        for i in range(NWUP):
            t, b = AT(8 * 512, BF16)
            wup.append((t.rearrange("p (k n) -> p k n", k=8), b))

        def emit_wdma(cidx):
            wt_, Bw_ = wup[cidx % NWUP]
            sn = f"wup{cidx % NWUP}"
            fw.dma("pool", sn, wt_[:, :, 0:256], wup_d[:, 256 * cidx:256 * cidx + 256].rearrange("(k p) n -> p k n", p=128), writes=[Bw_])
            fw.dma("pool", sn, wt_[:, :, 256:512],
                   wup_d[:, DFF + 256 * cidx:DFF + 256 * cidx + 256].rearrange("(k p) n -> p k n", p=128), writes=[Bw_])
        fw.dma("sp", "lnc", lnc, lnc_d, writes=[Blnc])
        fw.dma("pool", "wog", WoG, wo_d.rearrange("(k p) n -> p k n", p=128), writes=[BWoG])
        emit_wdma(0)
        emit_wdma(1)
        fw.dma("pool", "wple", Wple, wple_d.rearrange("(k p) n -> p k n", p=128), writes=[BWple])
        if debug == "ssm":
            dbt, Bdb = A(L, F32)
            for k in range(4):
                dve(lambda e, k=k: e.tensor_copy(out=dbt, in_=mixT[:, 4 + k, :]), Bmix[4 + k], [Bdb])
                fw.dma("sp", "out", dbg_d[k * 128:(k + 1) * 128, :], dbt, reads=[Bdb])
            fw.barrier()
            return nc
        qT = [[A(L, BF16)[0] for _c2 in range(2)] for _ in range(2)]
        kT = [A(L, BF16)[0] for _ in range(2)]
        Vh = [A(16 * 130, BF16)[0].rearrange("p (b e) -> p b e", b=16) for _ in range(2)]
        Bq = [Buf() for _ in range(2)]; Bk = [Buf() for _ in range(2)]; Bv = [Buf() for _ in range(2)]
        NPT = 4
        Pt = [A(512, BF16)[0] for _ in range(NPT)]
        BPt = [Buf() for _ in range(NPT)]
        O1, BO1 = A(512, F32)
        otmp, Bot = A(512, F32)
        osq, _ = A(512, F32)
        oN, BoN = A(512, BF16)
        O1v = O1.rearrange("p (j e) -> p j e", j=4)
        otv = otmp.rearrange("p (j e) -> p j e", j=4)
        osv = osq.rearrange("p (j e) -> p j e", j=4)
        oNv = oN.rearrange("p (j e) -> p j e", j=4)
        lt, Blt = A(64, F32)
        lcol, Blc = A(8, F32)
        rz, Brz = A(4, F32)
        ssum, _ = A(4, F32)
        mhalf, Bmh = A(4, F32)
        for i in range(2):
            dve(lambda e, i=i: e.memset(Vh[i][:, :, 128:130], 1.0), [], [Bv[i]])
            for c2 in range(2):
                dve(lambda e, i=i, c2=c2: e.memset(qT[i][c2], 0.0), [], [Bq[i]])
        dve(lambda e: e.memset(mhalf, -0.5), [], [Bmh])
        lq = C("lamq")
        for i in range(2):
            dve(lambda e, i=i: e.tensor_tensor(out=lt, in0=lq[:, 128 * i:128 * i + 64], in1=lq[:, 128 * i + 64:128 * i + 128], op=ALU.mult),
                [Bcst], [Blt])
            dve(lambda e, i=i: e.tensor_reduce(out=lcol[:, i:i + 1], in_=lt, axis=mybir.AxisListType.X, op=ALU.add), [Blt], [Blc])
        act(lambda e: e.activation(out=lcol[:, 0:2], in_=lcol[:, 0:2], func=AF.Exp), [Blc], [Blc])
        dve(lambda e: e.tensor_tensor(out=lcol[:, 2:3], in0=lcol[:, 1:2], in1=lcol[:, 0:1], op=ALU.subtract), [Blc], [Blc])
        dve(lambda e: e.tensor_scalar(out=lcol[:, 2:3], in0=lcol[:, 2:3], scalar1=-LAM_INIT, scalar2=None, op0=ALU.add), [Blc], [Blc])
        nlam = lcol[:, 2:3]
        dve(lambda e: e.tensor_scalar(out=lcol[:, 3:4], in0=C("gcol"), scalar1=1.0 - LAM_INIT, scalar2=None, op0=ALU.mult), [Bcst, Blc], [Blc])
        gsc = lcol[:, 3:4]

        pending = []
        SC = 1.0 / 8.0

        def proj_groups(h):
            sl = h % 2
            wt, Bw = head_w[h]
            fns = []
            for which, Bd in enumerate((Bq[sl], Bk[sl])):
                for tg in range(4):
                    def grp(bk, which=which, tg=tg, Bd=Bd, wt=wt, Bw=Bw, sl=sl):
                        ps = banks[bk]

                        def mm_p(e):
                            last = None
                            for kb in range(8):
                                last = e.matmul(ps[:, :], lhsT=wt[:, kb, which * 128:(which + 1) * 128], rhs=xT[:, kb, tg * 512:(tg + 1) * 512],
                                                start=(kb == 0), stop=(kb == 7))
                            return last
                        fw.op("pe", mm_p, [Bw, BxT], [PB[bk]])
                        cs = slice(tg * 512, (tg + 1) * 512)
                        if which == 0:
                            act(lambda e: e.activation(out=qT[sl][0][0:64, cs], in_=ps[0:64, :], func=AF.Copy), [PB[bk]], [Bd])
                            dve(lambda e: e.tensor_copy(out=qT[sl][1][64:128, cs], in_=ps[64:128, :]), [PB[bk]], [Bd])
                        else:
                            dve(lambda e: e.tensor_copy(out=kT[sl][:, cs], in_=ps[:, :]), [PB[bk]], [Bd])
                    fns.append(grp)
            for tb4 in range(4):
                def grpv(bk, tb4=tb4, wt=wt, Bw=Bw, sl=sl):
                    ps = banks[bk]

                    def mm_v(e):
                        last = None
                        for ti in range(4):
                            tkb = tb4 * 4 + ti
                            for kb in range(8):
                                last = e.matmul(ps[:, ti * 128:(ti + 1) * 128], lhsT=xT[:, kb, tkb * 128:(tkb + 1) * 128], rhs=wt[:, kb, 256:384],
                                                start=(kb == 0 and ti == 0), stop=(kb == 7), skip_group_check=True)
                        return last
                    fw.op("pe", mm_v, [Bw, BxT], [PB[bk]])
                    src = ps[:, :].rearrange("p (b e) -> p b e", b=4)
                    dst = Vh[sl][:, tb4 * 4:tb4 * 4 + 4, 0:128]
                    dve(lambda e: e.tensor_copy(out=dst, in_=src), [PB[bk]], [Bv[sl]])
                fns.append(grpv)
            return fns

        def finalize(h, G, c, aset, accv):
            nonlocal_pending = pending
            for a in range(2):
                acc = accv[a]
                jsl = slice(2 * a, 2 * a + 2)
                PBa = PB[aset[a]]
                dve(lambda e, acc=acc, a=a: e.reciprocal(out=rz[:, 2 * a:2 * a + 2], in_=acc[:, :, 128]), [PBa], [Brz])
                if c == 0:
                    dve(lambda e, acc=acc, jsl=jsl, a=a: e.tensor_tensor(
                        out=O1v[:, jsl, :], in0=acc[:, :, 0:128],
                        in1=rz[:, 2 * a:2 * a + 2].unsqueeze(2).to_broadcast([128, 2, 128]), op=ALU.mult), [PBa, Brz], [BO1])
                else:
                    dve(lambda e, a=a: e.tensor_scalar(out=rz[:, 2 * a:2 * a + 2], in0=rz[:, 2 * a:2 * a + 2], scalar1=nlam, scalar2=None,
                                                       op0=ALU.mult), [Brz, Blc], [Brz])
                    dve(lambda e, acc=acc, jsl=jsl, a=a: e.tensor_tensor(
                        out=otv[:, jsl, :], in0=acc[:, :, 0:128],
                        in1=rz[:, 2 * a:2 * a + 2].unsqueeze(2).to_broadcast([128, 2, 128]), op=ALU.mult), [PBa, Brz], [Bot])
            if c == 1:
                dve(lambda e: e.tensor_tensor(out=otmp, in0=otmp, in1=O1, op=ALU.add), [Bot, BO1], [Bot])
                dve(lambda e: e.tensor_tensor(out=osq, in0=otmp, in1=otmp, op=ALU.mult), [Bot], [Bot])
                dve(lambda e: e.tensor_reduce(out=ssum, in_=osv, axis=mybir.AxisListType.X, op=ALU.add), [Bot], [Bot])
                dve(lambda e: e.tensor_scalar(out=ssum, in0=ssum, scalar1=1.0 / 128.0, scalar2=LN_EPS, op0=ALU.mult, op1=ALU.add),
                    [Bot], [Bot])
                fw.op("pool", lambda e: e.tensor_tensor(out=ssum, in0=ssum, in1=mhalf, op=ALU.pow), [Bot, Bmh], [Bot])
                dve(lambda e: e.tensor_tensor(out=oNv, in0=otv, in1=ssum.unsqueeze(2).to_broadcast([128, 4, 128]), op=ALU.mult),
                    [Bot], [BoN])

                def do_tr(h=h, G=G):
                    psb = pbf(0)

                    def tr_o(e):
                        last = None
                        for jq in range(4):
                            last = e.transpose(psb[:, jq * 128:(jq + 1) * 128], oNv[:, jq, :], identb)
                        return last
                    fw.op("pe", tr_o, [BoN, Bidb], [PB[0]])
                    act(lambda e: e.activation(out=mixT[:, h, G * 512:(G + 1) * 512], in_=psb[:, 0:512], func=AF.Identity, scale=gsc),
                        [PB[0], Blc], [Bmix[h][G]])
                nonlocal_pending.append(do_tr)

        for gi, g_ in enumerate(proj_groups(0)):
            g_(gi % 4)
        head_w[1] = load_w(0, head_cols(1))
        SB = (1, 2, 3)
        for h in range(4):
            sl = h % 2
            tasks = []
            for G in range(4):
                for c in range(2):
                    for b in range(4 * G + 4):
                        tasks.append((2 * G + c, G, c, b))
            started = {}
            nxt = []
            n_half = None

            def emit_S(n, sl=sl, tasks=tasks):
                ep, G, c, b = tasks[n]
                i = b - 4 * G
                n0 = max(0, i) * 128
                bk = SB[n % 3]
                ps = banks[bk]

                def mm_s(e):
                    last = e.matmul(ps[:, n0:512], lhsT=kT[sl][:, b * 128:(b + 1) * 128],
                                    rhs=qT[sl][c][:, G * 512 + n0:(G + 1) * 512], start=True, stop=(i < 0))
                    if i >= 0:
                        last = e.matmul(ps[:, n0:n0 + 128], lhsT=identb, rhs=maskb, start=False, stop=True)
                    return last
                fw.op("pe", mm_s, [Bq[sl], Bk[sl], Bidb, Bmkb], [PB[bk]])
            for n_ in range(min(2, len(tasks))):
                emit_S(n_)
            for n, (ep, G, c, b) in enumerate(tasks):
                if n + 2 < len(tasks):
                    emit_S(n + 2)
                aset = (4, 5) if ep % 2 == 0 else (6, 7)
                accv = [banks[a][:, 0:258].rearrange("p (j e) -> p j e", j=2) for a in aset]
                st_ = started.setdefault(ep, [False, False])
                i = b - 4 * G
                n0 = max(0, i) * 128
                bk = SB[n % 3]
                pt = Pt[n % NPT]
                act(lambda e, pt=pt, bk=bk, n0=n0: e.activation(out=pt[:, n0:512], in_=banks[bk][:, n0:512], func=AF.Exp, scale=SC),
                    [PB[bk]], [BPt[n % NPT]])

                def mm_pv(e, b=b, i=i, G=G, pt=pt, sl=sl, accv=accv, st_=st_):
                    last = None
                    for jq in range(max(0, i), 4):
                        a = jq // 2
                        last = e.matmul(accv[a][:, jq % 2, 0:129], lhsT=pt[:, jq * 128:(jq + 1) * 128], rhs=Vh[sl][:, b, 0:129],
                                        start=(not st_[a]), stop=(b == 4 * G + jq), skip_group_check=True)
                        st_[a] = True
                    return last
                fw.op("pe", mm_pv, [BPt[n % NPT], Bv[sl]], [PB[aset[0]], PB[aset[1]]], acc=True)
                if ep >= 4 and h + 1 < 4:
                    if n_half is None:
                        n_half = n
                        nxt = proj_groups(h + 1)
                    if (n - n_half) % 4 == 0 and nxt:
                        nxt.pop(0)(0)
                        if not nxt and h + 2 < 4:
                            head_w[h + 2] = load_w((h + 1) % 2, head_cols(h + 2))
                if b == 4 * G + 3:
                    for fn in pending:
                        fn()
                    del pending[:]
                    finalize(h, G, c, aset, accv)
            while nxt:
                nxt.pop(0)(0)
                if not nxt and h + 2 < 4:
                    head_w[h + 2] = load_w((h + 1) % 2, head_cols(h + 2))
        for fn in pending:
            fn()
        del pending[:]

        if debug == "attn":
            fw.barrier()
            ar.reset(PM_A)
            dbt, Bdb = A(L, F32)
            for k in range(4):
                dve(lambda e, k=k: e.tensor_copy(out=dbt, in_=mixT[:, k, :]), Bmix[k], [Bdb])
                fw.dma("sp", "out", dbg_d[k * 128:(k + 1) * 128, :], dbt, reads=[Bdb])
            fw.barrier()
            return nc

        fw.barrier()
        ar.reset(PM)
        Wdn, BWdn = A(NPAIR * D, BF16)
        Wdn = Wdn.rearrange("p (k n) -> p k n", k=NPAIR)
        actT, _ = A(NPAIR * 512, BF16)
        actT = actT.rearrange("p (k t) -> p k t", k=NPAIR)
        Bact = Buf()
        x1a, _ = A(4 * D, F32)
        x1a = x1a.rearrange("p (b n) -> p b n", b=4)
        Bx1a = [Buf() for _ in range(4)]
        x1T, Bx1T = A(8 * 512, BF16)
        x1T = x1T.rearrange("p (k t) -> p k t", k=8)
        pTt, BpT = A(2 * 512, BF16)
        pTt = pTt.rearrange("p (k t) -> p k t", k=2)
        dgs = []
        for i in range(2):
            t, b = A(6 * 128, BF16)
            dgs.append((t.rearrange("p (j n) -> p j n", j=6), b))
        hss = []
        for i in range(2):
            tg_, bg_ = A(514, BF16)
            tv_, bv_ = A(514, BF16)
            hss.append((tg_, bg_, tv_, bv_))
        tails, Btl = A(44 * 2, BF16)
        tails = tails.rearrange("p (f c) -> p f c", f=44)
        sgt, Bsg = A(512, F32)
        W1, BW1 = A(D, F32)
        W2, BW2 = A(D, F32)
        nbf, Bnbf = A(D, BF16)
        nbf2, Bnbf2 = A(D, BF16)
        nbfs = [nbf, nbf2]
        Bnbfs = [Bnbf, Bnbf2]
        stt, Bst = A(2 * 6, F32)
        mv, _ = A(8, F32)
        mhalf2, Bmh2 = A(1, F32)
        pool = lambda fn, r, w: fw.op("pool", fn, reads=r, writes=w)
        lnv = lnc.rearrange("p (v n) -> p v n", v=4)
        dve(lambda e: e.tensor_scalar(out=lnv[:, 0:2, :], in0=lnv[:, 0:2, :], scalar1=ALPHA, scalar2=None, op0=ALU.mult), [Blnc], [Blnc])
        dve(lambda e: e.memset(tails, 0.0), [], [Btl])
        dve(lambda e: e.memset(mhalf2, -0.5), [], [Bmh2])

        def layer_norm_stats(src, Bsrc):
            for hh in range(2):
                dve(lambda e, hh=hh: e.bn_stats(out=stt[:, hh * 6:(hh + 1) * 6], in_=src[:, hh * 512:(hh + 1) * 512]), [Bsrc], [Bst])
            dve(lambda e: e.bn_aggr(out=mv[:, 0:2], in_=stt), [Bst], [Bst])
            dve(lambda e: e.tensor_scalar(out=mv[:, 2:3], in0=mv[:, 1:2], scalar1=LN_EPS, scalar2=None, op0=ALU.add), [Bst], [Bst])
            pool(lambda e: e.tensor_tensor(out=mv[:, 2:3], in0=mv[:, 2:3], in1=mhalf2, op=ALU.pow), [Bst, Bmh2], [Bst])
            dve(lambda e: e.scalar_tensor_tensor(out=mv[:, 3:4], in0=mv[:, 0:1], scalar=-1.0, in1=mv[:, 2:3], op0=ALU.mult, op1=ALU.mult),
                [Bst], [Bst])

        stt2, Bst2 = A(2 * 6, F32)
        mv2, _ = A(8, F32)
        sttA = [A(12, F32)[0] for _ in range(4)]; mvA = [A(8, F32)[0] for _ in range(4)]; BstA = [Buf() for _ in range(4)]
        sttB = [A(12, F32)[0] for _ in range(4)]; mvB = [A(8, F32)[0] for _ in range(4)]; BstB = [Buf() for _ in range(4)]

        def ln_stats(src, Bsrc, stt_, mv_, Bs_):
            for hh in range(2):
                dve(lambda e, hh=hh: e.bn_stats(out=stt_[:, hh * 6:(hh + 1) * 6], in_=src[:, hh * 512:(hh + 1) * 512]), [Bsrc], [Bs_])
            dve(lambda e: e.bn_aggr(out=mv_[:, 0:2], in_=stt_), [Bs_], [Bs_])
            dve(lambda e: e.tensor_scalar(out=mv_[:, 2:3], in0=mv_[:, 1:2], scalar1=LN_EPS, scalar2=None, op0=ALU.add), [Bs_], [Bs_])
            pool(lambda e: e.tensor_tensor(out=mv_[:, 2:3], in0=mv_[:, 2:3], in1=mhalf2, op=ALU.pow), [Bs_, Bmh2], [Bs_])
            dve(lambda e: e.scalar_tensor_tensor(out=mv_[:, 3:4], in0=mv_[:, 0:1], scalar=-1.0, in1=mv_[:, 2:3], op0=ALU.mult, op1=ALU.mult),
                [Bs_], [Bs_])

        mixbank = {}

        def emit_mix(tg, tb, mb=6):
            T = 4 * tg + tb
            mixbank[(tg, tb)] = mb

            def mm_mix(e):
                last = None
                for hf in range(2):
                    for k in range(8):
                        last = e.matmul(banks[mb + hf][:, :], lhsT=mixT[:, k, T * 128:(T + 1) * 128], rhs=WoG[:, k, hf * 512:(hf + 1) * 512],
                                        start=(k == 0), stop=(k == 7))
                return last
            fw.op("pe", mm_mix, [BWoG] + [Bmix[k][tg] for k in range(8)], [PB[mb], PB[mb + 1]])

        def emit_xload(tg, tb):
            T = 4 * tg + tb
            fw.dma("sp", f"x{tb}", x1a[:, tb, :], x_d[T * 128:(T + 1) * 128, :], writes=[Bx1a[tb]])

        def ln1_s1(tg, tb):
            xa = x1a[:, tb, :]
            Bxa = Bx1a[tb]
            mb = mixbank[(tg, tb)]
            for hf in range(2):
                dve(lambda e, hf=hf: e.scalar_tensor_tensor(out=xa[:, hf * 512:(hf + 1) * 512], in0=xa[:, hf * 512:(hf + 1) * 512],
                                                            scalar=ALPHA, in1=banks[mb + hf][:, :], op0=ALU.mult, op1=ALU.add),
                    [Bxa, PB[mb + hf]], [Bxa])
            ln_stats(xa, Bxa, sttA[tb], mvA[tb], BstA[tb])

        def ln1_s2(tg, tb):
            xa = x1a[:, tb, :]
            Bxa = Bx1a[tb]
            mv_ = mvA[tb]
            act(lambda e: e.activation(out=nbfs[tb % 2], in_=xa, func=AF.Identity, scale=mv_[:, 2:3], bias=mv_[:, 3:4]), [Bxa, BstA[tb]], [Bnbfs[tb % 2]])
            act(lambda e: e.activation(out=xa, in_=xa, func=AF.Identity, scale=mv_[:, 2:3], bias=mv_[:, 3:4]), [Bxa, BstA[tb]], [Bxa])

        def ln1_s3(tg, tb):
            xa = x1a[:, tb, :]
            Bxa = Bx1a[tb]
            dve(lambda e: e.tensor_tensor(out=xa, in0=xa, in1=lnv[:, 0, :], op=ALU.mult), [Bxa, Blnc], [Bxa])
            dve(lambda e: e.tensor_tensor(out=xa, in0=xa, in1=lnv[:, 1, :], op=ALU.add), [Bxa, Blnc], [Bxa])

        def emit_head(tg):
            ln1_s1(tg, 0)
            emit_mix(tg, 1)
            ln1_s1(tg, 1)
            ln1_s2(tg, 0)
            emit_mix(tg, 2)
            ln1_s1(tg, 2)
            ln1_s2(tg, 1)
            ln1_s3(tg, 0)
            emit_ln1_tr(tg, 0)
            emit_mix(tg, 3)
            ln1_s1(tg, 3)
            ln1_s2(tg, 2)
            ln1_s3(tg, 1)
            emit_ln1_tr(tg, 1)
            ln1_s2(tg, 3)
            ln1_s3(tg, 2)
            emit_ln1_tr(tg, 2)
            ln1_s3(tg, 3)
            emit_ln1_tr(tg, 3)

        def emit_ln1_tr(tg, tb):
            bk = 4 + tb % 2
            psb = pbf(bk)

            def tr_n(e):
                last = None
                for k in range(8):
                    last = e.transpose(psb[:, k * 128:(k + 1) * 128], nbfs[tb % 2][:, k * 128:(k + 1) * 128], identb)
                return last
            fw.op("pe", tr_n, [Bnbfs[tb % 2], Bidb], [PB[bk]])
            for k in range(8):
                act(lambda e, k=k: e.activation(out=x1T[:, k, tb * 128:(tb + 1) * 128], in_=psb[:, k * 128:(k + 1) * 128],
                                                func=AF.Identity, scale=CB("g1c")[:, k:k + 1], bias=CB("b1c")[:, k:k + 1]),
                    [PB[bk], BcstB], [Bx1T])

        def emit_down(tg, tb):
            db = 2 * (tb % 2)

            def mm_dn(e):
                last = None
                for hf in range(2):
                    for k in range(NPAIR):
                        last = e.matmul(banks[db + hf][:, :], lhsT=actT[:, k, tb * 128:(tb + 1) * 128], rhs=Wdn[:, k, hf * 512:(hf + 1) * 512],
                                        start=(k == 0), stop=(k == NPAIR - 1))
                return last
            fw.op("pe", mm_dn, [Bact, BWdn], [PB[db], PB[db + 1]])

        Wout = [W1, W2]
        BWout = [BW1, BW2]

        def tail_A(tg, tb, thS, BthS):
            xa = x1a[:, tb, :]
            Bxa = Bx1a[tb]
            db = 2 * (tb % 2)
            def mm_pp(e):
                last = None
                for hf in range(2):
                    for k in range(2):
                        last = e.matmul(banks[4 + hf][:, :], lhsT=pTt[:, k, tb * 128:(tb + 1) * 128], rhs=Wple[:, k, hf * 512:(hf + 1) * 512],
                                        start=(k == 0), stop=(k == 1))
                return last
            fw.op("pe", mm_pp, [BpT, BWple], [PB[4], PB[5]])
            for hf in range(2):
                hs_ = slice(hf * 512, (hf + 1) * 512)
                dve(lambda e, hf=hf: e.scalar_tensor_tensor(out=sgt, in0=thS[tb][:, hf, :], scalar=1.0, in1=banks[4 + hf][:, :],
                                                            op0=ALU.add, op1=ALU.mult), [BthS[tb][hf], PB[4 + hf]], [Bsg])
                dve(lambda e, hs_=hs_: e.scalar_tensor_tensor(out=xa[:, hs_], in0=sgt, scalar=0.5, in1=xa[:, hs_], op0=ALU.mult, op1=ALU.add),
                    [Bsg, Bxa], [Bxa])
            for hf in range(2):
                hs_ = slice(hf * 512, (hf + 1) * 512)
                dve(lambda e, hf=hf, hs_=hs_: e.tensor_tensor(out=xa[:, hs_], in0=xa[:, hs_], in1=banks[db + hf][:, :], op=ALU.add),
                    [Bxa, PB[db + hf]], [Bxa])
            ln_stats(xa, Bxa, sttB[tb], mvB[tb], BstB[tb])

        def tail_B(tg, tb, nx):
            xa = x1a[:, tb, :]
            mv_ = mvB[tb]
            wo_, Bwo_ = Wout[tb % 2], BWout[tb % 2]
            act(lambda e: e.activation(out=wo_, in_=xa, func=AF.Identity, scale=mv_[:, 2:3], bias=mv_[:, 3:4]), [Bx1a[tb], BstB[tb]], [Bwo_])
            if nx:
                emit_xload(tg + 1, tb)

        def tail_C(tg, tb):
            T = 4 * tg + tb
            wo_, Bwo_ = Wout[tb % 2], BWout[tb % 2]
            dve(lambda e: e.tensor_tensor(out=wo_, in0=wo_, in1=lnv[:, 2, :], op=ALU.mult), [Bwo_, Blnc], [Bwo_])
            dve(lambda e: e.tensor_tensor(out=wo_, in0=wo_, in1=lnv[:, 3, :], op=ALU.add), [Bwo_, Blnc], [Bwo_])
            fw.dma("sp", f"out{tb % 2}", out_d[T * 128:(T + 1) * 128, :], wo_, reads=[Bwo_])

        def emit_tail_head(tg, thS, BthS):
            g1 = tg + 1
            emit_down(tg, 0)
            emit_down(tg, 1)
            tail_A(tg, 0, thS, BthS)
            emit_down(tg, 2)
            tail_A(tg, 1, thS, BthS)
            tail_B(tg, 0, True)
            tail_C(tg, 0)
            emit_down(tg, 3)
            tail_A(tg, 2, thS, BthS)
            tail_B(tg, 1, True)
            tail_C(tg, 1)
            ln1_s1(g1, 0)
            emit_mix(g1, 1)
            tail_A(tg, 3, thS, BthS)
            tail_B(tg, 2, True)
            tail_C(tg, 2)
            ln1_s1(g1, 1)
            ln1_s2(g1, 0)
            emit_mix(g1, 2, 0)
            tail_B(tg, 3, True)
            tail_C(tg, 3)
            ln1_s1(g1, 2)
            ln1_s2(g1, 1)
            ln1_s3(g1, 0)
            emit_ln1_tr(g1, 0)
            emit_mix(g1, 3, 2)
            ln1_s1(g1, 3)
            ln1_s2(g1, 2)
            ln1_s3(g1, 1)
            emit_ln1_tr(g1, 1)
            ln1_s2(g1, 3)
            ln1_s3(g1, 2)
            emit_ln1_tr(g1, 2)
            ln1_s3(g1, 3)
            emit_ln1_tr(g1, 3)

        def emit_tail(tg, thS, BthS, nx):
            emit_down(tg, 0)
            emit_down(tg, 1)
            tail_A(tg, 0, thS, BthS)
            emit_down(tg, 2)
            tail_A(tg, 1, thS, BthS)
            tail_B(tg, 0, nx)
            tail_C(tg, 0)
            emit_down(tg, 3)
            tail_A(tg, 2, thS, BthS)
            tail_B(tg, 1, nx)
            tail_C(tg, 1)
            tail_A(tg, 3, thS, BthS)
            tail_B(tg, 2, nx)
            tail_C(tg, 2)
            tail_B(tg, 3, nx)
            tail_C(tg, 3)

        for tb in range(4):
            emit_xload(0, tb)
        emit_mix(0, 0)
        emit_head(0)

        for tg in range(4):
            t0 = tg * 512
            fw.dma("pool", "pT", pTt, pT_d[:, t0:t0 + 512].rearrange("(k p) t -> p k t", p=128), writes=[BpT])
            thS = [mixT[:, 2 * tb:2 * tb + 2, t0:t0 + 512] for tb in range(4)]
            BthS = [[Bmix[2 * tb][tg], Bmix[2 * tb + 1][tg]] for tb in range(4)]
            fw.dma("pool", "wog", WoG, wg_d.rearrange("(k p) n -> p k n", p=128), writes=[BWoG])
            if tg == 0:
                for hf in range(2):
                    fw.dma("pool", "wdn", Wdn[:, hf * 11:(hf + 1) * 11, :],
                           wdn_d[hf * 1408:(hf + 1) * 1408, :].rearrange("(k p) n -> p k n", p=128), writes=[BWdn])

            def emit_gate():
                for tb in range(4):
                    def mm_gate(e, tb=tb):
                        last = None
                        for hf in range(2):
                            for k in range(8):
                                last = e.matmul(banks[6 + hf][:, :], lhsT=x1T[:, k, tb * 128:(tb + 1) * 128], rhs=WoG[:, k, hf * 512:(hf + 1) * 512],
                                                start=(k == 0), stop=(k == 7))
                        return last
                    fw.op("pe", mm_gate, [Bx1T, BWoG], [PB[6], PB[7]])
                    for hf in range(2):
                        act(lambda e, hf=hf, tb=tb: e.activation(out=thS[tb][:, hf, :], in_=banks[6 + hf][:, :], func=AF.Tanh, scale=0.5),
                            [PB[6 + hf]], [BthS[tb][hf]])

            def emit_up(i):
                cidx, s2 = i // 2, i % 2
                wt_, Bw_ = wup[cidx % NWUP]
                dg_, Bdg_ = dgs[i % 2]
                for j in range(3):
                    for gv in range(2):
                        fb = i + 22 * gv
                        dve(lambda e, j=j, gv=gv, fb=fb, dg_=dg_: e.tensor_scalar(out=dg_[:, gv * 3 + j, :], in0=identb,
                                                                                  scalar1=CB("cw")[:, fb * 3 + j:fb * 3 + j + 1], scalar2=None,
                                                                                  op0=ALU.mult), [Bidb, BcstB], [Bdg_])
                for gv in range(2):
                    bk = 2 * (i % 2) + gv

                    def mm_up(e, gv=gv, bk=bk, wt_=wt_, s2=s2):
                        last = None
                        c0 = gv * 256 + s2 * 128
                        for k in range(8):
                            last = e.matmul(banks[bk][:, :], lhsT=wt_[:, k, c0:c0 + 128], rhs=x1T[:, k, :], start=(k == 0), stop=(k == 7))
                        return last
                    fw.op("pe", mm_up, [Bw_, Bx1T], [PB[bk]])

            def emit_conv(i):
                hg, Bhg, hv, Bhv = hss[i % 2]
                dg_, Bdg_ = dgs[i % 2]
                for gv, (ht, Bh) in enumerate(((hg, Bhg), (hv, Bhv))):
                    fb = i + 22 * gv
                    bk = 2 * (i % 2) + gv
                    act(lambda e, ht=ht, fb=fb: e.activation(out=ht[:, 0:2], in_=tails[:, fb, :], func=AF.Copy), [Btl], [Bh])
                    if gv == 0:
                        act(lambda e, ht=ht, bk=bk: e.activation(out=ht[:, 2:514], in_=banks[bk][:, :], func=AF.Copy), [PB[bk]], [Bh])
                    else:
                        dve(lambda e, ht=ht, bk=bk: e.tensor_copy(out=ht[:, 2:514], in_=banks[bk][:, :]), [PB[bk]], [Bh])
                    dve(lambda e, ht=ht, fb=fb: e.tensor_copy(out=tails[:, fb, :], in_=ht[:, 512:514]), [Bh], [Btl])

                    def mm_cv(e, gv=gv, ht=ht, dg_=dg_):
                        last = None
                        for j in range(3):
                            last = e.matmul(banks[4 + gv][:, :], lhsT=dg_[:, gv * 3 + j, :], rhs=ht[:, j:j + 512], start=(j == 0), stop=(j == 2))
                        return last
                    fw.op("pe", mm_cv, [Bh, Bdg_], [PB[4 + gv]])
                act(lambda e, i=i: e.activation(out=sgt, in_=banks[4][:, :], func=AF.Silu, bias=CB("cb")[:, i:i + 1]), [PB[4], BcstB], [Bsg])
                dve(lambda e, i=i: e.scalar_tensor_tensor(out=actT[:, i, :], in0=banks[5][:, :], scalar=CB("cb")[:, 22 + i:23 + i], in1=sgt,
                                                          op0=ALU.add, op1=ALU.mult), [PB[5], Bsg, BcstB], [Bact])
            emit_up(0)
            for i in range(NPAIR):
                if i % 2 == 0 and i // 2 + 2 < NCH:
                    emit_wdma(i // 2 + 2)
                if i == 8 and tg + 1 < 4:
                    fw.dma("pool", "wog", WoG, wo_d.rearrange("(k p) n -> p k n", p=128), writes=[BWoG])
                if i + 1 < NPAIR:
                    emit_up(i + 1)
                emit_conv(i)
                if i == 5:
                    emit_gate()
            nx = tg + 1 < 4
            if nx:
                emit_wdma(0)
                emit_wdma(1)
                emit_mix(tg + 1, 0)
            if nx:
                emit_tail_head(tg, thS, BthS)
            else:
                emit_tail(tg, thS, BthS, False)
        fw.barrier()
    return nc


def _prep_shared(inp):
    f = lambda a: np.ascontiguousarray(np.asarray(a, dtype=np.float32))
    cst = np.zeros((128, NCST), np.float32)

    def put(n, a):
        a = np.asarray(a, np.float32).reshape(128, -1)
        assert a.shape[1] == _c[n][1] - _c[n][0], (n, a.shape)
        cst[:, _c[n][0]:_c[n][1]] = a
    lamq = np.stack([inp["diff_lambda_q1"][0], inp["diff_lambda_k1"][0], inp["diff_lambda_q2"][0], inp["diff_lambda_k2"][0]], 0)
    put("lamq", np.broadcast_to(lamq.reshape(1, 256), (128, 256)))
    put("gcol", inp["diff_subln_g"][0].reshape(128, 1))
    gl = lambda a: np.asarray(a).reshape(16, 2, *a.shape[1:])
    put("lr", gl(inp["ssm_lambda_re"][0]).transpose(1, 2, 0).reshape(128, 16))
    put("li", gl(inp["ssm_lambda_im"][0]).transpose(1, 2, 0).reshape(128, 16))
    ldt = np.broadcast_to(np.asarray(inp["ssm_log_dt"][0]).reshape(16, 2, 1), (16, 2, 64))
    put("ldt", ldt.transpose(1, 2, 0).reshape(128, 16))
    put("bre", gl(inp["ssm_b_re"][0]).transpose(1, 2, 0, 3).reshape(128, 256))
    put("bim", gl(inp["ssm_b_im"][0]).transpose(1, 2, 0, 3).reshape(128, 256))
    put("cre", gl(inp["ssm_c_re"][0]).transpose(1, 3, 0, 2).reshape(128, 256))
    put("cim", gl(inp["ssm_c_im"][0]).transpose(1, 3, 0, 2).reshape(128, 256))
    dl = np.broadcast_to(np.asarray(inp["ssm_d"][0]).T.reshape(1, 16, 32), (8, 16, 32))
    put("dl", dl.reshape(128, 32))
    put("bglu", np.asarray(inp["ssm_b_glu"][0]).reshape(4, 128).T)
    put("cw", np.asarray(inp["ffn_conv_w"][0]).reshape(3, 44, 128).transpose(2, 1, 0).reshape(128, 132))
    put("cb", np.asarray(inp["ffn_conv_b"][0]).reshape(44, 128).T)
    put("ident", np.eye(128, dtype=np.float32))
    tk = np.arange(128)[:, None]
    tq = np.arange(128)[None, :]
    put("maskb", np.where(tk > tq, -30000.0, 0.0))
    j = (np.arange(128) // 16)
    put("tmask", (j[:, None] <= j[None, :]).astype(np.float32))
    put("kvec", np.broadcast_to(np.asarray(KV, np.float32).reshape(1, 25), (128, 25)))
    put("cidx", np.broadcast_to(np.arange(128, dtype=np.float32).reshape(1, 128), (128, 128)))
    put("i2", np.concatenate([np.eye(64), np.eye(64)], 0))
    put("g1c", np.asarray(inp["ln1_g"][0]).reshape(8, 128).T)
    put("b1c", np.asarray(inp["ln1_b"][0]).reshape(8, 128).T)
    lnc = np.concatenate([np.broadcast_to(np.asarray(inp[k][0]).reshape(1, D), (128, D))
                          for k in ("ln1_g", "ln1_b", "ln2_g", "ln2_b")], 1)
    return {"w_in": f(inp["w_in"][0]), "w_o": f(inp["w_o"][0]), "w_glu": f(inp["ssm_w_glu"][0]),
            "w_up": f(inp["ffn_w_up"][0]), "w_down": f(inp["ffn_w_down"][0]), "w_ple": f(inp["w_ple"][0]),
            "w_gate": f(inp["w_ple_gate"][0]), "cst": cst, "lnc": f(lnc)}


def make_in_maps(inp):
    shared = _prep_shared(inp)
    x = np.asarray(inp["x"], np.float32)
    p = np.asarray(inp["p"], np.float32)
    maps = []
    for b in range(8):
        m = dict(shared)
        m["x"] = np.ascontiguousarray(x[b])
        m["xT"] = np.ascontiguousarray(x[b].T)
        m["pT"] = np.ascontiguousarray(p[0, b].T)
        maps.append(m)
    return maps


def kernel(**inputs):
    nc = build_nc()
    maps = make_in_maps(inputs)
    res = run_bass_kernel_spmd(nc, maps, core_ids=list(range(8)))
    return np.stack([r["out"] for r in res.results], 0).astype(np.float32)
```

```python
import math
from contextlib import ExitStack
import numpy as np
import concourse.bass as bass
import concourse.mybir as mybir
from concourse.bass_utils import run_bass_kernel_spmd
from concourse.alu_op_type import AluOpType as ALU

F32 = mybir.dt.float32
BF16 = mybir.dt.bfloat16
I32 = mybir.dt.int32
AF = mybir.ActivationFunctionType

L = 2048
D = 1024
DFF = 2816
NPAIR = DFF // 128
LN_EPS = 1e-5
ALPHA = 2.0 ** 0.25
LAM_INIT = 0.8 - 0.6 * math.exp(0.0)
TWO_PI_LO = 6.28318

_c = {}
_off = 0
for _n, _w in (("lamq", 256), ("gcol", 1), ("lr", 16), ("li", 16), ("ldt", 16), ("bre", 256), ("bim", 256),
               ("cre", 256), ("cim", 256), ("dl", 32), ("bglu", 4), ("cw", 132), ("cb", 44),
               ("ident", 128), ("maskb", 128), ("tmask", 128), ("kvec", 25), ("cidx", 128), ("i2", 64), ("g1c", 8), ("b1c", 8)):
    _c[_n] = (_off, _off + _w)
    _off += _w
NCST = _off
KV = [0, -1, -2, -3, -4, -5, -6, -7] + list(range(0, 17))


class Buf:
    __slots__ = ("w", "r", "name")

    def __init__(self, name=""):
        self.w = None
        self.r = {}
        self.name = name


class Eng:
    def __init__(self, name, handle, sem):
        self.name = name
        self.h = handle
        self.sem = sem
        self.count = 0
        self.waited = {}


class FW:
    def __init__(self, nc, stack):
        self.nc = nc
        self.stack = stack
        self.engs = {}
        for name, h in (("pe", nc.tensor), ("act", nc.scalar), ("dve", nc.vector),
                        ("pool", nc.gpsimd), ("sp", nc.sync)):
            sem = stack.enter_context(nc.semaphore("sem_" + name))
            self.engs[name] = Eng(name, h, sem)
        self.dma_sems = {}

    def dma_slot(self, name):
        if name not in self.dma_sems:
            sem = self.stack.enter_context(self.nc.semaphore("dsem_" + name))
            self.dma_sems[name] = [sem, 0]
        return self.dma_sems[name]

    def _wait(self, eng, tok):
        key, sem, val = tok
        if eng.waited.get(key, 0) >= val:
            return
        eng.h.wait_ge(sem, val)
        eng.waited[key] = val

    def _deps(self, eng, reads, writes):
        best = {}

        def add(t):
            if t is None:
                return
            if eng.name == "pe" and t[0] == "pe":
                return
            if t[0] not in best or best[t[0]][2] < t[2]:
                best[t[0]] = t
        for b in reads:
            add(b.w)
        for b in writes:
            add(b.w)
            for k, t in b.r.items():
                add(t)
        for t in best.values():
            self._wait(eng, t)

    def _commit(self, tok, reads, writes):
        for b in reads:
            b.r[tok[0]] = tok
        for b in writes:
            b.w = tok
            b.r = {}

    def op(self, engname, fn, reads=(), writes=(), acc=False):
        eng = self.engs[engname]
        if engname == "pe" and not acc:
            for b in writes:
                if b.w is not None and b.w[0] == "pe" and not b.r:
                    raise RuntimeError(f"PSUM clobber: PE overwrites {b.name} before anyone read the previous PE result")
        self._deps(eng, reads, writes)
        ins = fn(eng.h)
        ins.then_inc(eng.sem, 1)
        eng.count += 1
        tok = (eng.name, eng.sem, eng.count)
        self._commit(tok, reads, writes)
        return tok

    def dma(self, qname, slot, out, in_, reads=(), writes=(), **kw):
        eng = self.engs[qname]
        self._deps(eng, reads, writes)
        s = self.dma_slot(slot)
        ins = eng.h.dma_start(out=out, in_=in_, **kw)
        ins.then_inc(s[0], 16)
        s[1] += 16
        tok = ("dma_" + slot, s[0], s[1])
        self._commit(tok, reads, writes)
        return tok

    def barrier(self):
        toks = [(e.name, e.sem, e.count) for e in self.engs.values() if e.count > 0]
        toks += [("dma_" + k, s[0], s[1]) for k, s in self.dma_sems.items() if s[1] > 0]
        for e in self.engs.values():
            for t in toks:
                if t[0] != e.name:
                    self._wait(e, t)


class Arena:
    def __init__(self, tensor, nbytes):
        self.t = tensor
        self.n = nbytes
        self.off = 0
        self.top = nbytes
        self.marks = []

    def alloc_top(self, cols, dt):
        esz = 4 if dt in (F32, I32) else 2
        nb = (cols * esz + 31) // 32 * 32
        assert self.top - nb >= self.off, ("SBUF arena overflow (top)", self.off, nb, self.top)
        self.top -= nb
        a = self.t[:, self.top // 4:(self.top + nb) // 4]
        if dt != F32:
            a = a.bitcast(dt)
        return a[:, 0:cols]

    def alloc(self, cols, dt):
        esz = 4 if dt in (F32, I32) else 2
        nb = (cols * esz + 31) // 32 * 32
        assert self.off + nb <= self.top, ("SBUF arena overflow", self.off, nb, self.top)
        a = self.t[:, self.off // 4:(self.off + nb) // 4]
        self.off += nb
        if dt != F32:
            a = a.bitcast(dt)
        return a[:, 0:cols]

    def mark(self):
        return self.off

    def reset(self, m):
        self.off = m


def build_nc(debug=None):
    nc = bass.Bass("TRN2", target_bir_lowering=False)
    dram = lambda n, s, k="ExternalInput": nc.dram_tensor(n, s, F32, kind=k).ap()
    xT_d = dram("xT", [D, L])
    x_d = dram("x", [L, D])
    pT_d = dram("pT", [256, L])
    win_d = dram("w_in", [D, 2048])
    wo_d = dram("w_o", [D, D])
    wglu_d = dram("w_glu", [512, 512])
    wup_d = dram("w_up", [D, 2 * DFF])
    wdn_d = dram("w_down", [DFF, D])
    wple_d = dram("w_ple", [256, D])
    wg_d = dram("w_gate", [D, D])
    cst_d = dram("cst", [128, NCST])
    lnc_d = dram("lnc", [128, 4 * D])
    out_d = dram("out", [L, D], "ExternalOutput")
    dbg_d = dram("dbg", [D, L], "ExternalOutput") if debug else None

    with ExitStack() as st:
        fw = FW(nc, st)
        ARENA_BYTES = 212736
        arena_t = st.enter_context(nc.sbuf_tensor("arena", [128, ARENA_BYTES // 4], F32))
        ar = Arena(arena_t, ARENA_BYTES)
        banks = [st.enter_context(nc.psum_tensor(f"bank{i}", [128, 512], F32)) for i in range(8)]
        PB = [Buf(f"bank{i}") for i in range(8)]
        pbf = lambda i: banks[i][:, :].bitcast(BF16)

        def A(cols, dt, name=""):
            return ar.alloc(cols, dt), Buf(name)

        NCB = 192
        cstB, BcstB = A(NCB, F32, "cstB")
        mixT, _ = A(8 * L, BF16, "mixT")
        mixT = mixT.rearrange("p (k t) -> p k t", k=8)
        Bmix = [[Buf(f"mix{k}_{g}") for g in range(4)] for k in range(8)]
        identb, Bidb = A(128, BF16)
        maskb, Bmkb = A(128, BF16)
        fw.dma("sp", "cstB", cstB[:, 0:176], cst_d[:, _c["cw"][0]:_c["cb"][1]], writes=[BcstB])
        fw.dma("sp", "cstB", cstB[:, 176:192], cst_d[:, _c["g1c"][0]:_c["b1c"][1]], writes=[BcstB])
        _cb = {"cw": (0, 132), "cb": (132, 176), "g1c": (176, 184), "b1c": (184, 192)}
        CB = lambda n: cstB[:, _cb[n][0]:_cb[n][1]]
        PM = ar.mark()
        cst, Bcst = A(NCST, F32, "cst")
        fw.dma("sp", "cst", cst, cst_d, writes=[Bcst])
        C = lambda n: cst[:, _c[n][0]:_c[n][1]]
        fw.op("dve", lambda e: e.tensor_copy(out=identb, in_=C("ident")), reads=[Bcst], writes=[Bidb])
        fw.op("dve", lambda e: e.tensor_copy(out=maskb, in_=C("maskb")), reads=[Bcst], writes=[Bmkb])

        xT, _ = A(8 * L, BF16, "xT")
        xT = xT.rearrange("p (k t) -> p k t", k=8)
        BxT = Buf("xT")
        for kb in range(8):
            fw.dma("pool", "xT", xT[:, kb, :], xT_d[kb * 128:(kb + 1) * 128, :], writes=[BxT], max_dma_last_dim=4096)
        wslot = []
        for i in range(2):
            t, b = A(8 * 512, BF16, f"wslot{i}")
            wslot.append((t.rearrange("p (k n) -> p k n", k=8), b))
        wglu, Bwglu = A(4 * 512, BF16, "wglu")
        wglu = wglu.rearrange("p (k n) -> p k n", k=4)
        fw.dma("pool", "wglu", wglu, wglu_d.rearrange("(k p) n -> p k n", p=128), writes=[Bwglu])
        def load_w(slot_i, src_cols_list):
            t, b = wslot[slot_i]
            o = 0
            for (c0, c1) in src_cols_list:
                fw.dma("pool", f"ws{slot_i}", t[:, :, o:o + (c1 - c0)],
                       win_d[:, c0:c1].rearrange("(k p) n -> p k n", p=128), writes=[b])
                o += c1 - c0
            return t, b

        head_cols = lambda h: [(128 * h, 128 * h + 128), (512 + 128 * h, 512 + 128 * h + 128), (1024 + 128 * h, 1024 + 128 * h + 128)]
        wtU, BwU = load_w(0, [(1536, 2048)])
        head_w = {0: load_w(1, head_cols(0))}

        PM_A = ar.mark()
        def T_(cols, name=""):
            return A(cols, F32, name)
        s_dt, Bs = T_(16)
        s_a, _ = T_(16); s_th, _ = T_(16); s_q, _ = T_(16); s_fq, _ = T_(16)
        s_i, _ = A(16, I32); s_n, _ = T_(16)
        V = lambda e: e
        dve = lambda fn, r, w: fw.op("dve", fn, reads=r, writes=w)
        act = lambda fn, r, w: fw.op("act", fn, reads=r, writes=w)
        act(lambda e: e.activation(out=s_dt, in_=C("ldt"), func=AF.Exp), [Bcst], [Bs])
        dve(lambda e: e.tensor_tensor(out=s_a, in0=C("lr"), in1=s_dt, op=ALU.mult), [Bcst, Bs], [Bs])
        dve(lambda e: e.tensor_tensor(out=s_th, in0=C("li"), in1=s_dt, op=ALU.mult), [Bcst, Bs], [Bs])
        dve(lambda e: e.tensor_scalar(out=s_q, in0=s_th, scalar1=1.0 / (2 * math.pi), scalar2=None, op0=ALU.mult), [Bs], [Bs])

        def frac(dst, src, itmp, ftmp, shape_reads):
            dve(lambda e: e.tensor_copy(out=itmp, in_=src), shape_reads, shape_reads)
            dve(lambda e: e.tensor_copy(out=ftmp, in_=itmp), shape_reads, shape_reads)
            dve(lambda e: e.tensor_tensor(out=dst, in0=src, in1=ftmp, op=ALU.subtract), shape_reads, shape_reads)
        frac(s_fq, s_q, s_i, s_n, [Bs])
        NK = 25
        pwr, _ = T_(16 * NK); pwi, _ = T_(16 * NK); ph, _ = T_(16 * NK); ph2, _ = T_(16 * NK)
        phi_, _ = A(16 * NK, I32); ex, _ = T_(16 * NK)
        v3 = lambda t: t.rearrange("p (g k) -> p g k", g=16)
        kv3 = C("kvec").unsqueeze(1).to_broadcast([128, 16, NK])
        bc3 = lambda t, n: t.unsqueeze(2).to_broadcast([128, 16, n])
        dve(lambda e: e.tensor_tensor(out=v3(ph), in0=kv3, in1=bc3(s_fq, NK), op=ALU.mult), [Bcst, Bs], [Bs])
        frac(ph, ph, phi_, ph2, [Bs])
        act(lambda e: e.activation(out=pwi, in_=ph, func=AF.Sin, scale=TWO_PI_LO), [Bs], [Bs])
        dve(lambda e: e.tensor_scalar(out=ph, in0=ph, scalar1=0.25, scalar2=None, op0=ALU.add), [Bs], [Bs])
        frac(ph, ph, phi_, ph2, [Bs])
        act(lambda e: e.activation(out=pwr, in_=ph, func=AF.Sin, scale=TWO_PI_LO), [Bs], [Bs])
        dve(lambda e: e.tensor_tensor(out=v3(ph2), in0=kv3, in1=bc3(s_a, NK), op=ALU.mult), [Bcst, Bs], [Bs])
        act(lambda e: e.activation(out=ex, in_=ph2, func=AF.Exp), [Bs], [Bs])
        dve(lambda e: e.tensor_tensor(out=pwr, in0=pwr, in1=ex, op=ALU.mult), [Bs], [Bs])
        dve(lambda e: e.tensor_tensor(out=pwi, in0=pwi, in1=ex, op=ALU.mult), [Bs], [Bs])
        pwr3, pwi3 = v3(pwr), v3(pwi)
        IDX1 = 9
        den, _ = T_(16); t1, _ = T_(16); t2, _ = T_(16); zr, _ = T_(16); fr, _ = T_(16); fi, _ = T_(16)
        lam_r, lam_i = pwr3[:, :, IDX1], pwi3[:, :, IDX1]
        dve(lambda e: e.tensor_tensor(out=den, in0=C("lr"), in1=C("lr"), op=ALU.mult), [Bcst, Bs], [Bs])
        dve(lambda e: e.tensor_tensor(out=t1, in0=C("li"), in1=C("li"), op=ALU.mult), [Bcst, Bs], [Bs])
        dve(lambda e: e.tensor_tensor(out=den, in0=den, in1=t1, op=ALU.add), [Bs], [Bs])
        dve(lambda e: e.reciprocal(out=den, in_=den), [Bs], [Bs])
        dve(lambda e: e.tensor_scalar(out=zr, in0=lam_r, scalar1=-1.0, scalar2=None, op0=ALU.add), [Bs], [Bs])
        dve(lambda e: e.tensor_tensor(out=t1, in0=zr, in1=C("lr"), op=ALU.mult), [Bcst, Bs], [Bs])
        dve(lambda e: e.tensor_tensor(out=t2, in0=lam_i, in1=C("li"), op=ALU.mult), [Bcst, Bs], [Bs])
        dve(lambda e: e.tensor_tensor(out=t1, in0=t1, in1=t2, op=ALU.add), [Bs], [Bs])
        dve(lambda e: e.tensor_tensor(out=fr, in0=t1, in1=den, op=ALU.mult), [Bs], [Bs])
        dve(lambda e: e.tensor_tensor(out=t1, in0=lam_i, in1=C("lr"), op=ALU.mult), [Bcst, Bs], [Bs])
        dve(lambda e: e.tensor_tensor(out=t2, in0=zr, in1=C("li"), op=ALU.mult), [Bcst, Bs], [Bs])
        dve(lambda e: e.tensor_tensor(out=t1, in0=t1, in1=t2, op=ALU.subtract), [Bs], [Bs])
        dve(lambda e: e.tensor_tensor(out=fi, in0=t1, in1=den, op=ALU.mult), [Bs], [Bs])
        bbr, _ = T_(256); bbi, _ = T_(256); tq1, _ = T_(256)
        g16 = lambda t: t.rearrange("p (g h) -> p g h", g=16)
        dve(lambda e: e.tensor_tensor(out=g16(bbr), in0=g16(C("bre")), in1=bc3(fr, 16), op=ALU.mult), [Bcst, Bs], [Bs])
        dve(lambda e: e.tensor_tensor(out=g16(tq1), in0=g16(C("bim")), in1=bc3(fi, 16), op=ALU.mult), [Bcst, Bs], [Bs])
        dve(lambda e: e.tensor_tensor(out=bbr, in0=bbr, in1=tq1, op=ALU.subtract), [Bs], [Bs])
        dve(lambda e: e.tensor_tensor(out=g16(bbi), in0=g16(C("bim")), in1=bc3(fr, 16), op=ALU.mult), [Bcst, Bs], [Bs])
        dve(lambda e: e.tensor_tensor(out=g16(tq1), in0=g16(C("bre")), in1=bc3(fi, 16), op=ALU.mult), [Bcst, Bs], [Bs])
        dve(lambda e: e.tensor_tensor(out=bbi, in0=bbi, in1=tq1, op=ALU.add), [Bs], [Bs])
        Rcol, _ = T_(16); f16, _ = T_(16)
        act(lambda e: e.activation(out=Rcol, in_=s_a, func=AF.Exp, scale=16.0), [Bs], [Bs])
        dve(lambda e: e.tensor_scalar(out=f16, in0=s_fq, scalar1=16.0, scalar2=None, op0=ALU.mult), [Bs], [Bs])
        frac(f16, f16, s_i, s_n, [Bs])
        zero1, _ = T_(1)
        dve(lambda e: e.memset(zero1, 0.0), [], [Bs])

        NG = 4
        NGR = 2 * NG
        A0r_s = [A(NG * 128, BF16)[0] for _ in range(2)]; A0i_s = [A(NG * 128, BF16)[0] for _ in range(2)]
        Bmr_s = [A(NG * 272, BF16)[0] for _ in range(2)]; Bmi_s = [A(NG * 272, BF16)[0] for _ in range(2)]
        BA0_s = [Buf("A0a"), Buf("A0b")]; BBm_s = [Buf("Bma"), Buf("Bmb")]
        tA, BtA = T_(NG * 272); tB, _ = T_(NG * 272)
        Dg_s = [A(NG * 2 * 3 * 64, BF16)[0] for _ in range(2)]
        BDg_s = [Buf("Dga"), Buf("Dgb")]
        TT, _ = A(NGR * 256, BF16)
        GmR, _ = A(NGR * 2 * 64, BF16); GmI, _ = A(NGR * 2 * 64, BF16)
        cosA, _ = T_(16 * 128); sinA, _ = T_(16 * 128)
        rph, _ = T_(4 * 128); rph2, _ = T_(4 * 128)
        rphi, _ = A(4 * 128, I32)
        SRsh, _ = A(NG * 128, BF16); SIsh, _ = A(NG * 128, BF16)
        l2a, _ = T_(NG * 128); l2b, _ = T_(NG * 128); l2c, _ = T_(NG * 128); l2d, _ = T_(NG * 128)
        Utok, _ = A(NGR * 256, BF16)
        UT, _ = A(NGR * 2 * 128, BF16)
        ygT, _ = A(4 * L, BF16)
        ygT = ygT.rearrange("p (k t) -> p k t", k=4)
        gtmp, _ = T_(512); gtmp2, _ = T_(512)
        BDg, BTT, BGm, Bl2x, BS, Bl2, BU, BUT, Bg = (Buf(n) for n in "Dg TT Gm l2x S l2 U UT g".split())
        BYt = BU
        Bg2 = Buf("g2")
        Byg = [Buf(f"yg{k}") for k in range(4)]
        tA4 = tA.rearrange("p (g t h) -> p g t h", g=NG, t=17)
        tB4 = tB.rearrange("p (g t h) -> p g t h", g=NG, t=17)
        Dg5_s = [d.rearrange("p (g j k n) -> p g j k n", g=NG, j=2, k=3) for d in Dg_s]
        TT3 = TT.rearrange("p (g n) -> p g n", g=NGR)
        GmR4 = GmR.rearrange("p (g j n) -> p g j n", g=NGR, j=2)
        GmI4 = GmI.rearrange("p (g j n) -> p g j n", g=NGR, j=2)
        Brot = Buf("rot")
        for hh in range(4):
            gsl = slice(4 * hh, 4 * hh + 4)
            csl = slice(4 * hh * 128, (4 * hh + 4) * 128)
            cid3 = C("cidx").unsqueeze(1).to_broadcast([128, 4, 128])
            dve(lambda e, gsl=gsl: e.tensor_tensor(out=rph.rearrange("p (g c) -> p g c", g=4), in0=cid3,
                                                   in1=f16[:, gsl].unsqueeze(2).to_broadcast([128, 4, 128]), op=ALU.mult), [Bs, Bcst], [Bl2x])
            frac(rph, rph, rphi, rph2, [Bl2x])
            act(lambda e, csl=csl: e.activation(out=sinA[:, csl], in_=rph, func=AF.Sin, scale=TWO_PI_LO), [Bl2x], [Brot])
            dve(lambda e: e.tensor_scalar(out=rph, in0=rph, scalar1=0.25, scalar2=None, op0=ALU.add), [Bl2x], [Bl2x])
            frac(rph, rph, rphi, rph2, [Bl2x])
            act(lambda e, csl=csl: e.activation(out=cosA[:, csl], in_=rph, func=AF.Sin, scale=TWO_PI_LO), [Bl2x], [Brot])
        SR3 = SRsh.rearrange("p (g c) -> p g c", g=NG)
        SI3 = SIsh.rearrange("p (g c) -> p g c", g=NG)
        Ut4 = Utok.rearrange("p (g t h) -> p g t h", g=NGR, t=16)
        Yt4 = Utok.rearrange("p (t g h) -> p t g h", t=16, g=NGR)
        UT4 = UT.rearrange("p (g j c) -> p g j c", g=NGR, j=2)
        dve(lambda e: e.memset(SRsh, 0.0), [], [BS])
        dve(lambda e: e.memset(SIsh, 0.0), [], [BS])

        def emit_tables(qd):
            gp0 = qd * NG
            gps = slice(gp0, gp0 + NG)
            bcj = lambda t, idx0, n, m: t[:, gps, idx0:idx0 + n].unsqueeze(3).to_broadcast([128, NG, n, m])
            bch = lambda t, n: g16(t)[:, gps, :].unsqueeze(2).to_broadcast([128, NG, n, 16])
            A0r4 = A0r_s[qd % 2].rearrange("p (g j h) -> p g j h", g=NG, j=8)
            A0i4 = A0i_s[qd % 2].rearrange("p (g j h) -> p g j h", g=NG, j=8)
            Bmr4 = Bmr_s[qd % 2].rearrange("p (g t h) -> p g t h", g=NG, t=17)
            Bmi4 = Bmi_s[qd % 2].rearrange("p (g t h) -> p g t h", g=NG, t=17)
            BA0, BBm = BA0_s[qd % 2], BBm_s[qd % 2]
            Dg5 = Dg5_s[qd % 2]
            BDg = BDg_s[qd % 2]
            pl = lambda fn, r, w: fw.op("dve", fn, reads=r, writes=w)
            tA8 = tA[:, 0:NG * 128].rearrange("p (g j h) -> p g j h", g=NG, j=8)
            tB8 = tB[:, 0:NG * 128].rearrange("p (g j h) -> p g j h", g=NG, j=8)
            pl(lambda e: e.tensor_tensor(out=tA8, in0=bcj(pwr3, 0, 8, 16), in1=bch(bbr, 8), op=ALU.mult), [Bs], [BtA])
            pl(lambda e: e.tensor_tensor(out=tB8, in0=bcj(pwi3, 0, 8, 16), in1=bch(bbi, 8), op=ALU.mult), [Bs], [BtA])
            pl(lambda e: e.tensor_tensor(out=A0r4, in0=tA8, in1=tB8, op=ALU.subtract), [BtA], [BA0])
            pl(lambda e: e.tensor_tensor(out=tA8, in0=bcj(pwr3, 0, 8, 16), in1=bch(bbi, 8), op=ALU.mult), [Bs], [BtA])
            pl(lambda e: e.tensor_tensor(out=tB8, in0=bcj(pwi3, 0, 8, 16), in1=bch(bbr, 8), op=ALU.mult), [Bs], [BtA])
            pl(lambda e: e.tensor_tensor(out=A0i4, in0=tA8, in1=tB8, op=ALU.add), [BtA], [BA0])
            cch = lambda n: g16(C(n))[:, gps, :].unsqueeze(2).to_broadcast([128, NG, 17, 16])
            pl(lambda e: e.tensor_tensor(out=tA4, in0=bcj(pwr3, 8, 17, 16), in1=cch("cre"), op=ALU.mult), [Bs, Bcst], [BtA])
            pl(lambda e: e.tensor_tensor(out=tB4, in0=bcj(pwi3, 8, 17, 16), in1=cch("cim"), op=ALU.mult), [Bs, Bcst], [BtA])
            pl(lambda e: e.tensor_tensor(out=Bmr4, in0=tA4, in1=tB4, op=ALU.subtract), [BtA], [BBm])
            pl(lambda e: e.tensor_tensor(out=tA4, in0=bcj(pwr3, 8, 17, 16), in1=cch("cim"), op=ALU.mult), [Bs, Bcst], [BtA])
            pl(lambda e: e.tensor_tensor(out=tB4, in0=bcj(pwi3, 8, 17, 16), in1=cch("cre"), op=ALU.mult), [Bs, Bcst], [BtA])
            pl(lambda e: e.tensor_tensor(out=tA4, in0=tA4, in1=tB4, op=ALU.add), [BtA], [BtA])
            pl(lambda e: e.tensor_scalar(out=Bmi4, in0=tA4, scalar1=-1.0, scalar2=None, op0=ALU.mult), [BtA], [BBm])
            for gl in range(NG):
                for jb in range(2):
                    idx = 8 + 15 - 8 * jb
                    for kind in range(3):
                        src = (pwr3, pwi3, pwi3)[kind][:, gp0 + gl, idx:idx + 1]
                        sgn = (1.0, 1.0, -1.0)[kind]
                        dve(lambda e, gl=gl, jb=jb, kind=kind, src=src, sgn=sgn: e.tensor_scalar(
                            out=Dg5[:, gl, jb, kind, :], in0=C("i2"), scalar1=src, scalar2=sgn, op0=ALU.mult, op1=ALU.mult),
                            [Bs, Bcst], [BDg])
        UCOL0 = 1536
        emit_tables(0)
        for qd in range(16 // NG):
            gp0 = qd * NG
            gps = slice(gp0, gp0 + NG)
            wt, Bw = wtU, BwU
            A0r4 = A0r_s[qd % 2].rearrange("p (g j h) -> p g j h", g=NG, j=8)
            A0i4 = A0i_s[qd % 2].rearrange("p (g j h) -> p g j h", g=NG, j=8)
            Bmr4 = Bmr_s[qd % 2].rearrange("p (g t h) -> p g t h", g=NG, t=17)
            Bmi4 = Bmi_s[qd % 2].rearrange("p (g t h) -> p g t h", g=NG, t=17)
            BA0, BBm = BA0_s[qd % 2], BBm_s[qd % 2]
            Dg5 = Dg5_s[qd % 2]
            BDg = BDg_s[qd % 2]
            cosT = cosA[:, gp0 * 128:(gp0 + NG) * 128]
            sinT = sinA[:, gp0 * 128:(gp0 + NG) * 128]

            for tb4 in range(4):
                bk = 4 + tb4 % 2
                ps = banks[bk]

                def mm_u(e, tb4=tb4, ps=ps):
                    last = None
                    for ti in range(4):
                        tau = tb4 * 4 + ti
                        for kb in range(8):
                            last = e.matmul(ps[:, ti * 128:(ti + 1) * 128], lhsT=xT[:, kb, tau:L:16], rhs=wt[:, kb, 128 * qd:128 * qd + 128],
                                            start=(kb == 0 and ti == 0), stop=(kb == 7), skip_group_check=True)
                    return last
                fw.op("pe", mm_u, [BxT, Bw], [PB[bk]])
                src = ps[:, :].rearrange("p (t g h) -> p t g h", t=4, g=NGR)
                dst = Ut4[:, :, tb4 * 4:tb4 * 4 + 4, :].rearrange("p g t h -> p t g h")
                act(lambda e, src=src, dst=dst: e.activation(out=dst, in_=src, func=AF.Copy), [PB[bk]], [BU])
            for half in range(2):
                bk = 6 + half
                psb = pbf(bk)

                def tr_u(e, half=half, psb=psb):
                    last = None
                    for i in range(8):
                        gl, jb = (half * 8 + i) // 2, (half * 8 + i) % 2
                        last = e.transpose(psb[:, i * 128:(i + 1) * 128],
                                           Ut4[:, gl, jb * 8:(jb + 1) * 8, :].rearrange("p j h -> p (j h)"), identb)
                    return last
                fw.op("pe", tr_u, [BU, Bidb], [PB[bk]])
                dst = UT[:, half * 1024:(half + 1) * 1024]
                act(lambda e, dst=dst, psb=psb: e.activation(out=dst, in_=psb, func=AF.Copy), [PB[bk]], [BUT])
            for pb in range(NGR // 2):
                def mm_toep(e, pb=pb):
                    last = None
                    for k in range(2):
                        gl = 4 * (pb // 2) + (pb % 2) + 2 * k
                        gpl, P0 = gl // 2, 64 * (gl % 2)
                        o = banks[pb][:, k * 256:(k + 1) * 256]
                        e.matmul(o, lhsT=A0r4[P0:P0 + 64, gpl].rearrange("p j h -> p (j h)"),
                                 rhs=Bmr4[P0:P0 + 64, gpl, 0:16].rearrange("p t h -> p (t h)"), start=(k == 0), stop=False, skip_group_check=True)
                        last = e.matmul(o, lhsT=A0i4[P0:P0 + 64, gpl].rearrange("p j h -> p (j h)"),
                                        rhs=Bmi4[P0:P0 + 64, gpl, 0:16].rearrange("p t h -> p (t h)"), start=False, stop=True, skip_group_check=True)
                    return last
                fw.op("pe", mm_toep, [BA0, BBm], [PB[pb]])
            for pb in range(NGR // 2):
                def mm_gm(e, pb=pb):
                    last = None
                    for k in range(2):
                        gl = 4 * (pb // 2) + (pb % 2) + 2 * k
                        gpl, P0 = gl // 2, 64 * (gl % 2)
                        for jb in range(2):
                            l_r = A0r4[P0:P0 + 64, gpl].rearrange("p j h -> p (j h)")
                            l_i = A0i4[P0:P0 + 64, gpl].rearrange("p j h -> p (j h)")
                            d_re = Dg5[P0:P0 + 64, gpl, jb, 0, :]
                            d_im = Dg5[P0:P0 + 64, gpl, jb, 1, :]
                            d_in = Dg5[P0:P0 + 64, gpl, jb, 2, :]
                            c0 = k * 256 + jb * 128
                            o_r = banks[4 + pb][:, c0:c0 + 64]
                            o_i = banks[4 + pb][:, c0 + 64:c0 + 128]
                            first = (jb == 0 and k == 0)
                            e.matmul(o_r, lhsT=l_r, rhs=d_re, start=first, stop=False, skip_group_check=True)
                            e.matmul(o_r, lhsT=l_i, rhs=d_in, start=False, stop=True, skip_group_check=True)
                            e.matmul(o_i, lhsT=l_r, rhs=d_im, start=False, stop=False, skip_group_check=True)
                            last = e.matmul(o_i, lhsT=l_i, rhs=d_re, start=False, stop=True, skip_group_check=True)
                    return last
                fw.op("pe", mm_gm, [BA0, BDg], [PB[4 + pb]])
            for pb in range(NGR // 2):
                ps3 = banks[pb][:, :].rearrange("p (g n) -> p g n", g=2)
                gt_ = (gtmp, gtmp2)[pb % 2]
                Bgt_ = (Bg, Bg2)[pb % 2]
                gt3 = gt_[:, 0:256].rearrange("p (g n) -> p g n", g=2)
                dve(lambda e, ps3=ps3, gt3=gt3: e.tensor_tensor(out=gt3, in0=ps3[:, :, 0:128],
                                                                 in1=C("tmask").unsqueeze(1).to_broadcast([128, 2, 128]), op=ALU.mult),
                    [PB[pb], Bcst], [Bgt_])
                gl0 = 4 * (pb // 2) + (pb % 2)
                for k in range(2):
                    gl = gl0 + 2 * k
                    g = 2 * gp0 + gl
                    dve(lambda e, gl=gl, g=g, k=k, gt3=gt3: e.scalar_tensor_tensor(out=TT3[:, gl, 0:128], in0=C("ident"),
                                                                                   scalar=C("dl")[:, g:g + 1], in1=gt3[:, k, :],
                                                                                   op0=ALU.mult, op1=ALU.add), [Bgt_, Bcst], [BTT])
                dve(lambda e, pb=pb, ps3=ps3: e.tensor_copy(out=TT3[:, gl0:gl0 + 3:2, 128:256], in_=ps3[:, :, 128:256]),
                    [PB[pb]], [BTT])
            for pb in range(NGR // 2):
                gl0 = 4 * (pb // 2) + (pb % 2)
                v = banks[4 + pb][:, :].rearrange("p (g j r n) -> p g j r n", g=2, j=2, r=2)
                act(lambda e, pb=pb, v=v: e.activation(out=GmR4[:, gl0:gl0 + 3:2], in_=v[:, :, :, 0, :], func=AF.Copy), [PB[4 + pb]], [BGm])
                act(lambda e, pb=pb, v=v: e.activation(out=GmI4[:, gl0:gl0 + 3:2], in_=v[:, :, :, 1, :], func=AF.Copy), [PB[4 + pb]], [BGm])

            def mm_G(e):
                last = None
                for ri, (bk, Gm4) in enumerate(((0, GmR4), (1, GmI4))):
                    for gpl in range(NG):
                        for g2 in range(2):
                            gl = 2 * gpl + g2
                            for jb in range(2):
                                last = e.matmul(banks[bk][64 * g2:64 * g2 + 64, gpl * 128:(gpl + 1) * 128],
                                                lhsT=Gm4[:, gl, jb, :], rhs=UT4[:, gl, jb, :],
                                                start=(jb == 0 and gpl == 0), stop=(jb == 1), skip_group_check=True)
                return last
            fw.op("pe", mm_G, [BGm, BUT], [PB[0], PB[1]])
            GRp, GIp = banks[0][:, :], banks[1][:, :]
            dve(lambda e: e.tensor_tensor(out=l2a, in0=GRp, in1=cosT, op=ALU.mult), [PB[0], Brot], [Bl2])
            dve(lambda e: e.tensor_tensor(out=l2b, in0=GIp, in1=sinT, op=ALU.mult), [PB[1], Brot], [Bl2])
            dve(lambda e: e.tensor_tensor(out=l2a, in0=l2a, in1=l2b, op=ALU.add), [Bl2], [Bl2])
            dve(lambda e: e.tensor_tensor(out=l2b, in0=GIp, in1=cosT, op=ALU.mult), [PB[1], Brot], [Bl2])
            dve(lambda e: e.tensor_tensor(out=l2c, in0=GRp, in1=sinT, op=ALU.mult), [PB[0], Brot], [Bl2])
            dve(lambda e: e.tensor_tensor(out=l2b, in0=l2b, in1=l2c, op=ALU.subtract), [Bl2], [Bl2])
            for gpl in range(NG):
                sl = slice(gpl * 128, (gpl + 1) * 128)
                rc = Rcol[:, gp0 + gpl:gp0 + gpl + 1].to_broadcast([128, 128])
                dve(lambda e, sl=sl, rc=rc: e.tensor_tensor_scan(out=l2c[:, sl], data0=rc, data1=l2a[:, sl], initial=0.0,
                                                                 op0=ALU.mult, op1=ALU.add), [Bl2, Bs], [Bl2])
                dve(lambda e, sl=sl, rc=rc: e.tensor_tensor_scan(out=l2d[:, sl], data0=rc, data1=l2b[:, sl], initial=0.0,
                                                                 op0=ALU.mult, op1=ALU.add), [Bl2, Bs], [Bl2])
            dve(lambda e: e.tensor_tensor(out=l2a, in0=l2c, in1=cosT, op=ALU.mult), [Bl2, Brot], [Bl2])
            dve(lambda e: e.tensor_tensor(out=l2b, in0=l2d, in1=sinT, op=ALU.mult), [Bl2, Brot], [Bl2])
            l2a3 = l2a.rearrange("p (g c) -> p g c", g=NG)
            l2b3 = l2b.rearrange("p (g c) -> p g c", g=NG)
            dve(lambda e: e.tensor_tensor(out=SR3[:, :, 1:128], in0=l2a3[:, :, 0:127], in1=l2b3[:, :, 0:127], op=ALU.subtract),
                [Bl2], [BS])
            dve(lambda e: e.tensor_tensor(out=l2a, in0=l2c, in1=sinT, op=ALU.mult), [Bl2, Brot], [Bl2])
            dve(lambda e: e.tensor_tensor(out=l2b, in0=l2d, in1=cosT, op=ALU.mult), [Bl2, Brot], [Bl2])
            dve(lambda e: e.tensor_tensor(out=SI3[:, :, 1:128], in0=l2a3[:, :, 0:127], in1=l2b3[:, :, 0:127], op=ALU.add),
                [Bl2], [BS])
            if qd + 1 < 16 // NG:
                emit_tables(qd + 1)
            for pr in range(NGR // 2):
                bk = 2 + pr % 2
                ps = banks[bk]

                def mm_y(e, pr=pr, ps=ps):
                    last = None
                    for k in range(2):
                        gl = 4 * (pr // 2) + (pr % 2) + 2 * k
                        gpl, g2 = gl // 2, gl % 2
                        P0 = 64 * g2
                        o = ps[:, k * 256:(k + 1) * 256]
                        e.matmul(o, lhsT=UT4[:, gl, 0, :], rhs=TT3[:, gl, 0:256], start=(k == 0), stop=False, skip_group_check=True)
                        e.matmul(o[:, 128:256], lhsT=UT4[:, gl, 1, :], rhs=TT3[:, gl, 0:128], start=False, stop=False, skip_group_check=True)
                        e.matmul(o, lhsT=SR3[P0:P0 + 64, gpl, :], rhs=Bmr4[P0:P0 + 64, gpl, 1:17].rearrange("p t h -> p (t h)"),
                                 start=False, stop=False, skip_group_check=True)
                        last = e.matmul(o, lhsT=SI3[P0:P0 + 64, gpl, :], rhs=Bmi4[P0:P0 + 64, gpl, 1:17].rearrange("p t h -> p (t h)"),
                                        start=False, stop=True, skip_group_check=True)
                    return last
                fw.op("pe", mm_y, [BUT, BTT, BS, BBm], [PB[bk]])
                gl0 = 4 * (pr // 2) + (pr % 2)
                dst = Yt4[:, :, gl0:gl0 + 3:2, :].rearrange("p t g h -> p g t h")
                act(lambda e, ps=ps, dst=dst: e.activation(out=dst, in_=ps[:, :].rearrange("p (g t h) -> p g t h", g=2, t=16),
                                                           func=AF.Gelu_apprx_tanh), [PB[bk]], [BYt])
            for half in range(2):
                bk = 6 + half
                psb = pbf(bk)

                def tr_y(e, half=half, psb=psb):
                    last = None
                    for i in range(8):
                        tau = half * 8 + i
                        last = e.transpose(psb[:, i * 128:(i + 1) * 128], Yt4[:, tau].rearrange("p g h -> p (g h)"), identb)
                    return last
                fw.op("pe", tr_y, [BYt, Bidb], [PB[bk]])
                dst = ygT[:, qd, :].rearrange("p (c t) -> p t c", t=16)[:, half * 8:half * 8 + 8, :]
                src = psb.rearrange("p (t c) -> p t c", t=8)
                act(lambda e, dst=dst, src=src: e.activation(out=dst, in_=src, func=AF.Copy), [PB[bk]], [Byg[qd]])

        hb, Bhb = A(4, F32)
        dve(lambda e: e.tensor_scalar(out=hb, in0=C("bglu"), scalar1=0.5, scalar2=None, op0=ALU.mult), [Bcst], [Bhb])
        gth, Bgth = A(512, BF16)
        for cb in range(4):
            for tg in range(4):
                bk = (cb * 4 + tg) % 2
                ps = banks[bk]

                def mm_glu(e, cb=cb, tg=tg, ps=ps):
                    last = None
                    for kb in range(4):
                        last = e.matmul(ps[:, :], lhsT=wglu[:, kb, cb * 128:(cb + 1) * 128], rhs=ygT[:, kb, tg * 512:(tg + 1) * 512],
                                        start=(kb == 0), stop=(kb == 3))
                    return last
                fw.op("pe", mm_glu, [Bwglu] + Byg, [PB[bk]])
                act(lambda e, cb=cb, ps=ps: e.activation(out=gth, in_=ps[:, :], func=AF.Sigmoid, bias=C("bglu")[:, cb:cb + 1]),
                    [PB[bk], Bcst], [Bgth])
                dve(lambda e, cb=cb, tg=tg: e.tensor_tensor(out=mixT[:, 4 + cb, tg * 512:(tg + 1) * 512], in0=gth,
                                                            in1=ygT[:, cb, tg * 512:(tg + 1) * 512], op=ALU.mult),
                    [Bgth, Byg[cb]], [Bmix[4 + cb][tg]])

        fw.barrier()
        ar.reset(PM_A)
        AT = lambda cols, dt: (ar.alloc_top(cols, dt), Buf())
        lnc, Blnc = AT(4 * D, F32)
        WoG, BWoG = AT(8 * D, BF16)
        WoG = WoG.rearrange("p (k n) -> p k n", k=8)
        Wple, BWple = AT(2 * D, BF16)
        Wple = Wple.rearrange("p (k n) -> p k n", k=2)
        NWUP = 3
        NCH = NPAIR // 2
        wup = []
        for i in range(NWUP):
            t, b = AT(8 * 512, BF16)
            wup.append((t.rearrange("p (k n) -> p k n", k=8), b))

        def emit_wdma(cidx):
            wt_, Bw_ = wup[cidx % NWUP]
            sn = f"wup{cidx % NWUP}"
            fw.dma("pool", sn, wt_[:, :, 0:256], wup_d[:, 256 * cidx:256 * cidx + 256].rearrange("(k p) n -> p k n", p=128), writes=[Bw_])
            fw.dma("pool", sn, wt_[:, :, 256:512],
                   wup_d[:, DFF + 256 * cidx:DFF + 256 * cidx + 256].rearrange("(k p) n -> p k n", p=128), writes=[Bw_])
        fw.dma("sp", "lnc", lnc, lnc_d, writes=[Blnc])
        lnv = lnc.rearrange("p (v n) -> p v n", v=4)
        dve(lambda e: e.tensor_scalar(out=lnv[:, 0:2, :], in0=lnv[:, 0:2, :], scalar1=ALPHA, scalar2=None, op0=ALU.mult), [Blnc], [Blnc])
        fw.dma("pool", "wog", WoG, wo_d.rearrange("(k p) n -> p k n", p=128), writes=[BWoG])
        emit_wdma(0)
        emit_wdma(1)
        fw.dma("pool", "wple", Wple, wple_d.rearrange("(k p) n -> p k n", p=128), writes=[BWple])
        if debug == "ssm":
            dbt, Bdb = A(L, F32)
            for k in range(4):
                dve(lambda e, k=k: e.tensor_copy(out=dbt, in_=mixT[:, 4 + k, :]), Bmix[4 + k], [Bdb])
                fw.dma("sp", "out", dbg_d[k * 128:(k + 1) * 128, :], dbt, reads=[Bdb])
            fw.barrier()
            return nc
        qT = [[A(L, BF16)[0] for _c2 in range(2)] for _ in range(2)]
        kT = [A(L, BF16)[0] for _ in range(2)]
        Vh = [A(16 * 130, BF16)[0].rearrange("p (b e) -> p b e", b=16) for _ in range(2)]
        Bq = [Buf() for _ in range(2)]; Bk = [Buf() for _ in range(2)]; Bv = [Buf() for _ in range(2)]
        NPT = 4
        Pt = [A(512, BF16)[0] for _ in range(NPT)]
        BPt = [Buf() for _ in range(NPT)]
        O1, BO1 = A(512, F32)
        otmp, Bot = A(512, F32)
        osq, _ = A(512, F32)
        oN, BoN = A(512, BF16)
        O1v = O1.rearrange("p (j e) -> p j e", j=4)
        otv = otmp.rearrange("p (j e) -> p j e", j=4)
        osv = osq.rearrange("p (j e) -> p j e", j=4)
        oNv = oN.rearrange("p (j e) -> p j e", j=4)
        lt, Blt = A(64, F32)
        lcol, Blc = A(8, F32)
        rz, Brz = A(4, F32)
        ssum, _ = A(4, F32)
        mhalf, Bmh = A(4, F32)
        for i in range(2):
            dve(lambda e, i=i: e.memset(Vh[i][:, :, 128:130], 1.0), [], [Bv[i]])
            for c2 in range(2):
                dve(lambda e, i=i, c2=c2: e.memset(qT[i][c2], 0.0), [], [Bq[i]])
        dve(lambda e: e.memset(mhalf, -0.5), [], [Bmh])
        lq = C("lamq")
        for i in range(2):
            dve(lambda e, i=i: e.tensor_tensor(out=lt, in0=lq[:, 128 * i:128 * i + 64], in1=lq[:, 128 * i + 64:128 * i + 128], op=ALU.mult),
                [Bcst], [Blt])
            dve(lambda e, i=i: e.tensor_reduce(out=lcol[:, i:i + 1], in_=lt, axis=mybir.AxisListType.X, op=ALU.add), [Blt], [Blc])
        act(lambda e: e.activation(out=lcol[:, 0:2], in_=lcol[:, 0:2], func=AF.Exp), [Blc], [Blc])
        dve(lambda e: e.tensor_tensor(out=lcol[:, 2:3], in0=lcol[:, 1:2], in1=lcol[:, 0:1], op=ALU.subtract), [Blc], [Blc])
        dve(lambda e: e.tensor_scalar(out=lcol[:, 2:3], in0=lcol[:, 2:3], scalar1=-LAM_INIT, scalar2=None, op0=ALU.add), [Blc], [Blc])
        nlam = lcol[:, 2:3]
        dve(lambda e: e.tensor_scalar(out=lcol[:, 3:4], in0=C("gcol"), scalar1=1.0 - LAM_INIT, scalar2=None, op0=ALU.mult), [Bcst, Blc], [Blc])
        gsc = lcol[:, 3:4]

        pending = []
        SC = 1.0 / 8.0

        def proj_groups(h):
            sl = h % 2
            wt, Bw = head_w[h]
            fns = []
            for which, Bd in enumerate((Bq[sl], Bk[sl])):
                for tg in range(4):
                    def grp(bk, which=which, tg=tg, Bd=Bd, wt=wt, Bw=Bw, sl=sl):
                        ps = banks[bk]

                        def mm_p(e):
                            last = None
                            for kb in range(8):
                                last = e.matmul(ps[:, :], lhsT=wt[:, kb, which * 128:(which + 1) * 128], rhs=xT[:, kb, tg * 512:(tg + 1) * 512],
                                                start=(kb == 0), stop=(kb == 7))
                            return last
                        fw.op("pe", mm_p, [Bw, BxT], [PB[bk]])
                        cs = slice(tg * 512, (tg + 1) * 512)
                        if which == 0:
                            act(lambda e: e.activation(out=qT[sl][0][0:64, cs], in_=ps[0:64, :], func=AF.Copy), [PB[bk]], [Bd])
                            dve(lambda e: e.tensor_copy(out=qT[sl][1][64:128, cs], in_=ps[64:128, :]), [PB[bk]], [Bd])
                        else:
                            dve(lambda e: e.tensor_copy(out=kT[sl][:, cs], in_=ps[:, :]), [PB[bk]], [Bd])
                    fns.append(grp)
            for tb4 in range(4):
                def grpv(bk, tb4=tb4, wt=wt, Bw=Bw, sl=sl):
                    ps = banks[bk]

                    def mm_v(e):
                        last = None
                        for ti in range(4):
                            tkb = tb4 * 4 + ti
                            for kb in range(8):
                                last = e.matmul(ps[:, ti * 128:(ti + 1) * 128], lhsT=xT[:, kb, tkb * 128:(tkb + 1) * 128], rhs=wt[:, kb, 256:384],
                                                start=(kb == 0 and ti == 0), stop=(kb == 7), skip_group_check=True)
                        return last
                    fw.op("pe", mm_v, [Bw, BxT], [PB[bk]])
                    src = ps[:, :].rearrange("p (b e) -> p b e", b=4)
                    dst = Vh[sl][:, tb4 * 4:tb4 * 4 + 4, 0:128]
                    dve(lambda e: e.tensor_copy(out=dst, in_=src), [PB[bk]], [Bv[sl]])
                fns.append(grpv)
            return fns

        def finalize(h, G, c, aset, accv):
            nonlocal_pending = pending
            for a in range(2):
                acc = accv[a]
                jsl = slice(2 * a, 2 * a + 2)
                PBa = PB[aset[a]]
                dve(lambda e, acc=acc, a=a: e.reciprocal(out=rz[:, 2 * a:2 * a + 2], in_=acc[:, :, 128]), [PBa], [Brz])
                if c == 0:
                    dve(lambda e, acc=acc, jsl=jsl, a=a: e.tensor_tensor(
                        out=O1v[:, jsl, :], in0=acc[:, :, 0:128],
                        in1=rz[:, 2 * a:2 * a + 2].unsqueeze(2).to_broadcast([128, 2, 128]), op=ALU.mult), [PBa, Brz], [BO1])
                else:
                    dve(lambda e, a=a: e.tensor_scalar(out=rz[:, 2 * a:2 * a + 2], in0=rz[:, 2 * a:2 * a + 2], scalar1=nlam, scalar2=None,
                                                       op0=ALU.mult), [Brz, Blc], [Brz])
                    dve(lambda e, acc=acc, jsl=jsl, a=a: e.tensor_tensor(
                        out=otv[:, jsl, :], in0=acc[:, :, 0:128],
                        in1=rz[:, 2 * a:2 * a + 2].unsqueeze(2).to_broadcast([128, 2, 128]), op=ALU.mult), [PBa, Brz], [Bot])
            if c == 1:
                dve(lambda e: e.tensor_tensor(out=otmp, in0=otmp, in1=O1, op=ALU.add), [Bot, BO1], [Bot])
                dve(lambda e: e.tensor_tensor(out=osq, in0=otmp, in1=otmp, op=ALU.mult), [Bot], [Bot])
                dve(lambda e: e.tensor_reduce(out=ssum, in_=osv, axis=mybir.AxisListType.X, op=ALU.add), [Bot], [Bot])
                dve(lambda e: e.tensor_scalar(out=ssum, in0=ssum, scalar1=1.0 / 128.0, scalar2=LN_EPS, op0=ALU.mult, op1=ALU.add),
                    [Bot], [Bot])
                fw.op("pool", lambda e: e.tensor_tensor(out=ssum, in0=ssum, in1=mhalf, op=ALU.pow), [Bot, Bmh], [Bot])
                dve(lambda e: e.tensor_tensor(out=oNv, in0=otv, in1=ssum.unsqueeze(2).to_broadcast([128, 4, 128]), op=ALU.mult),
                    [Bot], [BoN])

                def do_tr(h=h, G=G):
                    psb = pbf(0)

                    def tr_o(e):
                        last = None
                        for jq in range(4):
                            last = e.transpose(psb[:, jq * 128:(jq + 1) * 128], oNv[:, jq, :], identb)
                        return last
                    fw.op("pe", tr_o, [BoN, Bidb], [PB[0]])
                    act(lambda e: e.activation(out=mixT[:, h, G * 512:(G + 1) * 512], in_=psb[:, 0:512], func=AF.Identity, scale=gsc),
                        [PB[0], Blc], [Bmix[h][G]])
                nonlocal_pending.append(do_tr)

        for gi, g_ in enumerate(proj_groups(0)):
            g_(gi % 4)
        head_w[1] = load_w(0, head_cols(1))
        SB = (1, 2, 3)
        for h in range(4):
            sl = h % 2
            tasks = []
            for G in range(4):
                for c in range(2):
                    for b in range(4 * G + 4):
                        tasks.append((2 * G + c, G, c, b))
            started = {}
            nxt = []
            n_half = None

            def emit_S(n, sl=sl, tasks=tasks):
                ep, G, c, b = tasks[n]
                i = b - 4 * G
                n0 = max(0, i) * 128
                bk = SB[n % 3]
                ps = banks[bk]

                def mm_s(e):
                    last = e.matmul(ps[:, n0:512], lhsT=kT[sl][:, b * 128:(b + 1) * 128],
                                    rhs=qT[sl][c][:, G * 512 + n0:(G + 1) * 512], start=True, stop=(i < 0))
                    if i >= 0:
                        last = e.matmul(ps[:, n0:n0 + 128], lhsT=identb, rhs=maskb, start=False, stop=True)
                    return last
                fw.op("pe", mm_s, [Bq[sl], Bk[sl], Bidb, Bmkb], [PB[bk]])
            for n_ in range(min(2, len(tasks))):
                emit_S(n_)
            for n, (ep, G, c, b) in enumerate(tasks):
                if n + 2 < len(tasks):
                    emit_S(n + 2)
                aset = (4, 5) if ep % 2 == 0 else (6, 7)
                accv = [banks[a][:, 0:258].rearrange("p (j e) -> p j e", j=2) for a in aset]
                st_ = started.setdefault(ep, [False, False])
                i = b - 4 * G
                n0 = max(0, i) * 128
                bk = SB[n % 3]
                pt = Pt[n % NPT]
                act(lambda e, pt=pt, bk=bk, n0=n0: e.activation(out=pt[:, n0:512], in_=banks[bk][:, n0:512], func=AF.Exp, scale=SC),
                    [PB[bk]], [BPt[n % NPT]])

                def mm_pv(e, b=b, i=i, G=G, pt=pt, sl=sl, accv=accv, st_=st_):
                    last = None
                    for jq in range(max(0, i), 4):
                        a = jq // 2
                        last = e.matmul(accv[a][:, jq % 2, 0:129], lhsT=pt[:, jq * 128:(jq + 1) * 128], rhs=Vh[sl][:, b, 0:129],
                                        start=(not st_[a]), stop=(b == 4 * G + jq), skip_group_check=True)
                        st_[a] = True
                    return last
                fw.op("pe", mm_pv, [BPt[n % NPT], Bv[sl]], [PB[aset[0]], PB[aset[1]]], acc=True)
                if ep >= 4 and h + 1 < 4:
                    if n_half is None:
                        n_half = n
                        nxt = proj_groups(h + 1)
                    if (n - n_half) % 4 == 0 and nxt:
                        nxt.pop(0)(0)
                        if not nxt and h + 2 < 4:
                            head_w[h + 2] = load_w((h + 1) % 2, head_cols(h + 2))
                if b == 4 * G + 3:
                    for fn in pending:
                        fn()
                    del pending[:]
                    finalize(h, G, c, aset, accv)
            while nxt:
                nxt.pop(0)(0)
                if not nxt and h + 2 < 4:
                    head_w[h + 2] = load_w((h + 1) % 2, head_cols(h + 2))
        for fn in pending:
            fn()
        del pending[:]

        if debug == "attn":
            fw.barrier()
            ar.reset(PM_A)
            dbt, Bdb = A(L, F32)
            for k in range(4):
                dve(lambda e, k=k: e.tensor_copy(out=dbt, in_=mixT[:, k, :]), Bmix[k], [Bdb])
                fw.dma("sp", "out", dbg_d[k * 128:(k + 1) * 128, :], dbt, reads=[Bdb])
            fw.barrier()
            return nc

        fw.barrier()
        ar.reset(PM)
        Wdn, BWdn = A(NPAIR * D, BF16)
        Wdn = Wdn.rearrange("p (k n) -> p k n", k=NPAIR)
        actT, _ = A(NPAIR * 512, BF16)
        actT = actT.rearrange("p (k t) -> p k t", k=NPAIR)
        Bact = Buf()
        x1a, _ = A(4 * D, F32)
        x1a = x1a.rearrange("p (b n) -> p b n", b=4)
        Bx1a = [Buf() for _ in range(4)]
        x1T, Bx1T = A(8 * 512, BF16)
        x1T = x1T.rearrange("p (k t) -> p k t", k=8)
        pTt, BpT = A(2 * 512, BF16)
        pTt = pTt.rearrange("p (k t) -> p k t", k=2)
        dgs = []
        for i in range(2):
            t, b = A(6 * 128, BF16)
            dgs.append((t.rearrange("p (j n) -> p j n", j=6), b))
        hss = []
        for i in range(2):
            tg_, bg_ = A(514, BF16)
            tv_, bv_ = A(514, BF16)
            hss.append((tg_, bg_, tv_, bv_))
        tails, Btl = A(44 * 2, BF16)
        tails = tails.rearrange("p (f c) -> p f c", f=44)
        sgt, Bsg = A(512, F32)
        W1, BW1 = A(D, F32)
        W2, BW2 = A(D, F32)
        nbf, Bnbf = A(D, BF16)
        nbf2, Bnbf2 = A(D, BF16)
        nbfs = [nbf, nbf2]
        Bnbfs = [Bnbf, Bnbf2]
        stt, Bst = A(2 * 6, F32)
        mv, _ = A(8, F32)
        mhalf2, Bmh2 = A(1, F32)
        pool = lambda fn, r, w: fw.op("pool", fn, reads=r, writes=w)
        lnv = lnc.rearrange("p (v n) -> p v n", v=4)
        dve(lambda e: e.memset(tails, 0.0), [], [Btl])
        dve(lambda e: e.memset(mhalf2, -0.5), [], [Bmh2])

        def layer_norm_stats(src, Bsrc):
            for hh in range(2):
                dve(lambda e, hh=hh: e.bn_stats(out=stt[:, hh * 6:(hh + 1) * 6], in_=src[:, hh * 512:(hh + 1) * 512]), [Bsrc], [Bst])
            dve(lambda e: e.bn_aggr(out=mv[:, 0:2], in_=stt), [Bst], [Bst])
            dve(lambda e: e.tensor_scalar(out=mv[:, 2:3], in0=mv[:, 1:2], scalar1=LN_EPS, scalar2=None, op0=ALU.add), [Bst], [Bst])
            pool(lambda e: e.tensor_tensor(out=mv[:, 2:3], in0=mv[:, 2:3], in1=mhalf2, op=ALU.pow), [Bst, Bmh2], [Bst])
            dve(lambda e: e.scalar_tensor_tensor(out=mv[:, 3:4], in0=mv[:, 0:1], scalar=-1.0, in1=mv[:, 2:3], op0=ALU.mult, op1=ALU.mult),
                [Bst], [Bst])

        stt2, Bst2 = A(2 * 6, F32)
        mv2, _ = A(8, F32)
        sttA = [A(12, F32)[0] for _ in range(4)]; mvA = [A(8, F32)[0] for _ in range(4)]; BstA = [Buf() for _ in range(4)]
        sttB = [A(12, F32)[0] for _ in range(4)]; mvB = [A(8, F32)[0] for _ in range(4)]; BstB = [Buf() for _ in range(4)]

        def ln_stats(src, Bsrc, stt_, mv_, Bs_):
            for hh in range(2):
                dve(lambda e, hh=hh: e.bn_stats(out=stt_[:, hh * 6:(hh + 1) * 6], in_=src[:, hh * 512:(hh + 1) * 512]), [Bsrc], [Bs_])
            dve(lambda e: e.bn_aggr(out=mv_[:, 0:2], in_=stt_), [Bs_], [Bs_])
            dve(lambda e: e.tensor_scalar(out=mv_[:, 2:3], in0=mv_[:, 1:2], scalar1=LN_EPS, scalar2=None, op0=ALU.add), [Bs_], [Bs_])
            pool(lambda e: e.tensor_tensor(out=mv_[:, 2:3], in0=mv_[:, 2:3], in1=mhalf2, op=ALU.pow), [Bs_, Bmh2], [Bs_])
            dve(lambda e: e.scalar_tensor_tensor(out=mv_[:, 3:4], in0=mv_[:, 0:1], scalar=-1.0, in1=mv_[:, 2:3], op0=ALU.mult, op1=ALU.mult),
                [Bs_], [Bs_])

        mixbank = {}

        def emit_mix(tg, tb, mb=6):
            T = 4 * tg + tb
            mixbank[(tg, tb)] = mb

            def mm_mix(e):
                last = None
                for hf in range(2):
                    for k in range(8):
                        last = e.matmul(banks[mb + hf][:, :], lhsT=mixT[:, k, T * 128:(T + 1) * 128], rhs=WoG[:, k, hf * 512:(hf + 1) * 512],
                                        start=(k == 0), stop=(k == 7))
                return last
            fw.op("pe", mm_mix, [BWoG] + [Bmix[k][tg] for k in range(8)], [PB[mb], PB[mb + 1]])

        def emit_xload(tg, tb):
            T = 4 * tg + tb
            fw.dma("sp", f"x{tb}", x1a[:, tb, :], x_d[T * 128:(T + 1) * 128, :], writes=[Bx1a[tb]])

        def ln1_s1(tg, tb):
            xa = x1a[:, tb, :]
            Bxa = Bx1a[tb]
            mb = mixbank[(tg, tb)]
            for hf in range(2):
                dve(lambda e, hf=hf: e.scalar_tensor_tensor(out=xa[:, hf * 512:(hf + 1) * 512], in0=xa[:, hf * 512:(hf + 1) * 512],
                                                            scalar=ALPHA, in1=banks[mb + hf][:, :], op0=ALU.mult, op1=ALU.add),
                    [Bxa, PB[mb + hf]], [Bxa])
            ln_stats(xa, Bxa, sttA[tb], mvA[tb], BstA[tb])

        def ln1_s2(tg, tb):
            xa = x1a[:, tb, :]
            Bxa = Bx1a[tb]
            mv_ = mvA[tb]
            act(lambda e: e.activation(out=nbfs[tb % 2], in_=xa, func=AF.Identity, scale=mv_[:, 2:3], bias=mv_[:, 3:4]), [Bxa, BstA[tb]], [Bnbfs[tb % 2]])
            act(lambda e: e.activation(out=xa, in_=xa, func=AF.Identity, scale=mv_[:, 2:3], bias=mv_[:, 3:4]), [Bxa, BstA[tb]], [Bxa])

        def ln1_s3(tg, tb):
            xa = x1a[:, tb, :]
            Bxa = Bx1a[tb]
            dve(lambda e: e.tensor_tensor(out=xa, in0=xa, in1=lnv[:, 0, :], op=ALU.mult), [Bxa, Blnc], [Bxa])
            dve(lambda e: e.tensor_tensor(out=xa, in0=xa, in1=lnv[:, 1, :], op=ALU.add), [Bxa, Blnc], [Bxa])

        def emit_head(tg):
            ln1_s1(tg, 0)
            emit_mix(tg, 1)
            ln1_s1(tg, 1)
            ln1_s2(tg, 0)
            emit_mix(tg, 2)
            ln1_s1(tg, 2)
            ln1_s2(tg, 1)
            ln1_s3(tg, 0)
            emit_ln1_tr(tg, 0)
            emit_mix(tg, 3)
            ln1_s1(tg, 3)
            ln1_s2(tg, 2)
            ln1_s3(tg, 1)
            emit_ln1_tr(tg, 1)
            ln1_s2(tg, 3)
            ln1_s3(tg, 2)
            emit_ln1_tr(tg, 2)
            ln1_s3(tg, 3)
            emit_ln1_tr(tg, 3)

        def emit_ln1_tr(tg, tb):
            bk = 4 + tb % 2
            psb = pbf(bk)

            def tr_n(e):
                last = None
                for k in range(8):
                    last = e.transpose(psb[:, k * 128:(k + 1) * 128], nbfs[tb % 2][:, k * 128:(k + 1) * 128], identb)
                return last
            fw.op("pe", tr_n, [Bnbfs[tb % 2], Bidb], [PB[bk]])
            for k in range(8):
                act(lambda e, k=k: e.activation(out=x1T[:, k, tb * 128:(tb + 1) * 128], in_=psb[:, k * 128:(k + 1) * 128],
                                                func=AF.Identity, scale=CB("g1c")[:, k:k + 1], bias=CB("b1c")[:, k:k + 1]),
                    [PB[bk], BcstB], [Bx1T])

        def emit_down(tg, tb):
            db = 2 * (tb % 2)

            def mm_dn(e):
                last = None
                for hf in range(2):
                    for k in range(NPAIR):
                        last = e.matmul(banks[db + hf][:, :], lhsT=actT[:, k, tb * 128:(tb + 1) * 128], rhs=Wdn[:, k, hf * 512:(hf + 1) * 512],
                                        start=(k == 0), stop=(k == NPAIR - 1))
                return last
            fw.op("pe", mm_dn, [Bact, BWdn], [PB[db], PB[db + 1]])

        Wout = [W1, W2]
        BWout = [BW1, BW2]

        def tail_A(tg, tb, thS, BthS):
            xa = x1a[:, tb, :]
            Bxa = Bx1a[tb]
            db = 2 * (tb % 2)
            def mm_pp(e):
                last = None
                for hf in range(2):
                    for k in range(2):
                        last = e.matmul(banks[4 + hf][:, :], lhsT=pTt[:, k, tb * 128:(tb + 1) * 128], rhs=Wple[:, k, hf * 512:(hf + 1) * 512],
                                        start=(k == 0), stop=(k == 1))
                return last
            fw.op("pe", mm_pp, [BpT, BWple], [PB[4], PB[5]])
            for hf in range(2):
                hs_ = slice(hf * 512, (hf + 1) * 512)
                dve(lambda e, hf=hf: e.scalar_tensor_tensor(out=sgt, in0=thS[tb][:, hf, :], scalar=1.0, in1=banks[4 + hf][:, :],
                                                            op0=ALU.add, op1=ALU.mult), [BthS[tb][hf], PB[4 + hf]], [Bsg])
                dve(lambda e, hs_=hs_: e.scalar_tensor_tensor(out=xa[:, hs_], in0=sgt, scalar=0.5, in1=xa[:, hs_], op0=ALU.mult, op1=ALU.add),
                    [Bsg, Bxa], [Bxa])
            for hf in range(2):
                hs_ = slice(hf * 512, (hf + 1) * 512)
                dve(lambda e, hf=hf, hs_=hs_: e.tensor_tensor(out=xa[:, hs_], in0=xa[:, hs_], in1=banks[db + hf][:, :], op=ALU.add),
                    [Bxa, PB[db + hf]], [Bxa])
            ln_stats(xa, Bxa, sttB[tb], mvB[tb], BstB[tb])

        def tail_B(tg, tb, nx):
            xa = x1a[:, tb, :]
            mv_ = mvB[tb]
            wo_, Bwo_ = Wout[tb % 2], BWout[tb % 2]
            act(lambda e: e.activation(out=wo_, in_=xa, func=AF.Identity, scale=mv_[:, 2:3], bias=mv_[:, 3:4]), [Bx1a[tb], BstB[tb]], [Bwo_])
            if nx:
                emit_xload(tg + 1, tb)

        def tail_C(tg, tb):
            T = 4 * tg + tb
            wo_, Bwo_ = Wout[tb % 2], BWout[tb % 2]
            dve(lambda e: e.tensor_tensor(out=wo_, in0=wo_, in1=lnv[:, 2, :], op=ALU.mult), [Bwo_, Blnc], [Bwo_])
            dve(lambda e: e.tensor_tensor(out=wo_, in0=wo_, in1=lnv[:, 3, :], op=ALU.add), [Bwo_, Blnc], [Bwo_])
            fw.dma("sp", f"out{tb % 2}", out_d[T * 128:(T + 1) * 128, :], wo_, reads=[Bwo_])

        def emit_tail_head(tg, thS, BthS):
            g1 = tg + 1
            emit_down(tg, 0)
            emit_down(tg, 1)
            tail_A(tg, 0, thS, BthS)
            emit_down(tg, 2)
            tail_A(tg, 1, thS, BthS)
            tail_B(tg, 0, True)
            tail_C(tg, 0)
            emit_down(tg, 3)
            tail_A(tg, 2, thS, BthS)
            tail_B(tg, 1, True)
            tail_C(tg, 1)
            ln1_s1(g1, 0)
            emit_mix(g1, 1)
            tail_A(tg, 3, thS, BthS)
            tail_B(tg, 2, True)
            tail_C(tg, 2)
            ln1_s1(g1, 1)
            ln1_s2(g1, 0)
            emit_mix(g1, 2, 0)
            tail_B(tg, 3, True)
            tail_C(tg, 3)
            ln1_s1(g1, 2)
            ln1_s2(g1, 1)
            ln1_s3(g1, 0)
            emit_ln1_tr(g1, 0)
            emit_mix(g1, 3, 2)
            ln1_s1(g1, 3)
            ln1_s2(g1, 2)
            ln1_s3(g1, 1)
            emit_ln1_tr(g1, 1)
            ln1_s2(g1, 3)
            ln1_s3(g1, 2)
            emit_ln1_tr(g1, 2)
            ln1_s3(g1, 3)
            emit_ln1_tr(g1, 3)

        def emit_tail(tg, thS, BthS, nx):
            emit_down(tg, 0)
            emit_down(tg, 1)
            tail_A(tg, 0, thS, BthS)
            emit_down(tg, 2)
            tail_A(tg, 1, thS, BthS)
            tail_B(tg, 0, nx)
            tail_C(tg, 0)
            emit_down(tg, 3)
            tail_A(tg, 2, thS, BthS)
            tail_B(tg, 1, nx)
            tail_C(tg, 1)
            tail_A(tg, 3, thS, BthS)
            tail_B(tg, 2, nx)
            tail_C(tg, 2)
            tail_B(tg, 3, nx)
            tail_C(tg, 3)

        for tb in range(4):
            emit_xload(0, tb)
        emit_mix(0, 0)
        emit_head(0)

        for tg in range(4):
            t0 = tg * 512
            fw.dma("pool", "pT", pTt, pT_d[:, t0:t0 + 512].rearrange("(k p) t -> p k t", p=128), writes=[BpT])
            thS = [mixT[:, 2 * tb:2 * tb + 2, t0:t0 + 512] for tb in range(4)]
            BthS = [[Bmix[2 * tb][tg], Bmix[2 * tb + 1][tg]] for tb in range(4)]
            fw.dma("pool", "wog", WoG, wg_d.rearrange("(k p) n -> p k n", p=128), writes=[BWoG])
            if tg == 0:
                for hf in range(2):
                    fw.dma("pool", "wdn", Wdn[:, hf * 11:(hf + 1) * 11, :],
                           wdn_d[hf * 1408:(hf + 1) * 1408, :].rearrange("(k p) n -> p k n", p=128), writes=[BWdn])

            def emit_gate():
                for tb in range(4):
                    def mm_gate(e, tb=tb):
                        last = None
                        for hf in range(2):
                            for k in range(8):
                                last = e.matmul(banks[6 + hf][:, :], lhsT=x1T[:, k, tb * 128:(tb + 1) * 128], rhs=WoG[:, k, hf * 512:(hf + 1) * 512],
                                                start=(k == 0), stop=(k == 7))
                        return last
                    fw.op("pe", mm_gate, [Bx1T, BWoG], [PB[6], PB[7]])
                    for hf in range(2):
                        act(lambda e, hf=hf, tb=tb: e.activation(out=thS[tb][:, hf, :], in_=banks[6 + hf][:, :], func=AF.Tanh, scale=0.5),
                            [PB[6 + hf]], [BthS[tb][hf]])

            def emit_up(i):
                cidx, s2 = i // 2, i % 2
                wt_, Bw_ = wup[cidx % NWUP]
                dg_, Bdg_ = dgs[i % 2]
                for j in range(3):
                    for gv in range(2):
                        fb = i + 22 * gv
                        dve(lambda e, j=j, gv=gv, fb=fb, dg_=dg_: e.tensor_scalar(out=dg_[:, gv * 3 + j, :], in0=identb,
                                                                                  scalar1=CB("cw")[:, fb * 3 + j:fb * 3 + j + 1], scalar2=None,
                                                                                  op0=ALU.mult), [Bidb, BcstB], [Bdg_])
                for gv in range(2):
                    bk = 2 * (i % 2) + gv

                    def mm_up(e, gv=gv, bk=bk, wt_=wt_, s2=s2):
                        last = None
                        c0 = gv * 256 + s2 * 128
                        for k in range(8):
                            last = e.matmul(banks[bk][:, :], lhsT=wt_[:, k, c0:c0 + 128], rhs=x1T[:, k, :], start=(k == 0), stop=(k == 7))
                        return last
                    fw.op("pe", mm_up, [Bw_, Bx1T], [PB[bk]])

            def emit_conv(i):
                hg, Bhg, hv, Bhv = hss[i % 2]
                dg_, Bdg_ = dgs[i % 2]
                for gv, (ht, Bh) in enumerate(((hg, Bhg), (hv, Bhv))):
                    fb = i + 22 * gv
                    bk = 2 * (i % 2) + gv
                    act(lambda e, ht=ht, fb=fb: e.activation(out=ht[:, 0:2], in_=tails[:, fb, :], func=AF.Copy), [Btl], [Bh])
                    if gv == 0:
                        act(lambda e, ht=ht, bk=bk: e.activation(out=ht[:, 2:514], in_=banks[bk][:, :], func=AF.Copy), [PB[bk]], [Bh])
                    else:
                        dve(lambda e, ht=ht, bk=bk: e.tensor_copy(out=ht[:, 2:514], in_=banks[bk][:, :]), [PB[bk]], [Bh])
                    dve(lambda e, ht=ht, fb=fb: e.tensor_copy(out=tails[:, fb, :], in_=ht[:, 512:514]), [Bh], [Btl])

                    def mm_cv(e, gv=gv, ht=ht, dg_=dg_):
                        last = None
                        for j in range(3):
                            last = e.matmul(banks[4 + gv][:, :], lhsT=dg_[:, gv * 3 + j, :], rhs=ht[:, j:j + 512], start=(j == 0), stop=(j == 2))
                        return last
                    fw.op("pe", mm_cv, [Bh, Bdg_], [PB[4 + gv]])
                act(lambda e, i=i: e.activation(out=sgt, in_=banks[4][:, :], func=AF.Silu, bias=CB("cb")[:, i:i + 1]), [PB[4], BcstB], [Bsg])
                dve(lambda e, i=i: e.scalar_tensor_tensor(out=actT[:, i, :], in0=banks[5][:, :], scalar=CB("cb")[:, 22 + i:23 + i], in1=sgt,
                                                          op0=ALU.add, op1=ALU.mult), [PB[5], Bsg, BcstB], [Bact])
            emit_up(0)
            for i in range(NPAIR):
                if i % 2 == 0 and i // 2 + 2 < NCH:
                    emit_wdma(i // 2 + 2)
                if i == 8 and tg + 1 < 4:
                    fw.dma("pool", "wog", WoG, wo_d.rearrange("(k p) n -> p k n", p=128), writes=[BWoG])
                if i + 1 < NPAIR:
                    emit_up(i + 1)
                emit_conv(i)
                if i == 5:
                    emit_gate()
            nx = tg + 1 < 4
            if nx:
                emit_wdma(0)
                emit_wdma(1)
                emit_mix(tg + 1, 0)
            if nx:
                emit_tail_head(tg, thS, BthS)
            else:
                emit_tail(tg, thS, BthS, False)
        fw.barrier()
    return nc


def _prep_shared(inp):
    f = lambda a: np.ascontiguousarray(np.asarray(a, dtype=np.float32))
    cst = np.zeros((128, NCST), np.float32)

    def put(n, a):
        a = np.asarray(a, np.float32).reshape(128, -1)
        assert a.shape[1] == _c[n][1] - _c[n][0], (n, a.shape)
        cst[:, _c[n][0]:_c[n][1]] = a
    lamq = np.stack([inp["diff_lambda_q1"][0], inp["diff_lambda_k1"][0], inp["diff_lambda_q2"][0], inp["diff_lambda_k2"][0]], 0)
    put("lamq", np.broadcast_to(lamq.reshape(1, 256), (128, 256)))
    put("gcol", inp["diff_subln_g"][0].reshape(128, 1))
    gl = lambda a: np.asarray(a).reshape(16, 2, *a.shape[1:])
    put("lr", gl(inp["ssm_lambda_re"][0]).transpose(1, 2, 0).reshape(128, 16))
    put("li", gl(inp["ssm_lambda_im"][0]).transpose(1, 2, 0).reshape(128, 16))
    ldt = np.broadcast_to(np.asarray(inp["ssm_log_dt"][0]).reshape(16, 2, 1), (16, 2, 64))
    put("ldt", ldt.transpose(1, 2, 0).reshape(128, 16))
    put("bre", gl(inp["ssm_b_re"][0]).transpose(1, 2, 0, 3).reshape(128, 256))
    put("bim", gl(inp["ssm_b_im"][0]).transpose(1, 2, 0, 3).reshape(128, 256))
    put("cre", gl(inp["ssm_c_re"][0]).transpose(1, 3, 0, 2).reshape(128, 256))
    put("cim", gl(inp["ssm_c_im"][0]).transpose(1, 3, 0, 2).reshape(128, 256))
    dl = np.broadcast_to(np.asarray(inp["ssm_d"][0]).T.reshape(1, 16, 32), (8, 16, 32))
    put("dl", dl.reshape(128, 32))
    put("bglu", np.asarray(inp["ssm_b_glu"][0]).reshape(4, 128).T)
    put("cw", np.asarray(inp["ffn_conv_w"][0]).reshape(3, 44, 128).transpose(2, 1, 0).reshape(128, 132))
    put("cb", np.asarray(inp["ffn_conv_b"][0]).reshape(44, 128).T)
    put("ident", np.eye(128, dtype=np.float32))
    tk = np.arange(128)[:, None]
    tq = np.arange(128)[None, :]
    put("maskb", np.where(tk > tq, -30000.0, 0.0))
    j = (np.arange(128) // 16)
    put("tmask", (j[:, None] <= j[None, :]).astype(np.float32))
    put("kvec", np.broadcast_to(np.asarray(KV, np.float32).reshape(1, 25), (128, 25)))
    put("cidx", np.broadcast_to(np.arange(128, dtype=np.float32).reshape(1, 128), (128, 128)))
    put("i2", np.concatenate([np.eye(64), np.eye(64)], 0))
    put("g1c", np.asarray(inp["ln1_g"][0]).reshape(8, 128).T)
    put("b1c", np.asarray(inp["ln1_b"][0]).reshape(8, 128).T)
    lnc = np.concatenate([np.broadcast_to(np.asarray(inp[k][0]).reshape(1, D), (128, D))
                          for k in ("ln1_g", "ln1_b", "ln2_g", "ln2_b")], 1)
    return {"w_in": f(inp["w_in"][0]), "w_o": f(inp["w_o"][0]), "w_glu": f(inp["ssm_w_glu"][0]),
            "w_up": f(inp["ffn_w_up"][0]), "w_down": f(inp["ffn_w_down"][0]), "w_ple": f(inp["w_ple"][0]),
            "w_gate": f(inp["w_ple_gate"][0]), "cst": cst, "lnc": f(lnc)}


def make_in_maps(inp):
    shared = _prep_shared(inp)
    x = np.asarray(inp["x"], np.float32)
    p = np.asarray(inp["p"], np.float32)
    maps = []
    for b in range(8):
        m = dict(shared)
        m["x"] = np.ascontiguousarray(x[b])
        m["xT"] = np.ascontiguousarray(x[b].T)
        m["pT"] = np.ascontiguousarray(p[0, b].T)
        maps.append(m)
    return maps


def kernel(**inputs):
    nc = build_nc()
    maps = make_in_maps(inputs)
    res = run_bass_kernel_spmd(nc, maps, core_ids=list(range(8)))
    return np.stack([r["out"] for r in res.results], 0).astype(np.float32)
```

```python
import math
from contextlib import ExitStack
import numpy as np
import concourse.bass as bass
import concourse.mybir as mybir
from concourse.bass_utils import run_bass_kernel_spmd
from concourse.alu_op_type import AluOpType as ALU

F32 = mybir.dt.float32
BF16 = mybir.dt.bfloat16
I32 = mybir.dt.int32
AF = mybir.ActivationFunctionType

L = 2048
D = 1024
DFF = 2816
NPAIR = DFF // 128
LN_EPS = 1e-5
ALPHA = 2.0 ** 0.25
LAM_INIT = 0.8 - 0.6 * math.exp(0.0)
TWO_PI_LO = 6.28318

_c = {}
_off = 0
for _n, _w in (("lamq", 256), ("gcol", 1), ("lr", 16), ("li", 16), ("ldt", 16), ("bre", 256), ("bim", 256),
               ("cre", 256), ("cim", 256), ("dl", 32), ("bglu", 4), ("cw", 132), ("cb", 44),
               ("ident", 128), ("maskb", 128), ("tmask", 128), ("kvec", 25), ("cidx", 128), ("i2", 64), ("g1c", 8), ("b1c", 8)):
    _c[_n] = (_off, _off + _w)
    _off += _w
NCST = _off
KV = [0, -1, -2, -3, -4, -5, -6, -7] + list(range(0, 17))


class Buf:
    __slots__ = ("w", "r", "name")

    def __init__(self, name=""):
        self.w = None
        self.r = {}
        self.name = name


class Eng:
    def __init__(self, name, handle, sem):
        self.name = name
        self.h = handle
        self.sem = sem
        self.count = 0
        self.waited = {}


class FW:
    def __init__(self, nc, stack):
        self.nc = nc
        self.stack = stack
        self.engs = {}
        for name, h in (("pe", nc.tensor), ("act", nc.scalar), ("dve", nc.vector),
                        ("pool", nc.gpsimd), ("sp", nc.sync)):
            sem = stack.enter_context(nc.semaphore("sem_" + name))
            self.engs[name] = Eng(name, h, sem)
        self.dma_sems = {}

    def dma_slot(self, name):
        if name not in self.dma_sems:
            sem = self.stack.enter_context(self.nc.semaphore("dsem_" + name))
            self.dma_sems[name] = [sem, 0]
        return self.dma_sems[name]

    def _wait(self, eng, tok):
        key, sem, val = tok
        if eng.waited.get(key, 0) >= val:
            return
        eng.h.wait_ge(sem, val)
        eng.waited[key] = val

    def _deps(self, eng, reads, writes):
        best = {}

        def add(t):
            if t is None:
                return
            if eng.name == "pe" and t[0] == "pe":
                return
            if t[0] not in best or best[t[0]][2] < t[2]:
                best[t[0]] = t
        for b in reads:
            add(b.w)
        for b in writes:
            add(b.w)
            for k, t in b.r.items():
                add(t)
        for t in best.values():
            self._wait(eng, t)

    def _commit(self, tok, reads, writes):
        for b in reads:
            b.r[tok[0]] = tok
        for b in writes:
            b.w = tok
            b.r = {}

    def op(self, engname, fn, reads=(), writes=(), acc=False):
        eng = self.engs[engname]
        if engname == "pe" and not acc:
            for b in writes:
                if b.w is not None and b.w[0] == "pe" and not b.r:
                    raise RuntimeError(f"PSUM clobber: PE overwrites {b.name} before anyone read the previous PE result")
        self._deps(eng, reads, writes)
        ins = fn(eng.h)
        ins.then_inc(eng.sem, 1)
        eng.count += 1
        tok = (eng.name, eng.sem, eng.count)
        self._commit(tok, reads, writes)
        return tok

    def dma(self, qname, slot, out, in_, reads=(), writes=(), **kw):
        eng = self.engs[qname]
        self._deps(eng, reads, writes)
        s = self.dma_slot(slot)
        ins = eng.h.dma_start(out=out, in_=in_, **kw)
        ins.then_inc(s[0], 16)
        s[1] += 16
        tok = ("dma_" + slot, s[0], s[1])
        self._commit(tok, reads, writes)
        return tok

    def barrier(self):
        toks = [(e.name, e.sem, e.count) for e in self.engs.values() if e.count > 0]
        toks += [("dma_" + k, s[0], s[1]) for k, s in self.dma_sems.items() if s[1] > 0]
        for e in self.engs.values():
            for t in toks:
                if t[0] != e.name:
                    self._wait(e, t)


class Arena:
    def __init__(self, tensor, nbytes):
        self.t = tensor
        self.n = nbytes
        self.off = 0
        self.top = nbytes
        self.marks = []

    def alloc_top(self, cols, dt):
        esz = 4 if dt in (F32, I32) else 2
        nb = (cols * esz + 31) // 32 * 32
        assert self.top - nb >= self.off, ("SBUF arena overflow (top)", self.off, nb, self.top)
        self.top -= nb
        a = self.t[:, self.top // 4:(self.top + nb) // 4]
        if dt != F32:
            a = a.bitcast(dt)
        return a[:, 0:cols]

    def alloc(self, cols, dt):
        esz = 4 if dt in (F32, I32) else 2
        nb = (cols * esz + 31) // 32 * 32
        assert self.off + nb <= self.top, ("SBUF arena overflow", self.off, nb, self.top)
        a = self.t[:, self.off // 4:(self.off + nb) // 4]
        self.off += nb
        if dt != F32:
            a = a.bitcast(dt)
        return a[:, 0:cols]

    def mark(self):
        return self.off

    def reset(self, m):
        self.off = m


def build_nc(debug=None):
    nc = bass.Bass("TRN2", target_bir_lowering=False)
    dram = lambda n, s, k="ExternalInput": nc.dram_tensor(n, s, F32, kind=k).ap()
    xT_d = dram("xT", [D, L])
    x_d = dram("x", [L, D])
    pT_d = dram("pT", [256, L])
    win_d = dram("w_in", [D, 2048])
    wo_d = dram("w_o", [D, D])
    wglu_d = dram("w_glu", [512, 512])
    wup_d = dram("w_up", [D, 2 * DFF])
    wdn_d = dram("w_down", [DFF, D])
    wple_d = dram("w_ple", [256, D])
    wg_d = dram("w_gate", [D, D])
    cst_d = dram("cst", [128, NCST])
    lnc_d = dram("lnc", [128, 4 * D])
    out_d = dram("out", [L, D], "ExternalOutput")
    dbg_d = dram("dbg", [D, L], "ExternalOutput") if debug else None

    with ExitStack() as st:
        fw = FW(nc, st)
        ARENA_BYTES = 212736
        arena_t = st.enter_context(nc.sbuf_tensor("arena", [128, ARENA_BYTES // 4], F32))
        ar = Arena(arena_t, ARENA_BYTES)
        banks = [st.enter_context(nc.psum_tensor(f"bank{i}", [128, 512], F32)) for i in range(8)]
        PB = [Buf(f"bank{i}") for i in range(8)]
        pbf = lambda i: banks[i][:, :].bitcast(BF16)

        def A(cols, dt, name=""):
            return ar.alloc(cols, dt), Buf(name)

        NCB = 192
        cstB, BcstB = A(NCB, F32, "cstB")
        mixT, _ = A(8 * L, BF16, "mixT")
        mixT = mixT.rearrange("p (k t) -> p k t", k=8)
        Bmix = [[Buf(f"mix{k}_{g}") for g in range(4)] for k in range(8)]
        identb, Bidb = A(128, BF16)
        maskb, Bmkb = A(128, BF16)
        fw.dma("sp", "cstB", cstB[:, 0:176], cst_d[:, _c["cw"][0]:_c["cb"][1]], writes=[BcstB])
        fw.dma("sp", "cstB", cstB[:, 176:192], cst_d[:, _c["g1c"][0]:_c["b1c"][1]], writes=[BcstB])
        _cb = {"cw": (0, 132), "cb": (132, 176), "g1c": (176, 184), "b1c": (184, 192)}
        CB = lambda n: cstB[:, _cb[n][0]:_cb[n][1]]
        PM = ar.mark()
        cst, Bcst = A(NCST, F32, "cst")
        fw.dma("sp", "cst", cst, cst_d, writes=[Bcst])
        C = lambda n: cst[:, _c[n][0]:_c[n][1]]
        fw.op("dve", lambda e: e.tensor_copy(out=identb, in_=C("ident")), reads=[Bcst], writes=[Bidb])
        fw.op("dve", lambda e: e.tensor_copy(out=maskb, in_=C("maskb")), reads=[Bcst], writes=[Bmkb])

        xT, _ = A(8 * L, BF16, "xT")
        xT = xT.rearrange("p (k t) -> p k t", k=8)
        BxT = Buf("xT")
        for kb in range(8):
            fw.dma("pool", "xT", xT[:, kb, :], xT_d[kb * 128:(kb + 1) * 128, :], writes=[BxT], max_dma_last_dim=4096)
        wslot = []
        for i in range(2):
            t, b = A(8 * 512, BF16, f"wslot{i}")
            wslot.append((t.rearrange("p (k n) -> p k n", k=8), b))
        wglu, Bwglu = A(4 * 512, BF16, "wglu")
        wglu = wglu.rearrange("p (k n) -> p k n", k=4)
        fw.dma("pool", "wglu", wglu, wglu_d.rearrange("(k p) n -> p k n", p=128), writes=[Bwglu])
        def load_w(slot_i, src_cols_list):
            t, b = wslot[slot_i]
            o = 0
            for (c0, c1) in src_cols_list:
                fw.dma("pool", f"ws{slot_i}", t[:, :, o:o + (c1 - c0)],
                       win_d[:, c0:c1].rearrange("(k p) n -> p k n", p=128), writes=[b])
                o += c1 - c0
            return t, b

        head_cols = lambda h: [(128 * h, 128 * h + 128), (512 + 128 * h, 512 + 128 * h + 128), (1024 + 128 * h, 1024 + 128 * h + 128)]
        wtU, BwU = load_w(0, [(1536, 2048)])
        head_w = {0: load_w(1, head_cols(0))}

        PM_A = ar.mark()
        def T_(cols, name=""):
            return A(cols, F32, name)
        s_dt, Bs = T_(16)
        s_a, _ = T_(16); s_th, _ = T_(16); s_q, _ = T_(16); s_fq, _ = T_(16)
        s_i, _ = A(16, I32); s_n, _ = T_(16)
        V = lambda e: e
        dve = lambda fn, r, w: fw.op("dve", fn, reads=r, writes=w)
        act = lambda fn, r, w: fw.op("act", fn, reads=r, writes=w)
        act(lambda e: e.activation(out=s_dt, in_=C("ldt"), func=AF.Exp), [Bcst], [Bs])
        dve(lambda e: e.tensor_tensor(out=s_a, in0=C("lr"), in1=s_dt, op=ALU.mult), [Bcst, Bs], [Bs])
        dve(lambda e: e.tensor_tensor(out=s_th, in0=C("li"), in1=s_dt, op=ALU.mult), [Bcst, Bs], [Bs])
        dve(lambda e: e.tensor_scalar(out=s_q, in0=s_th, scalar1=1.0 / (2 * math.pi), scalar2=None, op0=ALU.mult), [Bs], [Bs])

        def frac(dst, src, itmp, ftmp, shape_reads):
            dve(lambda e: e.tensor_copy(out=itmp, in_=src), shape_reads, shape_reads)
            dve(lambda e: e.tensor_copy(out=ftmp, in_=itmp), shape_reads, shape_reads)
            dve(lambda e: e.tensor_tensor(out=dst, in0=src, in1=ftmp, op=ALU.subtract), shape_reads, shape_reads)
        frac(s_fq, s_q, s_i, s_n, [Bs])
        NK = 25
        pwr, _ = T_(16 * NK); pwi, _ = T_(16 * NK); ph, _ = T_(16 * NK); ph2, _ = T_(16 * NK)
        phi_, _ = A(16 * NK, I32); ex, _ = T_(16 * NK)
        v3 = lambda t: t.rearrange("p (g k) -> p g k", g=16)
        kv3 = C("kvec").unsqueeze(1).to_broadcast([128, 16, NK])
        bc3 = lambda t, n: t.unsqueeze(2).to_broadcast([128, 16, n])
        dve(lambda e: e.tensor_tensor(out=v3(ph), in0=kv3, in1=bc3(s_fq, NK), op=ALU.mult), [Bcst, Bs], [Bs])
        frac(ph, ph, phi_, ph2, [Bs])
        act(lambda e: e.activation(out=pwi, in_=ph, func=AF.Sin, scale=TWO_PI_LO), [Bs], [Bs])
        dve(lambda e: e.tensor_scalar(out=ph, in0=ph, scalar1=0.25, scalar2=None, op0=ALU.add), [Bs], [Bs])
        frac(ph, ph, phi_, ph2, [Bs])
        act(lambda e: e.activation(out=pwr, in_=ph, func=AF.Sin, scale=TWO_PI_LO), [Bs], [Bs])
        dve(lambda e: e.tensor_tensor(out=v3(ph2), in0=kv3, in1=bc3(s_a, NK), op=ALU.mult), [Bcst, Bs], [Bs])
        act(lambda e: e.activation(out=ex, in_=ph2, func=AF.Exp), [Bs], [Bs])
        dve(lambda e: e.tensor_tensor(out=pwr, in0=pwr, in1=ex, op=ALU.mult), [Bs], [Bs])
        dve(lambda e: e.tensor_tensor(out=pwi, in0=pwi, in1=ex, op=ALU.mult), [Bs], [Bs])
        pwr3, pwi3 = v3(pwr), v3(pwi)
        IDX1 = 9
        den, _ = T_(16); t1, _ = T_(16); t2, _ = T_(16); zr, _ = T_(16); fr, _ = T_(16); fi, _ = T_(16)
        lam_r, lam_i = pwr3[:, :, IDX1], pwi3[:, :, IDX1]
        dve(lambda e: e.tensor_tensor(out=den, in0=C("lr"), in1=C("lr"), op=ALU.mult), [Bcst, Bs], [Bs])
        dve(lambda e: e.tensor_tensor(out=t1, in0=C("li"), in1=C("li"), op=ALU.mult), [Bcst, Bs], [Bs])
        dve(lambda e: e.tensor_tensor(out=den, in0=den, in1=t1, op=ALU.add), [Bs], [Bs])
        dve(lambda e: e.reciprocal(out=den, in_=den), [Bs], [Bs])
        dve(lambda e: e.tensor_scalar(out=zr, in0=lam_r, scalar1=-1.0, scalar2=None, op0=ALU.add), [Bs], [Bs])
        dve(lambda e: e.tensor_tensor(out=t1, in0=zr, in1=C("lr"), op=ALU.mult), [Bcst, Bs], [Bs])
        dve(lambda e: e.tensor_tensor(out=t2, in0=lam_i, in1=C("li"), op=ALU.mult), [Bcst, Bs], [Bs])
        dve(lambda e: e.tensor_tensor(out=t1, in0=t1, in1=t2, op=ALU.add), [Bs], [Bs])
        dve(lambda e: e.tensor_tensor(out=fr, in0=t1, in1=den, op=ALU.mult), [Bs], [Bs])
        dve(lambda e: e.tensor_tensor(out=t1, in0=lam_i, in1=C("lr"), op=ALU.mult), [Bcst, Bs], [Bs])
        dve(lambda e: e.tensor_tensor(out=t2, in0=zr, in1=C("li"), op=ALU.mult), [Bcst, Bs], [Bs])
        dve(lambda e: e.tensor_tensor(out=t1, in0=t1, in1=t2, op=ALU.subtract), [Bs], [Bs])
        dve(lambda e: e.tensor_tensor(out=fi, in0=t1, in1=den, op=ALU.mult), [Bs], [Bs])
        bbr, _ = T_(256); bbi, _ = T_(256); tq1, _ = T_(256)
        g16 = lambda t: t.rearrange("p (g h) -> p g h", g=16)
        dve(lambda e: e.tensor_tensor(out=g16(bbr), in0=g16(C("bre")), in1=bc3(fr, 16), op=ALU.mult), [Bcst, Bs], [Bs])
        dve(lambda e: e.tensor_tensor(out=g16(tq1), in0=g16(C("bim")), in1=bc3(fi, 16), op=ALU.mult), [Bcst, Bs], [Bs])
        dve(lambda e: e.tensor_tensor(out=bbr, in0=bbr, in1=tq1, op=ALU.subtract), [Bs], [Bs])
        dve(lambda e: e.tensor_tensor(out=g16(bbi), in0=g16(C("bim")), in1=bc3(fr, 16), op=ALU.mult), [Bcst, Bs], [Bs])
        dve(lambda e: e.tensor_tensor(out=g16(tq1), in0=g16(C("bre")), in1=bc3(fi, 16), op=ALU.mult), [Bcst, Bs], [Bs])
        dve(lambda e: e.tensor_tensor(out=bbi, in0=bbi, in1=tq1, op=ALU.add), [Bs], [Bs])
        Rcol, _ = T_(16); f16, _ = T_(16)
        act(lambda e: e.activation(out=Rcol, in_=s_a, func=AF.Exp, scale=16.0), [Bs], [Bs])
        dve(lambda e: e.tensor_scalar(out=f16, in0=s_fq, scalar1=16.0, scalar2=None, op0=ALU.mult), [Bs], [Bs])
        frac(f16, f16, s_i, s_n, [Bs])
        zero1, _ = T_(1)
        dve(lambda e: e.memset(zero1, 0.0), [], [Bs])

        NG = 4
        NGR = 2 * NG
        A0r_s = [A(NG * 128, BF16)[0] for _ in range(2)]; A0i_s = [A(NG * 128, BF16)[0] for _ in range(2)]
        Bmr_s = [A(NG * 272, BF16)[0] for _ in range(2)]; Bmi_s = [A(NG * 272, BF16)[0] for _ in range(2)]
        BA0_s = [Buf("A0a"), Buf("A0b")]; BBm_s = [Buf("Bma"), Buf("Bmb")]
        tA, BtA = T_(NG * 272); tB, _ = T_(NG * 272)
        Dg_s = [A(NG * 2 * 3 * 64, BF16)[0] for _ in range(2)]
        BDg_s = [Buf("Dga"), Buf("Dgb")]
        TT, _ = A(NGR * 256, BF16)
        GmR, _ = A(NGR * 2 * 64, BF16); GmI, _ = A(NGR * 2 * 64, BF16)
        cosA, _ = T_(16 * 128); sinA, _ = T_(16 * 128)
        rph, _ = T_(4 * 128); rph2, _ = T_(4 * 128)
        rphi, _ = A(4 * 128, I32)
        SRsh, _ = A(NG * 128, BF16); SIsh, _ = A(NG * 128, BF16)
        l2a, _ = T_(NG * 128); l2b, _ = T_(NG * 128); l2c, _ = T_(NG * 128); l2d, _ = T_(NG * 128)
        Utok, _ = A(NGR * 256, BF16)
        UT, _ = A(NGR * 2 * 128, BF16)
        ygT, _ = A(4 * L, BF16)
        ygT = ygT.rearrange("p (k t) -> p k t", k=4)
        gtmp, _ = T_(512); gtmp2, _ = T_(512)
        BDg, BTT, BGm, Bl2x, BS, Bl2, BU, BUT, Bg = (Buf(n) for n in "Dg TT Gm l2x S l2 U UT g".split())
        BYt = BU
        Bg2 = Buf("g2")
        Byg = [Buf(f"yg{k}") for k in range(4)]
        tA4 = tA.rearrange("p (g t h) -> p g t h", g=NG, t=17)
        tB4 = tB.rearrange("p (g t h) -> p g t h", g=NG, t=17)
        Dg5_s = [d.rearrange("p (g j k n) -> p g j k n", g=NG, j=2, k=3) for d in Dg_s]
        TT3 = TT.rearrange("p (g n) -> p g n", g=NGR)
        GmR4 = GmR.rearrange("p (g j n) -> p g j n", g=NGR, j=2)
        GmI4 = GmI.rearrange("p (g j n) -> p g j n", g=NGR, j=2)
        Brot = Buf("rot")
        for hh in range(4):
            gsl = slice(4 * hh, 4 * hh + 4)
            csl = slice(4 * hh * 128, (4 * hh + 4) * 128)
            cid3 = C("cidx").unsqueeze(1).to_broadcast([128, 4, 128])
            dve(lambda e, gsl=gsl: e.tensor_tensor(out=rph.rearrange("p (g c) -> p g c", g=4), in0=cid3,
                                                   in1=f16[:, gsl].unsqueeze(2).to_broadcast([128, 4, 128]), op=ALU.mult), [Bs, Bcst], [Bl2x])
            frac(rph, rph, rphi, rph2, [Bl2x])
            act(lambda e, csl=csl: e.activation(out=sinA[:, csl], in_=rph, func=AF.Sin, scale=TWO_PI_LO), [Bl2x], [Brot])
            dve(lambda e: e.tensor_scalar(out=rph, in0=rph, scalar1=0.25, scalar2=None, op0=ALU.add), [Bl2x], [Bl2x])
            frac(rph, rph, rphi, rph2, [Bl2x])
            act(lambda e, csl=csl: e.activation(out=cosA[:, csl], in_=rph, func=AF.Sin, scale=TWO_PI_LO), [Bl2x], [Brot])
        SR3 = SRsh.rearrange("p (g c) -> p g c", g=NG)
        SI3 = SIsh.rearrange("p (g c) -> p g c", g=NG)
        Ut4 = Utok.rearrange("p (g t h) -> p g t h", g=NGR, t=16)
        Yt4 = Utok.rearrange("p (t g h) -> p t g h", t=16, g=NGR)
        UT4 = UT.rearrange("p (g j c) -> p g j c", g=NGR, j=2)
        dve(lambda e: e.memset(SRsh, 0.0), [], [BS])
        dve(lambda e: e.memset(SIsh, 0.0), [], [BS])

        def emit_tables(qd):
            gp0 = qd * NG
            gps = slice(gp0, gp0 + NG)
            bcj = lambda t, idx0, n, m: t[:, gps, idx0:idx0 + n].unsqueeze(3).to_broadcast([128, NG, n, m])
            bch = lambda t, n: g16(t)[:, gps, :].unsqueeze(2).to_broadcast([128, NG, n, 16])
            A0r4 = A0r_s[qd % 2].rearrange("p (g j h) -> p g j h", g=NG, j=8)
            A0i4 = A0i_s[qd % 2].rearrange("p (g j h) -> p g j h", g=NG, j=8)
            Bmr4 = Bmr_s[qd % 2].rearrange("p (g t h) -> p g t h", g=NG, t=17)
            Bmi4 = Bmi_s[qd % 2].rearrange("p (g t h) -> p g t h", g=NG, t=17)
            BA0, BBm = BA0_s[qd % 2], BBm_s[qd % 2]
            Dg5 = Dg5_s[qd % 2]
            BDg = BDg_s[qd % 2]
            pl = lambda fn, r, w: fw.op("dve", fn, reads=r, writes=w)
            tA8 = tA[:, 0:NG * 128].rearrange("p (g j h) -> p g j h", g=NG, j=8)
            tB8 = tB[:, 0:NG * 128].rearrange("p (g j h) -> p g j h", g=NG, j=8)
            pl(lambda e: e.tensor_tensor(out=tA8, in0=bcj(pwr3, 0, 8, 16), in1=bch(bbr, 8), op=ALU.mult), [Bs], [BtA])
            pl(lambda e: e.tensor_tensor(out=tB8, in0=bcj(pwi3, 0, 8, 16), in1=bch(bbi, 8), op=ALU.mult), [Bs], [BtA])
            pl(lambda e: e.tensor_tensor(out=A0r4, in0=tA8, in1=tB8, op=ALU.subtract), [BtA], [BA0])
            pl(lambda e: e.tensor_tensor(out=tA8, in0=bcj(pwr3, 0, 8, 16), in1=bch(bbi, 8), op=ALU.mult), [Bs], [BtA])
            pl(lambda e: e.tensor_tensor(out=tB8, in0=bcj(pwi3, 0, 8, 16), in1=bch(bbr, 8), op=ALU.mult), [Bs], [BtA])
            pl(lambda e: e.tensor_tensor(out=A0i4, in0=tA8, in1=tB8, op=ALU.add), [BtA], [BA0])
            cch = lambda n: g16(C(n))[:, gps, :].unsqueeze(2).to_broadcast([128, NG, 17, 16])
            pl(lambda e: e.tensor_tensor(out=tA4, in0=bcj(pwr3, 8, 17, 16), in1=cch("cre"), op=ALU.mult), [Bs, Bcst], [BtA])
            pl(lambda e: e.tensor_tensor(out=tB4, in0=bcj(pwi3, 8, 17, 16), in1=cch("cim"), op=ALU.mult), [Bs, Bcst], [BtA])
            pl(lambda e: e.tensor_tensor(out=Bmr4, in0=tA4, in1=tB4, op=ALU.subtract), [BtA], [BBm])
            pl(lambda e: e.tensor_tensor(out=tA4, in0=bcj(pwr3, 8, 17, 16), in1=cch("cim"), op=ALU.mult), [Bs, Bcst], [BtA])
            pl(lambda e: e.tensor_tensor(out=tB4, in0=bcj(pwi3, 8, 17, 16), in1=cch("cre"), op=ALU.mult), [Bs, Bcst], [BtA])
            pl(lambda e: e.tensor_tensor(out=tA4, in0=tA4, in1=tB4, op=ALU.add), [BtA], [BtA])
            pl(lambda e: e.tensor_scalar(out=Bmi4, in0=tA4, scalar1=-1.0, scalar2=None, op0=ALU.mult), [BtA], [BBm])
            i2b = C("i2").unsqueeze(1).to_broadcast([128, NG, 64])
            for jb in range(2):
                idx = 8 + 15 - 8 * jb
                for kind in range(3):
                    srcb = (pwr3, pwi3, pwi3)[kind][:, gps, idx:idx + 1].to_broadcast([128, NG, 64])
                    dst = Dg5[:, :, jb, kind, :]
                    if kind < 2:
                        dve(lambda e, dst=dst, srcb=srcb: e.tensor_tensor(out=dst, in0=i2b, in1=srcb, op=ALU.mult), [Bs, Bcst], [BDg])
                    else:
                        dve(lambda e, dst=dst, srcb=srcb: e.scalar_tensor_tensor(out=dst, in0=i2b, scalar=-1.0, in1=srcb,
                                                                                 op0=ALU.mult, op1=ALU.mult), [Bs, Bcst], [BDg])
        UCOL0 = 1536
        emit_tables(0)
        for qd in range(16 // NG):
            gp0 = qd * NG
            gps = slice(gp0, gp0 + NG)
            wt, Bw = wtU, BwU
            A0r4 = A0r_s[qd % 2].rearrange("p (g j h) -> p g j h", g=NG, j=8)
            A0i4 = A0i_s[qd % 2].rearrange("p (g j h) -> p g j h", g=NG, j=8)
            Bmr4 = Bmr_s[qd % 2].rearrange("p (g t h) -> p g t h", g=NG, t=17)
            Bmi4 = Bmi_s[qd % 2].rearrange("p (g t h) -> p g t h", g=NG, t=17)
            BA0, BBm = BA0_s[qd % 2], BBm_s[qd % 2]
            Dg5 = Dg5_s[qd % 2]
            BDg = BDg_s[qd % 2]
            cosT = cosA[:, gp0 * 128:(gp0 + NG) * 128]
            sinT = sinA[:, gp0 * 128:(gp0 + NG) * 128]

            for tb4 in range(4):
                bk = 4 + tb4 % 2
                ps = banks[bk]

                def mm_u(e, tb4=tb4, ps=ps):
                    last = None
                    for ti in range(4):
                        tau = tb4 * 4 + ti
                        for kb in range(8):
                            last = e.matmul(ps[:, ti * 128:(ti + 1) * 128], lhsT=xT[:, kb, tau:L:16], rhs=wt[:, kb, 128 * qd:128 * qd + 128],
                                            start=(kb == 0 and ti == 0), stop=(kb == 7), skip_group_check=True)
                    return last
                fw.op("pe", mm_u, [BxT, Bw], [PB[bk]])
                src = ps[:, :].rearrange("p (t g h) -> p t g h", t=4, g=NGR)
                dst = Ut4[:, :, tb4 * 4:tb4 * 4 + 4, :].rearrange("p g t h -> p t g h")
                act(lambda e, src=src, dst=dst: e.activation(out=dst, in_=src, func=AF.Copy), [PB[bk]], [BU])
            for half in range(2):
                bk = 6 + half
                psb = pbf(bk)

                def tr_u(e, half=half, psb=psb):
                    last = None
                    for i in range(8):
                        gl, jb = (half * 8 + i) // 2, (half * 8 + i) % 2
                        last = e.transpose(psb[:, i * 128:(i + 1) * 128],
                                           Ut4[:, gl, jb * 8:(jb + 1) * 8, :].rearrange("p j h -> p (j h)"), identb)
                    return last
                fw.op("pe", tr_u, [BU, Bidb], [PB[bk]])
                dst = UT[:, half * 1024:(half + 1) * 1024]
                act(lambda e, dst=dst, psb=psb: e.activation(out=dst, in_=psb, func=AF.Copy), [PB[bk]], [BUT])
            for pb in range(NGR // 2):
                def mm_toep(e, pb=pb):
                    last = None
                    for k in range(2):
                        gl = 4 * (pb // 2) + (pb % 2) + 2 * k
                        gpl, P0 = gl // 2, 64 * (gl % 2)
                        o = banks[pb][:, k * 256:(k + 1) * 256]
                        e.matmul(o, lhsT=A0r4[P0:P0 + 64, gpl].rearrange("p j h -> p (j h)"),
                                 rhs=Bmr4[P0:P0 + 64, gpl, 0:16].rearrange("p t h -> p (t h)"), start=(k == 0), stop=False, skip_group_check=True)
                        last = e.matmul(o, lhsT=A0i4[P0:P0 + 64, gpl].rearrange("p j h -> p (j h)"),
                                        rhs=Bmi4[P0:P0 + 64, gpl, 0:16].rearrange("p t h -> p (t h)"), start=False, stop=True, skip_group_check=True)
                    return last
                fw.op("pe", mm_toep, [BA0, BBm], [PB[pb]])
            for pb in range(NGR // 2):
                def mm_gm(e, pb=pb):
                    last = None
                    for k in range(2):
                        gl = 4 * (pb // 2) + (pb % 2) + 2 * k
                        gpl, P0 = gl // 2, 64 * (gl % 2)
                        for jb in range(2):
                            l_r = A0r4[P0:P0 + 64, gpl].rearrange("p j h -> p (j h)")
                            l_i = A0i4[P0:P0 + 64, gpl].rearrange("p j h -> p (j h)")
                            d_re = Dg5[P0:P0 + 64, gpl, jb, 0, :]
                            d_im = Dg5[P0:P0 + 64, gpl, jb, 1, :]
                            d_in = Dg5[P0:P0 + 64, gpl, jb, 2, :]
                            c0 = k * 256 + jb * 128
                            o_r = banks[4 + pb][:, c0:c0 + 64]
                            o_i = banks[4 + pb][:, c0 + 64:c0 + 128]
                            first = (jb == 0 and k == 0)
                            e.matmul(o_r, lhsT=l_r, rhs=d_re, start=first, stop=False, skip_group_check=True)
                            e.matmul(o_r, lhsT=l_i, rhs=d_in, start=False, stop=True, skip_group_check=True)
                            e.matmul(o_i, lhsT=l_r, rhs=d_im, start=False, stop=False, skip_group_check=True)
                            last = e.matmul(o_i, lhsT=l_i, rhs=d_re, start=False, stop=True, skip_group_check=True)
                    return last
                fw.op("pe", mm_gm, [BA0, BDg], [PB[4 + pb]])
            for pb in range(NGR // 2):
                ps3 = banks[pb][:, :].rearrange("p (g n) -> p g n", g=2)
                gt_ = (gtmp, gtmp2)[pb % 2]
                Bgt_ = (Bg, Bg2)[pb % 2]
                gt3 = gt_[:, 0:256].rearrange("p (g n) -> p g n", g=2)
                dve(lambda e, ps3=ps3, gt3=gt3: e.tensor_tensor(out=gt3, in0=ps3[:, :, 0:128],
                                                                 in1=C("tmask").unsqueeze(1).to_broadcast([128, 2, 128]), op=ALU.mult),
                    [PB[pb], Bcst], [Bgt_])
                gl0 = 4 * (pb // 2) + (pb % 2)
                for k in range(2):
                    gl = gl0 + 2 * k
                    g = 2 * gp0 + gl
                    dve(lambda e, gl=gl, g=g, k=k, gt3=gt3: e.scalar_tensor_tensor(out=TT3[:, gl, 0:128], in0=C("ident"),
                                                                                   scalar=C("dl")[:, g:g + 1], in1=gt3[:, k, :],
                                                                                   op0=ALU.mult, op1=ALU.add), [Bgt_, Bcst], [BTT])
                dve(lambda e, pb=pb, ps3=ps3: e.tensor_copy(out=TT3[:, gl0:gl0 + 3:2, 128:256], in_=ps3[:, :, 128:256]),
                    [PB[pb]], [BTT])
            for pb in range(NGR // 2):
                gl0 = 4 * (pb // 2) + (pb % 2)
                v = banks[4 + pb][:, :].rearrange("p (g j r n) -> p g j r n", g=2, j=2, r=2)
                act(lambda e, pb=pb, v=v: e.activation(out=GmR4[:, gl0:gl0 + 3:2], in_=v[:, :, :, 0, :], func=AF.Copy), [PB[4 + pb]], [BGm])
                act(lambda e, pb=pb, v=v: e.activation(out=GmI4[:, gl0:gl0 + 3:2], in_=v[:, :, :, 1, :], func=AF.Copy), [PB[4 + pb]], [BGm])

            def mm_G(e):
                last = None
                for ri, (bk, Gm4) in enumerate(((0, GmR4), (1, GmI4))):
                    for gpl in range(NG):
                        for g2 in range(2):
                            gl = 2 * gpl + g2
                            for jb in range(2):
                                last = e.matmul(banks[bk][64 * g2:64 * g2 + 64, gpl * 128:(gpl + 1) * 128],
                                                lhsT=Gm4[:, gl, jb, :], rhs=UT4[:, gl, jb, :],
                                                start=(jb == 0 and gpl == 0), stop=(jb == 1), skip_group_check=True)
                return last
            fw.op("pe", mm_G, [BGm, BUT], [PB[0], PB[1]])
            GRp, GIp = banks[0][:, :], banks[1][:, :]
            dve(lambda e: e.tensor_tensor(out=l2a, in0=GRp, in1=cosT, op=ALU.mult), [PB[0], Brot], [Bl2])
            dve(lambda e: e.tensor_tensor(out=l2b, in0=GIp, in1=sinT, op=ALU.mult), [PB[1], Brot], [Bl2])
            dve(lambda e: e.tensor_tensor(out=l2a, in0=l2a, in1=l2b, op=ALU.add), [Bl2], [Bl2])
            dve(lambda e: e.tensor_tensor(out=l2b, in0=GIp, in1=cosT, op=ALU.mult), [PB[1], Brot], [Bl2])
            dve(lambda e: e.tensor_tensor(out=l2c, in0=GRp, in1=sinT, op=ALU.mult), [PB[0], Brot], [Bl2])
            dve(lambda e: e.tensor_tensor(out=l2b, in0=l2b, in1=l2c, op=ALU.subtract), [Bl2], [Bl2])
            for gpl in range(NG):
                sl = slice(gpl * 128, (gpl + 1) * 128)
                rc = Rcol[:, gp0 + gpl:gp0 + gpl + 1].to_broadcast([128, 128])
                dve(lambda e, sl=sl, rc=rc: e.tensor_tensor_scan(out=l2c[:, sl], data0=rc, data1=l2a[:, sl], initial=0.0,
                                                                 op0=ALU.mult, op1=ALU.add), [Bl2, Bs], [Bl2])
                dve(lambda e, sl=sl, rc=rc: e.tensor_tensor_scan(out=l2d[:, sl], data0=rc, data1=l2b[:, sl], initial=0.0,
                                                                 op0=ALU.mult, op1=ALU.add), [Bl2, Bs], [Bl2])
            dve(lambda e: e.tensor_tensor(out=l2a, in0=l2c, in1=cosT, op=ALU.mult), [Bl2, Brot], [Bl2])
            dve(lambda e: e.tensor_tensor(out=l2b, in0=l2d, in1=sinT, op=ALU.mult), [Bl2, Brot], [Bl2])
            l2a3 = l2a.rearrange("p (g c) -> p g c", g=NG)
            l2b3 = l2b.rearrange("p (g c) -> p g c", g=NG)
            dve(lambda e: e.tensor_tensor(out=SR3[:, :, 1:128], in0=l2a3[:, :, 0:127], in1=l2b3[:, :, 0:127], op=ALU.subtract),
                [Bl2], [BS])
            dve(lambda e: e.tensor_tensor(out=l2a, in0=l2c, in1=sinT, op=ALU.mult), [Bl2, Brot], [Bl2])
            dve(lambda e: e.tensor_tensor(out=l2b, in0=l2d, in1=cosT, op=ALU.mult), [Bl2, Brot], [Bl2])
            dve(lambda e: e.tensor_tensor(out=SI3[:, :, 1:128], in0=l2a3[:, :, 0:127], in1=l2b3[:, :, 0:127], op=ALU.add),
                [Bl2], [BS])
            if qd + 1 < 16 // NG:
                emit_tables(qd + 1)
            for pr in range(NGR // 2):
                bk = 2 + pr % 2
                ps = banks[bk]

                def mm_y(e, pr=pr, ps=ps):
                    last = None
                    for k in range(2):
                        gl = 4 * (pr // 2) + (pr % 2) + 2 * k
                        gpl, g2 = gl // 2, gl % 2
                        P0 = 64 * g2
                        o = ps[:, k * 256:(k + 1) * 256]
                        e.matmul(o, lhsT=UT4[:, gl, 0, :], rhs=TT3[:, gl, 0:256], start=(k == 0), stop=False, skip_group_check=True)
                        e.matmul(o[:, 128:256], lhsT=UT4[:, gl, 1, :], rhs=TT3[:, gl, 0:128], start=False, stop=False, skip_group_check=True)
                        e.matmul(o, lhsT=SR3[P0:P0 + 64, gpl, :], rhs=Bmr4[P0:P0 + 64, gpl, 1:17].rearrange("p t h -> p (t h)"),
                                 start=False, stop=False, skip_group_check=True)
                        last = e.matmul(o, lhsT=SI3[P0:P0 + 64, gpl, :], rhs=Bmi4[P0:P0 + 64, gpl, 1:17].rearrange("p t h -> p (t h)"),
                                        start=False, stop=True, skip_group_check=True)
                    return last
                fw.op("pe", mm_y, [BUT, BTT, BS, BBm], [PB[bk]])
                gl0 = 4 * (pr // 2) + (pr % 2)
                dst = Yt4[:, :, gl0:gl0 + 3:2, :].rearrange("p t g h -> p g t h")
                act(lambda e, ps=ps, dst=dst: e.activation(out=dst, in_=ps[:, :].rearrange("p (g t h) -> p g t h", g=2, t=16),
                                                           func=AF.Gelu_apprx_tanh), [PB[bk]], [BYt])
            for half in range(2):
                bk = 6 + half
                psb = pbf(bk)

                def tr_y(e, half=half, psb=psb):
                    last = None
                    for i in range(8):
                        tau = half * 8 + i
                        last = e.transpose(psb[:, i * 128:(i + 1) * 128], Yt4[:, tau].rearrange("p g h -> p (g h)"), identb)
                    return last
                fw.op("pe", tr_y, [BYt, Bidb], [PB[bk]])
                dst = ygT[:, qd, :].rearrange("p (c t) -> p t c", t=16)[:, half * 8:half * 8 + 8, :]
                src = psb.rearrange("p (t c) -> p t c", t=8)
                act(lambda e, dst=dst, src=src: e.activation(out=dst, in_=src, func=AF.Copy), [PB[bk]], [Byg[qd]])

        hb, Bhb = A(4, F32)
        dve(lambda e: e.tensor_scalar(out=hb, in0=C("bglu"), scalar1=0.5, scalar2=None, op0=ALU.mult), [Bcst], [Bhb])
        gth, Bgth = A(512, BF16)
        for cb in range(4):
            for tg in range(4):
                bk = (cb * 4 + tg) % 2
                ps = banks[bk]

                def mm_glu(e, cb=cb, tg=tg, ps=ps):
                    last = None
                    for kb in range(4):
                        last = e.matmul(ps[:, :], lhsT=wglu[:, kb, cb * 128:(cb + 1) * 128], rhs=ygT[:, kb, tg * 512:(tg + 1) * 512],
                                        start=(kb == 0), stop=(kb == 3))
                    return last
                fw.op("pe", mm_glu, [Bwglu] + Byg, [PB[bk]])
                act(lambda e, cb=cb, ps=ps: e.activation(out=gth, in_=ps[:, :], func=AF.Sigmoid, bias=C("bglu")[:, cb:cb + 1]),
                    [PB[bk], Bcst], [Bgth])
                dve(lambda e, cb=cb, tg=tg: e.tensor_tensor(out=mixT[:, 4 + cb, tg * 512:(tg + 1) * 512], in0=gth,
                                                            in1=ygT[:, cb, tg * 512:(tg + 1) * 512], op=ALU.mult),
                    [Bgth, Byg[cb]], [Bmix[4 + cb][tg]])

        fw.barrier()
        ar.reset(PM_A)
        AT = lambda cols, dt: (ar.alloc_top(cols, dt), Buf())
        lnc, Blnc = AT(4 * D, F32)
        WoG, BWoG = AT(8 * D, BF16)
        WoG = WoG.rearrange("p (k n) -> p k n", k=8)
        Wple, BWple = AT(2 * D, BF16)
        Wple = Wple.rearrange("p (k n) -> p k n", k=2)
        NWUP = 3
        NCH = NPAIR // 2
        wup = []
        for i in range(NWUP):
            t, b = AT(8 * 512, BF16)
            wup.append((t.rearrange("p (k n) -> p k n", k=8), b))

        def emit_wdma(cidx):
            wt_, Bw_ = wup[cidx % NWUP]
            sn = f"wup{cidx % NWUP}"
            fw.dma("pool", sn, wt_[:, :, 0:256], wup_d[:, 256 * cidx:256 * cidx + 256].rearrange("(k p) n -> p k n", p=128), writes=[Bw_])
            fw.dma("pool", sn, wt_[:, :, 256:512],
                   wup_d[:, DFF + 256 * cidx:DFF + 256 * cidx + 256].rearrange("(k p) n -> p k n", p=128), writes=[Bw_])
        fw.dma("sp", "lnc", lnc, lnc_d, writes=[Blnc])
        lnv = lnc.rearrange("p (v n) -> p v n", v=4)
        dve(lambda e: e.tensor_scalar(out=lnv[:, 0:2, :], in0=lnv[:, 0:2, :], scalar1=ALPHA, scalar2=None, op0=ALU.mult), [Blnc], [Blnc])
        fw.dma("pool", "wog", WoG, wo_d.rearrange("(k p) n -> p k n", p=128), writes=[BWoG])
        emit_wdma(0)
        emit_wdma(1)
        fw.dma("pool", "wple", Wple, wple_d.rearrange("(k p) n -> p k n", p=128), writes=[BWple])
        if debug == "ssm":
            dbt, Bdb = A(L, F32)
            for k in range(4):
                dve(lambda e, k=k: e.tensor_copy(out=dbt, in_=mixT[:, 4 + k, :]), Bmix[4 + k], [Bdb])
                fw.dma("sp", "out", dbg_d[k * 128:(k + 1) * 128, :], dbt, reads=[Bdb])
            fw.barrier()
            return nc
        qT = [[A(L, BF16)[0] for _c2 in range(2)] for _ in range(2)]
        kT = [A(L, BF16)[0] for _ in range(2)]
        Vh = [A(16 * 130, BF16)[0].rearrange("p (b e) -> p b e", b=16) for _ in range(2)]
        Bq = [Buf() for _ in range(2)]; Bk = [Buf() for _ in range(2)]; Bv = [Buf() for _ in range(2)]
        NPT = 4
        Pt = [A(512, BF16)[0] for _ in range(NPT)]
        BPt = [Buf() for _ in range(NPT)]
        O1, BO1 = A(512, F32)
        otmp, Bot = A(512, F32)
        osq, _ = A(512, F32)
        oN, BoN = A(512, BF16)
        O1v = O1.rearrange("p (j e) -> p j e", j=4)
        otv = otmp.rearrange("p (j e) -> p j e", j=4)
        osv = osq.rearrange("p (j e) -> p j e", j=4)
        oNv = oN.rearrange("p (j e) -> p j e", j=4)
        lt, Blt = A(64, F32)
        lcol, Blc = A(8, F32)
        rz, Brz = A(4, F32)
        ssum, _ = A(4, F32)
        mhalf, Bmh = A(4, F32)
        for i in range(2):
            dve(lambda e, i=i: e.memset(Vh[i][:, :, 128:130], 1.0), [], [Bv[i]])
            for c2 in range(2):
                dve(lambda e, i=i, c2=c2: e.memset(qT[i][c2], 0.0), [], [Bq[i]])
        dve(lambda e: e.memset(mhalf, -0.5), [], [Bmh])
        lq = C("lamq")
        for i in range(2):
            dve(lambda e, i=i: e.tensor_tensor(out=lt, in0=lq[:, 128 * i:128 * i + 64], in1=lq[:, 128 * i + 64:128 * i + 128], op=ALU.mult),
                [Bcst], [Blt])
            dve(lambda e, i=i: e.tensor_reduce(out=lcol[:, i:i + 1], in_=lt, axis=mybir.AxisListType.X, op=ALU.add), [Blt], [Blc])
        act(lambda e: e.activation(out=lcol[:, 0:2], in_=lcol[:, 0:2], func=AF.Exp), [Blc], [Blc])
        dve(lambda e: e.tensor_tensor(out=lcol[:, 2:3], in0=lcol[:, 1:2], in1=lcol[:, 0:1], op=ALU.subtract), [Blc], [Blc])
        dve(lambda e: e.tensor_scalar(out=lcol[:, 2:3], in0=lcol[:, 2:3], scalar1=-LAM_INIT, scalar2=None, op0=ALU.add), [Blc], [Blc])
        nlam = lcol[:, 2:3]
        dve(lambda e: e.tensor_scalar(out=lcol[:, 3:4], in0=C("gcol"), scalar1=1.0 - LAM_INIT, scalar2=None, op0=ALU.mult), [Bcst, Blc], [Blc])
        gsc = lcol[:, 3:4]

        pending = []
        SC = 1.0 / 8.0

        def proj_groups(h):
            sl = h % 2
            wt, Bw = head_w[h]
            fns = []
            for which, Bd in enumerate((Bq[sl], Bk[sl])):
                for tg in range(4):
                    def grp(bk, which=which, tg=tg, Bd=Bd, wt=wt, Bw=Bw, sl=sl):
                        ps = banks[bk]

                        def mm_p(e):
                            last = None
                            for kb in range(8):
                                last = e.matmul(ps[:, :], lhsT=wt[:, kb, which * 128:(which + 1) * 128], rhs=xT[:, kb, tg * 512:(tg + 1) * 512],
                                                start=(kb == 0), stop=(kb == 7))
                            return last
                        fw.op("pe", mm_p, [Bw, BxT], [PB[bk]])
                        cs = slice(tg * 512, (tg + 1) * 512)
                        if which == 0:
                            act(lambda e: e.activation(out=qT[sl][0][0:64, cs], in_=ps[0:64, :], func=AF.Copy), [PB[bk]], [Bd])
                            dve(lambda e: e.tensor_copy(out=qT[sl][1][64:128, cs], in_=ps[64:128, :]), [PB[bk]], [Bd])
                        else:
                            dve(lambda e: e.tensor_copy(out=kT[sl][:, cs], in_=ps[:, :]), [PB[bk]], [Bd])
                    fns.append(grp)
            for tb4 in range(4):
                def grpv(bk, tb4=tb4, wt=wt, Bw=Bw, sl=sl):
                    ps = banks[bk]

                    def mm_v(e):
                        last = None
                        for ti in range(4):
                            tkb = tb4 * 4 + ti
                            for kb in range(8):
                                last = e.matmul(ps[:, ti * 128:(ti + 1) * 128], lhsT=xT[:, kb, tkb * 128:(tkb + 1) * 128], rhs=wt[:, kb, 256:384],
                                                start=(kb == 0 and ti == 0), stop=(kb == 7), skip_group_check=True)
                        return last
                    fw.op("pe", mm_v, [Bw, BxT], [PB[bk]])
                    src = ps[:, :].rearrange("p (b e) -> p b e", b=4)
                    dst = Vh[sl][:, tb4 * 4:tb4 * 4 + 4, 0:128]
                    dve(lambda e: e.tensor_copy(out=dst, in_=src), [PB[bk]], [Bv[sl]])
                fns.append(grpv)
            return fns

        def finalize(h, G, c, aset, accv):
            nonlocal_pending = pending
            for a in range(2):
                acc = accv[a]
                jsl = slice(2 * a, 2 * a + 2)
                PBa = PB[aset[a]]
                dve(lambda e, acc=acc, a=a: e.reciprocal(out=rz[:, 2 * a:2 * a + 2], in_=acc[:, :, 128]), [PBa], [Brz])
                if c == 0:
                    dve(lambda e, acc=acc, jsl=jsl, a=a: e.tensor_tensor(
                        out=O1v[:, jsl, :], in0=acc[:, :, 0:128],
                        in1=rz[:, 2 * a:2 * a + 2].unsqueeze(2).to_broadcast([128, 2, 128]), op=ALU.mult), [PBa, Brz], [BO1])
                else:
                    dve(lambda e, a=a: e.tensor_scalar(out=rz[:, 2 * a:2 * a + 2], in0=rz[:, 2 * a:2 * a + 2], scalar1=nlam, scalar2=None,
                                                       op0=ALU.mult), [Brz, Blc], [Brz])
                    dve(lambda e, acc=acc, jsl=jsl, a=a: e.tensor_tensor(
                        out=otv[:, jsl, :], in0=acc[:, :, 0:128],
                        in1=rz[:, 2 * a:2 * a + 2].unsqueeze(2).to_broadcast([128, 2, 128]), op=ALU.mult), [PBa, Brz], [Bot])
            if c == 1:
                dve(lambda e: e.tensor_tensor(out=otmp, in0=otmp, in1=O1, op=ALU.add), [Bot, BO1], [Bot])
                dve(lambda e: e.tensor_tensor(out=osq, in0=otmp, in1=otmp, op=ALU.mult), [Bot], [Bot])
                dve(lambda e: e.tensor_reduce(out=ssum, in_=osv, axis=mybir.AxisListType.X, op=ALU.add), [Bot], [Bot])
                dve(lambda e: e.tensor_scalar(out=ssum, in0=ssum, scalar1=1.0 / 128.0, scalar2=LN_EPS, op0=ALU.mult, op1=ALU.add),
                    [Bot], [Bot])
                fw.op("pool", lambda e: e.tensor_tensor(out=ssum, in0=ssum, in1=mhalf, op=ALU.pow), [Bot, Bmh], [Bot])
                dve(lambda e: e.tensor_tensor(out=oNv, in0=otv, in1=ssum.unsqueeze(2).to_broadcast([128, 4, 128]), op=ALU.mult),
                    [Bot], [BoN])

                def do_tr(h=h, G=G):
                    psb = pbf(0)

                    def tr_o(e):
                        last = None
                        for jq in range(4):
                            last = e.transpose(psb[:, jq * 128:(jq + 1) * 128], oNv[:, jq, :], identb)
                        return last
                    fw.op("pe", tr_o, [BoN, Bidb], [PB[0]])
                    act(lambda e: e.activation(out=mixT[:, h, G * 512:(G + 1) * 512], in_=psb[:, 0:512], func=AF.Identity, scale=gsc),
                        [PB[0], Blc], [Bmix[h][G]])
                nonlocal_pending.append(do_tr)

        for gi, g_ in enumerate(proj_groups(0)):
            g_(gi % 4)
        head_w[1] = load_w(0, head_cols(1))
        SB = (1, 2, 3)
        for h in range(4):
            sl = h % 2
            tasks = []
            for G in range(4):
                for c in range(2):
                    for b in range(4 * G + 4):
                        tasks.append((2 * G + c, G, c, b))
            started = {}
            nxt = []
            n_half = None

            def emit_S(n, sl=sl, tasks=tasks):
                ep, G, c, b = tasks[n]
                i = b - 4 * G
                n0 = max(0, i) * 128
                bk = SB[n % 3]
                ps = banks[bk]

                def mm_s(e):
                    last = e.matmul(ps[:, n0:512], lhsT=kT[sl][:, b * 128:(b + 1) * 128],
                                    rhs=qT[sl][c][:, G * 512 + n0:(G + 1) * 512], start=True, stop=(i < 0))
                    if i >= 0:
                        last = e.matmul(ps[:, n0:n0 + 128], lhsT=identb, rhs=maskb, start=False, stop=True)
                    return last
                fw.op("pe", mm_s, [Bq[sl], Bk[sl], Bidb, Bmkb], [PB[bk]])
            for n_ in range(min(2, len(tasks))):
                emit_S(n_)
            for n, (ep, G, c, b) in enumerate(tasks):
                if n + 2 < len(tasks):
                    emit_S(n + 2)
                aset = (4, 5) if ep % 2 == 0 else (6, 7)
                accv = [banks[a][:, 0:258].rearrange("p (j e) -> p j e", j=2) for a in aset]
                st_ = started.setdefault(ep, [False, False])
                i = b - 4 * G
                n0 = max(0, i) * 128
                bk = SB[n % 3]
                pt = Pt[n % NPT]
                act(lambda e, pt=pt, bk=bk, n0=n0: e.activation(out=pt[:, n0:512], in_=banks[bk][:, n0:512], func=AF.Exp, scale=SC),
                    [PB[bk]], [BPt[n % NPT]])

                def mm_pv(e, b=b, i=i, G=G, pt=pt, sl=sl, accv=accv, st_=st_):
                    last = None
                    for jq in range(max(0, i), 4):
                        a = jq // 2
                        last = e.matmul(accv[a][:, jq % 2, 0:129], lhsT=pt[:, jq * 128:(jq + 1) * 128], rhs=Vh[sl][:, b, 0:129],
                                        start=(not st_[a]), stop=(b == 4 * G + jq), skip_group_check=True)
                        st_[a] = True
                    return last
                fw.op("pe", mm_pv, [BPt[n % NPT], Bv[sl]], [PB[aset[0]], PB[aset[1]]], acc=True)
                if ep >= 4 and h + 1 < 4:
                    if n_half is None:
                        n_half = n
                        nxt = proj_groups(h + 1)
                    if (n - n_half) % 4 == 0 and nxt:
                        nxt.pop(0)(0)
                        if not nxt and h + 2 < 4:
                            head_w[h + 2] = load_w((h + 1) % 2, head_cols(h + 2))
                if b == 4 * G + 3:
                    for fn in pending:
                        fn()
                    del pending[:]
                    finalize(h, G, c, aset, accv)
            while nxt:
                nxt.pop(0)(0)
                if not nxt and h + 2 < 4:
                    head_w[h + 2] = load_w((h + 1) % 2, head_cols(h + 2))
        for fn in pending:
            fn()
        del pending[:]

        if debug == "attn":
            fw.barrier()
            ar.reset(PM_A)
            dbt, Bdb = A(L, F32)
            for k in range(4):
                dve(lambda e, k=k: e.tensor_copy(out=dbt, in_=mixT[:, k, :]), Bmix[k], [Bdb])
                fw.dma("sp", "out", dbg_d[k * 128:(k + 1) * 128, :], dbt, reads=[Bdb])
            fw.barrier()
            return nc

        fw.barrier()
        ar.reset(PM)
        Wdn, BWdn = A(NPAIR * D, BF16)
        Wdn = Wdn.rearrange("p (k n) -> p k n", k=NPAIR)
        actT, _ = A(NPAIR * 512, BF16)
        actT = actT.rearrange("p (k t) -> p k t", k=NPAIR)
        Bact = Buf()
        x1a, _ = A(4 * D, F32)
        x1a = x1a.rearrange("p (b n) -> p b n", b=4)
        Bx1a = [Buf() for _ in range(4)]
        x1T, Bx1T = A(8 * 512, BF16)
        x1T = x1T.rearrange("p (k t) -> p k t", k=8)
        pTt, BpT = A(2 * 512, BF16)
        pTt = pTt.rearrange("p (k t) -> p k t", k=2)
        dgs = []
        for i in range(2):
            t, b = A(6 * 128, BF16)
            dgs.append((t.rearrange("p (j n) -> p j n", j=6), b))
        hss = []
        for i in range(2):
            tg_, bg_ = A(514, BF16)
            tv_, bv_ = A(514, BF16)
            hss.append((tg_, bg_, tv_, bv_))
        tails, Btl = A(44 * 2, BF16)
        tails = tails.rearrange("p (f c) -> p f c", f=44)
        sgt, Bsg = A(512, F32)
        W1, BW1 = A(D, F32)
        W2, BW2 = A(D, F32)
        nbf, Bnbf = A(D, BF16)
        nbf2, Bnbf2 = A(D, BF16)
        nbfs = [nbf, nbf2]
        Bnbfs = [Bnbf, Bnbf2]
        stt, Bst = A(2 * 6, F32)
        mv, _ = A(8, F32)
        mhalf2, Bmh2 = A(1, F32)
        pool = lambda fn, r, w: fw.op("pool", fn, reads=r, writes=w)
        lnv = lnc.rearrange("p (v n) -> p v n", v=4)
        dve(lambda e: e.memset(tails, 0.0), [], [Btl])
        dve(lambda e: e.memset(mhalf2, -0.5), [], [Bmh2])

        def layer_norm_stats(src, Bsrc):
            for hh in range(2):
                dve(lambda e, hh=hh: e.bn_stats(out=stt[:, hh * 6:(hh + 1) * 6], in_=src[:, hh * 512:(hh + 1) * 512]), [Bsrc], [Bst])
            dve(lambda e: e.bn_aggr(out=mv[:, 0:2], in_=stt), [Bst], [Bst])
            dve(lambda e: e.tensor_scalar(out=mv[:, 2:3], in0=mv[:, 1:2], scalar1=LN_EPS, scalar2=None, op0=ALU.add), [Bst], [Bst])
            pool(lambda e: e.tensor_tensor(out=mv[:, 2:3], in0=mv[:, 2:3], in1=mhalf2, op=ALU.pow), [Bst, Bmh2], [Bst])
            dve(lambda e: e.scalar_tensor_tensor(out=mv[:, 3:4], in0=mv[:, 0:1], scalar=-1.0, in1=mv[:, 2:3], op0=ALU.mult, op1=ALU.mult),
                [Bst], [Bst])

        stt2, Bst2 = A(2 * 6, F32)
        mv2, _ = A(8, F32)
        sttA = [A(12, F32)[0] for _ in range(4)]; mvA = [A(8, F32)[0] for _ in range(4)]; BstA = [Buf() for _ in range(4)]
        sttB = [A(12, F32)[0] for _ in range(4)]; mvB = [A(8, F32)[0] for _ in range(4)]; BstB = [Buf() for _ in range(4)]

        def ln_stats(src, Bsrc, stt_, mv_, Bs_):
            for hh in range(2):
                dve(lambda e, hh=hh: e.bn_stats(out=stt_[:, hh * 6:(hh + 1) * 6], in_=src[:, hh * 512:(hh + 1) * 512]), [Bsrc], [Bs_])
            dve(lambda e: e.bn_aggr(out=mv_[:, 0:2], in_=stt_), [Bs_], [Bs_])
            dve(lambda e: e.tensor_scalar(out=mv_[:, 2:3], in0=mv_[:, 1:2], scalar1=LN_EPS, scalar2=None, op0=ALU.add), [Bs_], [Bs_])
            pool(lambda e: e.tensor_tensor(out=mv_[:, 2:3], in0=mv_[:, 2:3], in1=mhalf2, op=ALU.pow), [Bs_, Bmh2], [Bs_])
            dve(lambda e: e.scalar_tensor_tensor(out=mv_[:, 3:4], in0=mv_[:, 0:1], scalar=-1.0, in1=mv_[:, 2:3], op0=ALU.mult, op1=ALU.mult),
                [Bs_], [Bs_])

        mixbank = {}

        def emit_mix(tg, tb, mb=6):
            T = 4 * tg + tb
            mixbank[(tg, tb)] = mb

            def mm_mix(e):
                last = None
                for hf in range(2):
                    for k in range(8):
                        last = e.matmul(banks[mb + hf][:, :], lhsT=mixT[:, k, T * 128:(T + 1) * 128], rhs=WoG[:, k, hf * 512:(hf + 1) * 512],
                                        start=(k == 0), stop=(k == 7))
                return last
            fw.op("pe", mm_mix, [BWoG] + [Bmix[k][tg] for k in range(8)], [PB[mb], PB[mb + 1]])

        def emit_xload(tg, tb):
            T = 4 * tg + tb
            fw.dma("sp", f"x{tb}", x1a[:, tb, :], x_d[T * 128:(T + 1) * 128, :], writes=[Bx1a[tb]])

        def ln1_s1(tg, tb):
            xa = x1a[:, tb, :]
            Bxa = Bx1a[tb]
            mb = mixbank[(tg, tb)]
            for hf in range(2):
                dve(lambda e, hf=hf: e.scalar_tensor_tensor(out=xa[:, hf * 512:(hf + 1) * 512], in0=xa[:, hf * 512:(hf + 1) * 512],
                                                            scalar=ALPHA, in1=banks[mb + hf][:, :], op0=ALU.mult, op1=ALU.add),
                    [Bxa, PB[mb + hf]], [Bxa])
            ln_stats(xa, Bxa, sttA[tb], mvA[tb], BstA[tb])

        def ln1_s2(tg, tb):
            xa = x1a[:, tb, :]
            Bxa = Bx1a[tb]
            mv_ = mvA[tb]
            act(lambda e: e.activation(out=nbfs[tb % 2], in_=xa, func=AF.Identity, scale=mv_[:, 2:3], bias=mv_[:, 3:4]), [Bxa, BstA[tb]], [Bnbfs[tb % 2]])
            act(lambda e: e.activation(out=xa, in_=xa, func=AF.Identity, scale=mv_[:, 2:3], bias=mv_[:, 3:4]), [Bxa, BstA[tb]], [Bxa])

        def ln1_s3(tg, tb):
            xa = x1a[:, tb, :]
            Bxa = Bx1a[tb]
            dve(lambda e: e.tensor_tensor(out=xa, in0=xa, in1=lnv[:, 0, :], op=ALU.mult), [Bxa, Blnc], [Bxa])
            dve(lambda e: e.tensor_tensor(out=xa, in0=xa, in1=lnv[:, 1, :], op=ALU.add), [Bxa, Blnc], [Bxa])

        def emit_head(tg):
            ln1_s1(tg, 0)
            emit_mix(tg, 1)
            ln1_s1(tg, 1)
            ln1_s2(tg, 0)
            emit_mix(tg, 2)
            ln1_s1(tg, 2)
            ln1_s2(tg, 1)
            ln1_s3(tg, 0)
            emit_ln1_tr(tg, 0)
            emit_mix(tg, 3)
            ln1_s1(tg, 3)
            ln1_s2(tg, 2)
            ln1_s3(tg, 1)
            emit_ln1_tr(tg, 1)
            ln1_s2(tg, 3)
            ln1_s3(tg, 2)
            emit_ln1_tr(tg, 2)
            ln1_s3(tg, 3)
            emit_ln1_tr(tg, 3)

        def emit_ln1_tr(tg, tb):
            bk = 4 + tb % 2
            psb = pbf(bk)

            def tr_n(e):
                last = None
                for k in range(8):
                    last = e.transpose(psb[:, k * 128:(k + 1) * 128], nbfs[tb % 2][:, k * 128:(k + 1) * 128], identb)
                return last
            fw.op("pe", tr_n, [Bnbfs[tb % 2], Bidb], [PB[bk]])
            for k in range(8):
                act(lambda e, k=k: e.activation(out=x1T[:, k, tb * 128:(tb + 1) * 128], in_=psb[:, k * 128:(k + 1) * 128],
                                                func=AF.Identity, scale=CB("g1c")[:, k:k + 1], bias=CB("b1c")[:, k:k + 1]),
                    [PB[bk], BcstB], [Bx1T])

        def emit_down(tg, tb):
            db = 2 * (tb % 2)

            def mm_dn(e):
                last = None
                for hf in range(2):
                    for k in range(NPAIR):
                        last = e.matmul(banks[db + hf][:, :], lhsT=actT[:, k, tb * 128:(tb + 1) * 128], rhs=Wdn[:, k, hf * 512:(hf + 1) * 512],
                                        start=(k == 0), stop=(k == NPAIR - 1))
                return last
            fw.op("pe", mm_dn, [Bact, BWdn], [PB[db], PB[db + 1]])

        Wout = [W1, W2]
        BWout = [BW1, BW2]

        def tail_A(tg, tb, thS, BthS):
            xa = x1a[:, tb, :]
            Bxa = Bx1a[tb]
            db = 2 * (tb % 2)
            def mm_pp(e):
                last = None
                for hf in range(2):
                    for k in range(2):
                        last = e.matmul(banks[4 + hf][:, :], lhsT=pTt[:, k, tb * 128:(tb + 1) * 128], rhs=Wple[:, k, hf * 512:(hf + 1) * 512],
                                        start=(k == 0), stop=(k == 1))
                return last
            fw.op("pe", mm_pp, [BpT, BWple], [PB[4], PB[5]])
            for hf in range(2):
                hs_ = slice(hf * 512, (hf + 1) * 512)
                dve(lambda e, hf=hf: e.scalar_tensor_tensor(out=sgt, in0=thS[tb][:, hf, :], scalar=1.0, in1=banks[4 + hf][:, :],
                                                            op0=ALU.add, op1=ALU.mult), [BthS[tb][hf], PB[4 + hf]], [Bsg])
                dve(lambda e, hs_=hs_: e.scalar_tensor_tensor(out=xa[:, hs_], in0=sgt, scalar=0.5, in1=xa[:, hs_], op0=ALU.mult, op1=ALU.add),
                    [Bsg, Bxa], [Bxa])
            for hf in range(2):
                hs_ = slice(hf * 512, (hf + 1) * 512)
                dve(lambda e, hf=hf, hs_=hs_: e.tensor_tensor(out=xa[:, hs_], in0=xa[:, hs_], in1=banks[db + hf][:, :], op=ALU.add),
                    [Bxa, PB[db + hf]], [Bxa])
            ln_stats(xa, Bxa, sttB[tb], mvB[tb], BstB[tb])

        def tail_B(tg, tb, nx):
            xa = x1a[:, tb, :]
            mv_ = mvB[tb]
            wo_, Bwo_ = Wout[tb % 2], BWout[tb % 2]
            act(lambda e: e.activation(out=wo_, in_=xa, func=AF.Identity, scale=mv_[:, 2:3], bias=mv_[:, 3:4]), [Bx1a[tb], BstB[tb]], [Bwo_])
            if nx:
                emit_xload(tg + 1, tb)

        def tail_C(tg, tb):
            T = 4 * tg + tb
            wo_, Bwo_ = Wout[tb % 2], BWout[tb % 2]
            dve(lambda e: e.tensor_tensor(out=wo_, in0=wo_, in1=lnv[:, 2, :], op=ALU.mult), [Bwo_, Blnc], [Bwo_])
            dve(lambda e: e.tensor_tensor(out=wo_, in0=wo_, in1=lnv[:, 3, :], op=ALU.add), [Bwo_, Blnc], [Bwo_])
            fw.dma("sp", f"out{tb % 2}", out_d[T * 128:(T + 1) * 128, :], wo_, reads=[Bwo_])

        def emit_tail_head(tg, thS, BthS):
            g1 = tg + 1
            emit_down(tg, 0)
            emit_down(tg, 1)
            tail_A(tg, 0, thS, BthS)
            emit_down(tg, 2)
            tail_A(tg, 1, thS, BthS)
            tail_B(tg, 0, True)
            tail_C(tg, 0)
            emit_down(tg, 3)
            tail_A(tg, 2, thS, BthS)
            tail_B(tg, 1, True)
            tail_C(tg, 1)
            ln1_s1(g1, 0)
            emit_mix(g1, 1)
            tail_A(tg, 3, thS, BthS)
            tail_B(tg, 2, True)
            tail_C(tg, 2)
            ln1_s1(g1, 1)
            ln1_s2(g1, 0)
            emit_mix(g1, 2, 0)
            tail_B(tg, 3, True)
            tail_C(tg, 3)
            ln1_s1(g1, 2)
            ln1_s2(g1, 1)
            ln1_s3(g1, 0)
            emit_ln1_tr(g1, 0)
            emit_mix(g1, 3, 2)
            ln1_s1(g1, 3)
            ln1_s2(g1, 2)
            ln1_s3(g1, 1)
            emit_ln1_tr(g1, 1)
            ln1_s2(g1, 3)
            ln1_s3(g1, 2)
            emit_ln1_tr(g1, 2)
            ln1_s3(g1, 3)
            emit_ln1_tr(g1, 3)

        def emit_tail(tg, thS, BthS, nx):
            emit_down(tg, 0)
            emit_down(tg, 1)
            tail_A(tg, 0, thS, BthS)
            emit_down(tg, 2)
            tail_A(tg, 1, thS, BthS)
            tail_B(tg, 0, nx)
            tail_C(tg, 0)
            emit_down(tg, 3)
            tail_A(tg, 2, thS, BthS)
            tail_B(tg, 1, nx)
            tail_C(tg, 1)
            tail_A(tg, 3, thS, BthS)
            tail_B(tg, 2, nx)
            tail_C(tg, 2)
            tail_B(tg, 3, nx)
            tail_C(tg, 3)

        for tb in range(4):
            emit_xload(0, tb)
        emit_mix(0, 0)
        emit_head(0)

        for tg in range(4):
            t0 = tg * 512
            fw.dma("pool", "pT", pTt, pT_d[:, t0:t0 + 512].rearrange("(k p) t -> p k t", p=128), writes=[BpT])
            thS = [mixT[:, 2 * tb:2 * tb + 2, t0:t0 + 512] for tb in range(4)]
            BthS = [[Bmix[2 * tb][tg], Bmix[2 * tb + 1][tg]] for tb in range(4)]
            fw.dma("pool", "wog", WoG, wg_d.rearrange("(k p) n -> p k n", p=128), writes=[BWoG])
            if tg == 0:
                for hf in range(2):
                    fw.dma("pool", "wdn", Wdn[:, hf * 11:(hf + 1) * 11, :],
                           wdn_d[hf * 1408:(hf + 1) * 1408, :].rearrange("(k p) n -> p k n", p=128), writes=[BWdn])

            def emit_gate():
                for tb in range(4):
                    def mm_gate(e, tb=tb):
                        last = None
                        for hf in range(2):
                            for k in range(8):
                                last = e.matmul(banks[6 + hf][:, :], lhsT=x1T[:, k, tb * 128:(tb + 1) * 128], rhs=WoG[:, k, hf * 512:(hf + 1) * 512],
                                                start=(k == 0), stop=(k == 7))
                        return last
                    fw.op("pe", mm_gate, [Bx1T, BWoG], [PB[6], PB[7]])
                    for hf in range(2):
                        act(lambda e, hf=hf, tb=tb: e.activation(out=thS[tb][:, hf, :], in_=banks[6 + hf][:, :], func=AF.Tanh, scale=0.5),
                            [PB[6 + hf]], [BthS[tb][hf]])

            def emit_up(i):
                cidx, s2 = i // 2, i % 2
                wt_, Bw_ = wup[cidx % NWUP]
                dg_, Bdg_ = dgs[i % 2]
                for j in range(3):
                    for gv in range(2):
                        fb = i + 22 * gv
                        dve(lambda e, j=j, gv=gv, fb=fb, dg_=dg_: e.tensor_scalar(out=dg_[:, gv * 3 + j, :], in0=identb,
                                                                                  scalar1=CB("cw")[:, fb * 3 + j:fb * 3 + j + 1], scalar2=None,
                                                                                  op0=ALU.mult), [Bidb, BcstB], [Bdg_])
                for gv in range(2):
                    bk = 2 * (i % 2) + gv

                    def mm_up(e, gv=gv, bk=bk, wt_=wt_, s2=s2):
                        last = None
                        c0 = gv * 256 + s2 * 128
                        for k in range(8):
                            last = e.matmul(banks[bk][:, :], lhsT=wt_[:, k, c0:c0 + 128], rhs=x1T[:, k, :], start=(k == 0), stop=(k == 7))
                        return last
                    fw.op("pe", mm_up, [Bw_, Bx1T], [PB[bk]])

            def emit_conv(i):
                hg, Bhg, hv, Bhv = hss[i % 2]
                dg_, Bdg_ = dgs[i % 2]
                for gv, (ht, Bh) in enumerate(((hg, Bhg), (hv, Bhv))):
                    fb = i + 22 * gv
                    bk = 2 * (i % 2) + gv
                    act(lambda e, ht=ht, fb=fb: e.activation(out=ht[:, 0:2], in_=tails[:, fb, :], func=AF.Copy), [Btl], [Bh])
                    if gv == 0:
                        act(lambda e, ht=ht, bk=bk: e.activation(out=ht[:, 2:514], in_=banks[bk][:, :], func=AF.Copy), [PB[bk]], [Bh])
                    else:
                        dve(lambda e, ht=ht, bk=bk: e.tensor_copy(out=ht[:, 2:514], in_=banks[bk][:, :]), [PB[bk]], [Bh])
                    dve(lambda e, ht=ht, fb=fb: e.tensor_copy(out=tails[:, fb, :], in_=ht[:, 512:514]), [Bh], [Btl])

                    def mm_cv(e, gv=gv, ht=ht, dg_=dg_):
                        last = None
                        for j in range(3):
                            last = e.matmul(banks[4 + gv][:, :], lhsT=dg_[:, gv * 3 + j, :], rhs=ht[:, j:j + 512], start=(j == 0), stop=(j == 2))
                        return last
                    fw.op("pe", mm_cv, [Bh, Bdg_], [PB[4 + gv]])
                act(lambda e, i=i: e.activation(out=sgt, in_=banks[4][:, :], func=AF.Silu, bias=CB("cb")[:, i:i + 1]), [PB[4], BcstB], [Bsg])
                dve(lambda e, i=i: e.scalar_tensor_tensor(out=actT[:, i, :], in0=banks[5][:, :], scalar=CB("cb")[:, 22 + i:23 + i], in1=sgt,
                                                          op0=ALU.add, op1=ALU.mult), [PB[5], Bsg, BcstB], [Bact])
            emit_up(0)
            for i in range(NPAIR):
                if i % 2 == 0 and i // 2 + 2 < NCH:
                    emit_wdma(i // 2 + 2)
                if i == 8 and tg + 1 < 4:
                    fw.dma("pool", "wog", WoG, wo_d.rearrange("(k p) n -> p k n", p=128), writes=[BWoG])
                if i + 1 < NPAIR:
                    emit_up(i + 1)
                emit_conv(i)
                if i == 5:
                    emit_gate()
            nx = tg + 1 < 4
            if nx:
                emit_wdma(0)
                emit_wdma(1)
                emit_mix(tg + 1, 0)
            if nx:
                emit_tail_head(tg, thS, BthS)
            else:
                emit_tail(tg, thS, BthS, False)
        fw.barrier()
    return nc


def _prep_shared(inp):
    f = lambda a: np.ascontiguousarray(np.asarray(a, dtype=np.float32))
    cst = np.zeros((128, NCST), np.float32)

    def put(n, a):
        a = np.asarray(a, np.float32).reshape(128, -1)
        assert a.shape[1] == _c[n][1] - _c[n][0], (n, a.shape)
        cst[:, _c[n][0]:_c[n][1]] = a
    lamq = np.stack([inp["diff_lambda_q1"][0], inp["diff_lambda_k1"][0], inp["diff_lambda_q2"][0], inp["diff_lambda_k2"][0]], 0)
    put("lamq", np.broadcast_to(lamq.reshape(1, 256), (128, 256)))
    put("gcol", inp["diff_subln_g"][0].reshape(128, 1))
    gl = lambda a: np.asarray(a).reshape(16, 2, *a.shape[1:])
    put("lr", gl(inp["ssm_lambda_re"][0]).transpose(1, 2, 0).reshape(128, 16))
    put("li", gl(inp["ssm_lambda_im"][0]).transpose(1, 2, 0).reshape(128, 16))
    ldt = np.broadcast_to(np.asarray(inp["ssm_log_dt"][0]).reshape(16, 2, 1), (16, 2, 64))
    put("ldt", ldt.transpose(1, 2, 0).reshape(128, 16))
    put("bre", gl(inp["ssm_b_re"][0]).transpose(1, 2, 0, 3).reshape(128, 256))
    put("bim", gl(inp["ssm_b_im"][0]).transpose(1, 2, 0, 3).reshape(128, 256))
    put("cre", gl(inp["ssm_c_re"][0]).transpose(1, 3, 0, 2).reshape(128, 256))
    put("cim", gl(inp["ssm_c_im"][0]).transpose(1, 3, 0, 2).reshape(128, 256))
    dl = np.broadcast_to(np.asarray(inp["ssm_d"][0]).T.reshape(1, 16, 32), (8, 16, 32))
    put("dl", dl.reshape(128, 32))
    put("bglu", np.asarray(inp["ssm_b_glu"][0]).reshape(4, 128).T)
    put("cw", np.asarray(inp["ffn_conv_w"][0]).reshape(3, 44, 128).transpose(2, 1, 0).reshape(128, 132))
    put("cb", np.asarray(inp["ffn_conv_b"][0]).reshape(44, 128).T)
    put("ident", np.eye(128, dtype=np.float32))
    tk = np.arange(128)[:, None]
    tq = np.arange(128)[None, :]
    put("maskb", np.where(tk > tq, -30000.0, 0.0))
    j = (np.arange(128) // 16)
    put("tmask", (j[:, None] <= j[None, :]).astype(np.float32))
    put("kvec", np.broadcast_to(np.asarray(KV, np.float32).reshape(1, 25), (128, 25)))
    put("cidx", np.broadcast_to(np.arange(128, dtype=np.float32).reshape(1, 128), (128, 128)))
    put("i2", np.concatenate([np.eye(64), np.eye(64)], 0))
    put("g1c", np.asarray(inp["ln1_g"][0]).reshape(8, 128).T)
    put("b1c", np.asarray(inp["ln1_b"][0]).reshape(8, 128).T)
    lnc = np.concatenate([np.broadcast_to(np.asarray(inp[k][0]).reshape(1, D), (128, D))
                          for k in ("ln1_g", "ln1_b", "ln2_g", "ln2_b")], 1)
    return {"w_in": f(inp["w_in"][0]), "w_o": f(inp["w_o"][0]), "w_glu": f(inp["ssm_w_glu"][0]),
            "w_up": f(inp["ffn_w_up"][0]), "w_down": f(inp["ffn_w_down"][0]), "w_ple": f(inp["w_ple"][0]),
            "w_gate": f(inp["w_ple_gate"][0]), "cst": cst, "lnc": f(lnc)}


def make_in_maps(inp):
    shared = _prep_shared(inp)
    x = np.asarray(inp["x"], np.float32)
    p = np.asarray(inp["p"], np.float32)
    maps = []
    for b in range(8):
        m = dict(shared)
        m["x"] = np.ascontiguousarray(x[b])
        m["xT"] = np.ascontiguousarray(x[b].T)
        m["pT"] = np.ascontiguousarray(p[0, b].T)
        maps.append(m)
    return maps


def kernel(**inputs):
    nc = build_nc()
    maps = make_in_maps(inputs)
    res = run_bass_kernel_spmd(nc, maps, core_ids=list(range(8)))
    return np.stack([r["out"] for r in res.results], 0).astype(np.float32)
```

```python
import math
from contextlib import ExitStack
import numpy as np
import concourse.bass as bass
import concourse.mybir as mybir
from concourse.bass_utils import run_bass_kernel_spmd
from concourse.alu_op_type import AluOpType as ALU

F32 = mybir.dt.float32
BF16 = mybir.dt.bfloat16
I32 = mybir.dt.int32
AF = mybir.ActivationFunctionType

L = 2048
D = 1024
DFF = 2816
NPAIR = DFF // 128
LN_EPS = 1e-5
ALPHA = 2.0 ** 0.25
LAM_INIT = 0.8 - 0.6 * math.exp(0.0)
TWO_PI_LO = 6.28318

_c = {}
_off = 0
for _n, _w in (("lamq", 256), ("gcol", 1), ("lr", 16), ("li", 16), ("ldt", 16), ("bre", 256), ("bim", 256),
               ("cre", 256), ("cim", 256), ("dl", 32), ("bglu", 4), ("cw", 132), ("cb", 44),
               ("ident", 128), ("maskb", 128), ("tmask", 128), ("kvec", 25), ("cidx", 128), ("i2", 64), ("g1c", 8), ("b1c", 8)):
    _c[_n] = (_off, _off + _w)
    _off += _w
NCST = _off
KV = [0, -1, -2, -3, -4, -5, -6, -7] + list(range(0, 17))


class Buf:
    __slots__ = ("w", "r", "name")

    def __init__(self, name=""):
        self.w = None
        self.r = {}
        self.name = name


class Eng:
    def __init__(self, name, handle, sem):
        self.name = name
        self.h = handle
        self.sem = sem
        self.count = 0
        self.waited = {}


class FW:
    def __init__(self, nc, stack):
        self.nc = nc
        self.stack = stack
        self.engs = {}
        for name, h in (("pe", nc.tensor), ("act", nc.scalar), ("dve", nc.vector),
                        ("pool", nc.gpsimd), ("sp", nc.sync)):
            sem = stack.enter_context(nc.semaphore("sem_" + name))
            self.engs[name] = Eng(name, h, sem)
        self.dma_sems = {}

    def dma_slot(self, name):
        if name not in self.dma_sems:
            sem = self.stack.enter_context(self.nc.semaphore("dsem_" + name))
            self.dma_sems[name] = [sem, 0]
        return self.dma_sems[name]

    def _wait(self, eng, tok):
        key, sem, val = tok
        if eng.waited.get(key, 0) >= val:
            return
        eng.h.wait_ge(sem, val)
        eng.waited[key] = val

    def _deps(self, eng, reads, writes):
        best = {}

        def add(t):
            if t is None:
                return
            if eng.name == "pe" and t[0] == "pe":
                return
            if t[0] not in best or best[t[0]][2] < t[2]:
                best[t[0]] = t
        for b in reads:
            add(b.w)
        for b in writes:
            add(b.w)
            for k, t in b.r.items():
                add(t)
        for t in best.values():
            self._wait(eng, t)

    def _commit(self, tok, reads, writes):
        for b in reads:
            b.r[tok[0]] = tok
        for b in writes:
            b.w = tok
            b.r = {}

    def op(self, engname, fn, reads=(), writes=(), acc=False):
        eng = self.engs[engname]
        if engname == "pe" and not acc:
            for b in writes:
                if b.w is not None and b.w[0] == "pe" and not b.r:
                    raise RuntimeError(f"PSUM clobber: PE overwrites {b.name} before anyone read the previous PE result")
        self._deps(eng, reads, writes)
        ins = fn(eng.h)
        ins.then_inc(eng.sem, 1)
        eng.count += 1
        tok = (eng.name, eng.sem, eng.count)
        self._commit(tok, reads, writes)
        return tok

    def dma(self, qname, slot, out, in_, reads=(), writes=(), **kw):
        eng = self.engs[qname]
        self._deps(eng, reads, writes)
        s = self.dma_slot(slot)
        ins = eng.h.dma_start(out=out, in_=in_, **kw)
        ins.then_inc(s[0], 16)
        s[1] += 16
        tok = ("dma_" + slot, s[0], s[1])
        self._commit(tok, reads, writes)
        return tok

    def barrier(self):
        toks = [(e.name, e.sem, e.count) for e in self.engs.values() if e.count > 0]
        toks += [("dma_" + k, s[0], s[1]) for k, s in self.dma_sems.items() if s[1] > 0]
        for e in self.engs.values():
            for t in toks:
                if t[0] != e.name:
                    self._wait(e, t)


class Arena:
    def __init__(self, tensor, nbytes):
        self.t = tensor
        self.n = nbytes
        self.off = 0
        self.top = nbytes
        self.marks = []

    def alloc_top(self, cols, dt):
        esz = 4 if dt in (F32, I32) else 2
        nb = (cols * esz + 31) // 32 * 32
        assert self.top - nb >= self.off, ("SBUF arena overflow (top)", self.off, nb, self.top)
        self.top -= nb
        a = self.t[:, self.top // 4:(self.top + nb) // 4]
        if dt != F32:
            a = a.bitcast(dt)
        return a[:, 0:cols]

    def alloc(self, cols, dt):
        esz = 4 if dt in (F32, I32) else 2
        nb = (cols * esz + 31) // 32 * 32
        assert self.off + nb <= self.top, ("SBUF arena overflow", self.off, nb, self.top)
        a = self.t[:, self.off // 4:(self.off + nb) // 4]
        self.off += nb
        if dt != F32:
            a = a.bitcast(dt)
        return a[:, 0:cols]

    def mark(self):
        return self.off

    def reset(self, m):
        self.off = m


def build_nc(debug=None):
    nc = bass.Bass("TRN2", target_bir_lowering=False)
    dram = lambda n, s, k="ExternalInput": nc.dram_tensor(n, s, F32, kind=k).ap()
    xT_d = dram("xT", [D, L])
    x_d = dram("x", [L, D])
    pT_d = dram("pT", [256, L])
    win_d = dram("w_in", [D, 2048])
    wo_d = dram("w_o", [D, D])
    wglu_d = dram("w_glu", [512, 512])
    wup_d = dram("w_up", [D, 2 * DFF])
    wdn_d = dram("w_down", [DFF, D])
    wple_d = dram("w_ple", [256, D])
    wg_d = dram("w_gate", [D, D])
    cst_d = dram("cst", [128, NCST])
    lnc_d = dram("lnc", [128, 4 * D])
    out_d = dram("out", [L, D], "ExternalOutput")
    dbg_d = dram("dbg", [D, L], "ExternalOutput") if debug else None

    with ExitStack() as st:
        fw = FW(nc, st)
        ARENA_BYTES = 212736
        arena_t = st.enter_context(nc.sbuf_tensor("arena", [128, ARENA_BYTES // 4], F32))
        ar = Arena(arena_t, ARENA_BYTES)
        banks = [st.enter_context(nc.psum_tensor(f"bank{i}", [128, 512], F32)) for i in range(8)]
        PB = [Buf(f"bank{i}") for i in range(8)]
        pbf = lambda i: banks[i][:, :].bitcast(BF16)

        def A(cols, dt, name=""):
            return ar.alloc(cols, dt), Buf(name)

        NCB = 192
        cstB, BcstB = A(NCB, F32, "cstB")
        mixT, _ = A(8 * L, BF16, "mixT")
        mixT = mixT.rearrange("p (k t) -> p k t", k=8)
        Bmix = [[Buf(f"mix{k}_{g}") for g in range(4)] for k in range(8)]
        identb, Bidb = A(128, BF16)
        maskb, Bmkb = A(128, BF16)
        fw.dma("sp", "cstB", cstB[:, 0:176], cst_d[:, _c["cw"][0]:_c["cb"][1]], writes=[BcstB])
        fw.dma("sp", "cstB", cstB[:, 176:192], cst_d[:, _c["g1c"][0]:_c["b1c"][1]], writes=[BcstB])
        _cb = {"cw": (0, 132), "cb": (132, 176), "g1c": (176, 184), "b1c": (184, 192)}
        CB = lambda n: cstB[:, _cb[n][0]:_cb[n][1]]
        PM = ar.mark()
        cst, Bcst = A(NCST, F32, "cst")
        fw.dma("sp", "cst", cst, cst_d, writes=[Bcst])
        C = lambda n: cst[:, _c[n][0]:_c[n][1]]
        fw.op("dve", lambda e: e.tensor_copy(out=identb, in_=C("ident")), reads=[Bcst], writes=[Bidb])
        fw.op("dve", lambda e: e.tensor_copy(out=maskb, in_=C("maskb")), reads=[Bcst], writes=[Bmkb])

        xT, _ = A(8 * L, BF16, "xT")
        xT = xT.rearrange("p (k t) -> p k t", k=8)
        BxT = Buf("xT")
        for kb in range(8):
            fw.dma("pool", "xT", xT[:, kb, :], xT_d[kb * 128:(kb + 1) * 128, :], writes=[BxT], max_dma_last_dim=4096)
        wslot = []
        for i in range(2):
            t, b = A(8 * 512, BF16, f"wslot{i}")
            wslot.append((t.rearrange("p (k n) -> p k n", k=8), b))
        wglu, Bwglu = A(4 * 512, BF16, "wglu")
        wglu = wglu.rearrange("p (k n) -> p k n", k=4)
        fw.dma("pool", "wglu", wglu, wglu_d.rearrange("(k p) n -> p k n", p=128), writes=[Bwglu])
        def load_w(slot_i, src_cols_list):
            t, b = wslot[slot_i]
            o = 0
            for (c0, c1) in src_cols_list:
                fw.dma("pool", f"ws{slot_i}", t[:, :, o:o + (c1 - c0)],
                       win_d[:, c0:c1].rearrange("(k p) n -> p k n", p=128), writes=[b])
                o += c1 - c0
            return t, b

        head_cols = lambda h: [(128 * h, 128 * h + 128), (512 + 128 * h, 512 + 128 * h + 128), (1024 + 128 * h, 1024 + 128 * h + 128)]
        wtU, BwU = load_w(0, [(1536, 2048)])
        head_w = {0: load_w(1, head_cols(0))}

        PM_A = ar.mark()
        def T_(cols, name=""):
            return A(cols, F32, name)
        s_dt, Bs = T_(16)
        s_a, _ = T_(16); s_th, _ = T_(16); s_q, _ = T_(16); s_fq, _ = T_(16)
        s_i, _ = A(16, I32); s_n, _ = T_(16)
        V = lambda e: e
        dve = lambda fn, r, w: fw.op("dve", fn, reads=r, writes=w)
        act = lambda fn, r, w: fw.op("act", fn, reads=r, writes=w)
        act(lambda e: e.activation(out=s_dt, in_=C("ldt"), func=AF.Exp), [Bcst], [Bs])
        dve(lambda e: e.tensor_tensor(out=s_a, in0=C("lr"), in1=s_dt, op=ALU.mult), [Bcst, Bs], [Bs])
        dve(lambda e: e.tensor_tensor(out=s_th, in0=C("li"), in1=s_dt, op=ALU.mult), [Bcst, Bs], [Bs])
        dve(lambda e: e.tensor_scalar(out=s_q, in0=s_th, scalar1=1.0 / (2 * math.pi), scalar2=None, op0=ALU.mult), [Bs], [Bs])

        def frac(dst, src, itmp, ftmp, shape_reads):
            dve(lambda e: e.tensor_copy(out=itmp, in_=src), shape_reads, shape_reads)
            dve(lambda e: e.tensor_copy(out=ftmp, in_=itmp), shape_reads, shape_reads)
            dve(lambda e: e.tensor_tensor(out=dst, in0=src, in1=ftmp, op=ALU.subtract), shape_reads, shape_reads)
        frac(s_fq, s_q, s_i, s_n, [Bs])
        NK = 25
        pwr, _ = T_(16 * NK); pwi, _ = T_(16 * NK); ph, _ = T_(16 * NK); ph2, _ = T_(16 * NK)
        phi_, _ = A(16 * NK, I32); ex, _ = T_(16 * NK)
        v3 = lambda t: t.rearrange("p (g k) -> p g k", g=16)
        kv3 = C("kvec").unsqueeze(1).to_broadcast([128, 16, NK])
        bc3 = lambda t, n: t.unsqueeze(2).to_broadcast([128, 16, n])
        dve(lambda e: e.tensor_tensor(out=v3(ph), in0=kv3, in1=bc3(s_fq, NK), op=ALU.mult), [Bcst, Bs], [Bs])
        frac(ph, ph, phi_, ph2, [Bs])
        act(lambda e: e.activation(out=pwi, in_=ph, func=AF.Sin, scale=TWO_PI_LO), [Bs], [Bs])
        dve(lambda e: e.tensor_scalar(out=ph, in0=ph, scalar1=0.25, scalar2=None, op0=ALU.add), [Bs], [Bs])
        frac(ph, ph, phi_, ph2, [Bs])
        act(lambda e: e.activation(out=pwr, in_=ph, func=AF.Sin, scale=TWO_PI_LO), [Bs], [Bs])
        dve(lambda e: e.tensor_tensor(out=v3(ph2), in0=kv3, in1=bc3(s_a, NK), op=ALU.mult), [Bcst, Bs], [Bs])
        act(lambda e: e.activation(out=ex, in_=ph2, func=AF.Exp), [Bs], [Bs])
        dve(lambda e: e.tensor_tensor(out=pwr, in0=pwr, in1=ex, op=ALU.mult), [Bs], [Bs])
        dve(lambda e: e.tensor_tensor(out=pwi, in0=pwi, in1=ex, op=ALU.mult), [Bs], [Bs])
        pwr3, pwi3 = v3(pwr), v3(pwi)
        IDX1 = 9
        den, _ = T_(16); t1, _ = T_(16); t2, _ = T_(16); zr, _ = T_(16); fr, _ = T_(16); fi, _ = T_(16)
        lam_r, lam_i = pwr3[:, :, IDX1], pwi3[:, :, IDX1]
        dve(lambda e: e.tensor_tensor(out=den, in0=C("lr"), in1=C("lr"), op=ALU.mult), [Bcst, Bs], [Bs])
        dve(lambda e: e.tensor_tensor(out=t1, in0=C("li"), in1=C("li"), op=ALU.mult), [Bcst, Bs], [Bs])
        dve(lambda e: e.tensor_tensor(out=den, in0=den, in1=t1, op=ALU.add), [Bs], [Bs])
        dve(lambda e: e.reciprocal(out=den, in_=den), [Bs], [Bs])
        dve(lambda e: e.tensor_scalar(out=zr, in0=lam_r, scalar1=-1.0, scalar2=None, op0=ALU.add), [Bs], [Bs])
        dve(lambda e: e.tensor_tensor(out=t1, in0=zr, in1=C("lr"), op=ALU.mult), [Bcst, Bs], [Bs])
        dve(lambda e: e.tensor_tensor(out=t2, in0=lam_i, in1=C("li"), op=ALU.mult), [Bcst, Bs], [Bs])
        dve(lambda e: e.tensor_tensor(out=t1, in0=t1, in1=t2, op=ALU.add), [Bs], [Bs])
        dve(lambda e: e.tensor_tensor(out=fr, in0=t1, in1=den, op=ALU.mult), [Bs], [Bs])
        dve(lambda e: e.tensor_tensor(out=t1, in0=lam_i, in1=C("lr"), op=ALU.mult), [Bcst, Bs], [Bs])
        dve(lambda e: e.tensor_tensor(out=t2, in0=zr, in1=C("li"), op=ALU.mult), [Bcst, Bs], [Bs])
        dve(lambda e: e.tensor_tensor(out=t1, in0=t1, in1=t2, op=ALU.subtract), [Bs], [Bs])
        dve(lambda e: e.tensor_tensor(out=fi, in0=t1, in1=den, op=ALU.mult), [Bs], [Bs])
        bbr, _ = T_(256); bbi, _ = T_(256); tq1, _ = T_(256)
        g16 = lambda t: t.rearrange("p (g h) -> p g h", g=16)
        dve(lambda e: e.tensor_tensor(out=g16(bbr), in0=g16(C("bre")), in1=bc3(fr, 16), op=ALU.mult), [Bcst, Bs], [Bs])
        dve(lambda e: e.tensor_tensor(out=g16(tq1), in0=g16(C("bim")), in1=bc3(fi, 16), op=ALU.mult), [Bcst, Bs], [Bs])
        dve(lambda e: e.tensor_tensor(out=bbr, in0=bbr, in1=tq1, op=ALU.subtract), [Bs], [Bs])
        dve(lambda e: e.tensor_tensor(out=g16(bbi), in0=g16(C("bim")), in1=bc3(fr, 16), op=ALU.mult), [Bcst, Bs], [Bs])
        dve(lambda e: e.tensor_tensor(out=g16(tq1), in0=g16(C("bre")), in1=bc3(fi, 16), op=ALU.mult), [Bcst, Bs], [Bs])
        dve(lambda e: e.tensor_tensor(out=bbi, in0=bbi, in1=tq1, op=ALU.add), [Bs], [Bs])
        Rcol, _ = T_(16); f16, _ = T_(16)
        act(lambda e: e.activation(out=Rcol, in_=s_a, func=AF.Exp, scale=16.0), [Bs], [Bs])
        dve(lambda e: e.tensor_scalar(out=f16, in0=s_fq, scalar1=16.0, scalar2=None, op0=ALU.mult), [Bs], [Bs])
        frac(f16, f16, s_i, s_n, [Bs])
        zero1, _ = T_(1)
        dve(lambda e: e.memset(zero1, 0.0), [], [Bs])

        NG = 4
        NGR = 2 * NG
        A0r_s = [A(NG * 128, BF16)[0] for _ in range(2)]; A0i_s = [A(NG * 128, BF16)[0] for _ in range(2)]
        Bmr_s = [A(NG * 272, BF16)[0] for _ in range(2)]; Bmi_s = [A(NG * 272, BF16)[0] for _ in range(2)]
        BA0_s = [Buf("A0a"), Buf("A0b")]; BBm_s = [Buf("Bma"), Buf("Bmb")]
        tA, BtA = T_(NG * 272); tB, _ = T_(NG * 272)
        Dg_s = [A(NG * 2 * 3 * 64, BF16)[0] for _ in range(2)]
        BDg_s = [Buf("Dga"), Buf("Dgb")]
        TT, _ = A(NGR * 256, BF16)
        GmR, _ = A(NGR * 2 * 64, BF16); GmI, _ = A(NGR * 2 * 64, BF16)
        cosA, _ = T_(16 * 128); sinA, _ = T_(16 * 128)
        rph, _ = T_(4 * 128); rph2, _ = T_(4 * 128)
        rphi, _ = A(4 * 128, I32)
        SRsh, _ = A(NG * 128, BF16); SIsh, _ = A(NG * 128, BF16)
        l2a, _ = T_(NG * 128); l2b, _ = T_(NG * 128); l2c, _ = T_(NG * 128); l2d, _ = T_(NG * 128)
        Utok, _ = A(NGR * 256, BF16)
        UT, _ = A(NGR * 2 * 128, BF16)
        ygT, _ = A(4 * L, BF16)
        ygT = ygT.rearrange("p (k t) -> p k t", k=4)
        gtmp, _ = T_(512); gtmp2, _ = T_(512)
        BDg, BTT, BGm, Bl2x, BS, Bl2, BU, BUT, Bg = (Buf(n) for n in "Dg TT Gm l2x S l2 U UT g".split())
        BYt = BU
        Bg2 = Buf("g2")
        Byg = [Buf(f"yg{k}") for k in range(4)]
        tA4 = tA.rearrange("p (g t h) -> p g t h", g=NG, t=17)
        tB4 = tB.rearrange("p (g t h) -> p g t h", g=NG, t=17)
        Dg5_s = [d.rearrange("p (g j k n) -> p g j k n", g=NG, j=2, k=3) for d in Dg_s]
        TT3 = TT.rearrange("p (g n) -> p g n", g=NGR)
        GmR4 = GmR.rearrange("p (g j n) -> p g j n", g=NGR, j=2)
        GmI4 = GmI.rearrange("p (g j n) -> p g j n", g=NGR, j=2)
        Brot = Buf("rot")
        for hh in range(4):
            gsl = slice(4 * hh, 4 * hh + 4)
            csl = slice(4 * hh * 128, (4 * hh + 4) * 128)
            cid3 = C("cidx").unsqueeze(1).to_broadcast([128, 4, 128])
            dve(lambda e, gsl=gsl: e.tensor_tensor(out=rph.rearrange("p (g c) -> p g c", g=4), in0=cid3,
                                                   in1=f16[:, gsl].unsqueeze(2).to_broadcast([128, 4, 128]), op=ALU.mult), [Bs, Bcst], [Bl2x])
            frac(rph, rph, rphi, rph2, [Bl2x])
            act(lambda e, csl=csl: e.activation(out=sinA[:, csl], in_=rph, func=AF.Sin, scale=TWO_PI_LO), [Bl2x], [Brot])
            dve(lambda e: e.tensor_scalar(out=rph, in0=rph, scalar1=0.25, scalar2=None, op0=ALU.add), [Bl2x], [Bl2x])
            frac(rph, rph, rphi, rph2, [Bl2x])
            act(lambda e, csl=csl: e.activation(out=cosA[:, csl], in_=rph, func=AF.Sin, scale=TWO_PI_LO), [Bl2x], [Brot])
        SR3 = SRsh.rearrange("p (g c) -> p g c", g=NG)
        SI3 = SIsh.rearrange("p (g c) -> p g c", g=NG)
        Ut4 = Utok.rearrange("p (g t h) -> p g t h", g=NGR, t=16)
        Yt4 = Utok.rearrange("p (t g h) -> p t g h", t=16, g=NGR)
        UT4 = UT.rearrange("p (g j c) -> p g j c", g=NGR, j=2)
        dve(lambda e: e.memset(SRsh, 0.0), [], [BS])
        dve(lambda e: e.memset(SIsh, 0.0), [], [BS])

        def emit_tables(qd):
            gp0 = qd * NG
            gps = slice(gp0, gp0 + NG)
            bcj = lambda t, idx0, n, m: t[:, gps, idx0:idx0 + n].unsqueeze(3).to_broadcast([128, NG, n, m])
            bch = lambda t, n: g16(t)[:, gps, :].unsqueeze(2).to_broadcast([128, NG, n, 16])
            A0r4 = A0r_s[qd % 2].rearrange("p (g j h) -> p g j h", g=NG, j=8)
            A0i4 = A0i_s[qd % 2].rearrange("p (g j h) -> p g j h", g=NG, j=8)
            Bmr4 = Bmr_s[qd % 2].rearrange("p (g t h) -> p g t h", g=NG, t=17)
            Bmi4 = Bmi_s[qd % 2].rearrange("p (g t h) -> p g t h", g=NG, t=17)
            BA0, BBm = BA0_s[qd % 2], BBm_s[qd % 2]
            Dg5 = Dg5_s[qd % 2]
            BDg = BDg_s[qd % 2]
            pl = lambda fn, r, w: fw.op("dve", fn, reads=r, writes=w)
            tA8 = tA[:, 0:NG * 128].rearrange("p (g j h) -> p g j h", g=NG, j=8)
            tB8 = tB[:, 0:NG * 128].rearrange("p (g j h) -> p g j h", g=NG, j=8)
            pl(lambda e: e.tensor_tensor(out=tA8, in0=bcj(pwr3, 0, 8, 16), in1=bch(bbr, 8), op=ALU.mult), [Bs], [BtA])
            pl(lambda e: e.tensor_tensor(out=tB8, in0=bcj(pwi3, 0, 8, 16), in1=bch(bbi, 8), op=ALU.mult), [Bs], [BtA])
            pl(lambda e: e.tensor_tensor(out=A0r4, in0=tA8, in1=tB8, op=ALU.subtract), [BtA], [BA0])
            pl(lambda e: e.tensor_tensor(out=tA8, in0=bcj(pwr3, 0, 8, 16), in1=bch(bbi, 8), op=ALU.mult), [Bs], [BtA])
            pl(lambda e: e.tensor_tensor(out=tB8, in0=bcj(pwi3, 0, 8, 16), in1=bch(bbr, 8), op=ALU.mult), [Bs], [BtA])
            pl(lambda e: e.tensor_tensor(out=A0i4, in0=tA8, in1=tB8, op=ALU.add), [BtA], [BA0])
            cch = lambda n: g16(C(n))[:, gps, :].unsqueeze(2).to_broadcast([128, NG, 17, 16])
            pl(lambda e: e.tensor_tensor(out=tA4, in0=bcj(pwr3, 8, 17, 16), in1=cch("cre"), op=ALU.mult), [Bs, Bcst], [BtA])
            pl(lambda e: e.tensor_tensor(out=tB4, in0=bcj(pwi3, 8, 17, 16), in1=cch("cim"), op=ALU.mult), [Bs, Bcst], [BtA])
            pl(lambda e: e.tensor_tensor(out=Bmr4, in0=tA4, in1=tB4, op=ALU.subtract), [BtA], [BBm])
            pl(lambda e: e.tensor_tensor(out=tA4, in0=bcj(pwr3, 8, 17, 16), in1=cch("cim"), op=ALU.mult), [Bs, Bcst], [BtA])
            pl(lambda e: e.tensor_tensor(out=tB4, in0=bcj(pwi3, 8, 17, 16), in1=cch("cre"), op=ALU.mult), [Bs, Bcst], [BtA])
            pl(lambda e: e.tensor_tensor(out=tA4, in0=tA4, in1=tB4, op=ALU.add), [BtA], [BtA])
            pl(lambda e: e.tensor_scalar(out=Bmi4, in0=tA4, scalar1=-1.0, scalar2=None, op0=ALU.mult), [BtA], [BBm])
            i2b = C("i2").unsqueeze(1).to_broadcast([128, NG, 64])
            for jb in range(2):
                idx = 8 + 15 - 8 * jb
                for kind in range(3):
                    srcb = (pwr3, pwi3, pwi3)[kind][:, gps, idx:idx + 1].to_broadcast([128, NG, 64])
                    dst = Dg5[:, :, jb, kind, :]
                    if kind < 2:
                        dve(lambda e, dst=dst, srcb=srcb: e.tensor_tensor(out=dst, in0=i2b, in1=srcb, op=ALU.mult), [Bs, Bcst], [BDg])
                    else:
                        dve(lambda e, dst=dst, srcb=srcb: e.scalar_tensor_tensor(out=dst, in0=i2b, scalar=-1.0, in1=srcb,
                                                                                 op0=ALU.mult, op1=ALU.mult), [Bs, Bcst], [BDg])
        UCOL0 = 1536
        emit_tables(0)
        for qd in range(16 // NG):
            gp0 = qd * NG
            gps = slice(gp0, gp0 + NG)
            wt, Bw = wtU, BwU
            A0r4 = A0r_s[qd % 2].rearrange("p (g j h) -> p g j h", g=NG, j=8)
            A0i4 = A0i_s[qd % 2].rearrange("p (g j h) -> p g j h", g=NG, j=8)
            Bmr4 = Bmr_s[qd % 2].rearrange("p (g t h) -> p g t h", g=NG, t=17)
            Bmi4 = Bmi_s[qd % 2].rearrange("p (g t h) -> p g t h", g=NG, t=17)
            BA0, BBm = BA0_s[qd % 2], BBm_s[qd % 2]
            Dg5 = Dg5_s[qd % 2]
            BDg = BDg_s[qd % 2]
            cosT = cosA[:, gp0 * 128:(gp0 + NG) * 128]
            sinT = sinA[:, gp0 * 128:(gp0 + NG) * 128]

            for tb4 in range(4):
                bk = 4 + tb4 % 2
                ps = banks[bk]

                def mm_u(e, tb4=tb4, ps=ps):
                    last = None
                    for ti in range(4):
                        tau = tb4 * 4 + ti
                        for kb in range(8):
                            last = e.matmul(ps[:, ti * 128:(ti + 1) * 128], lhsT=xT[:, kb, tau:L:16], rhs=wt[:, kb, 128 * qd:128 * qd + 128],
                                            start=(kb == 0 and ti == 0), stop=(kb == 7), skip_group_check=True)
                    return last
                fw.op("pe", mm_u, [BxT, Bw], [PB[bk]])
                src = ps[:, :].rearrange("p (t g h) -> p t g h", t=4, g=NGR)
                dst = Ut4[:, :, tb4 * 4:tb4 * 4 + 4, :].rearrange("p g t h -> p t g h")
                act(lambda e, src=src, dst=dst: e.activation(out=dst, in_=src, func=AF.Copy), [PB[bk]], [BU])
            for half in range(2):
                bk = 6 + half
                psb = pbf(bk)

                def tr_u(e, half=half, psb=psb):
                    last = None
                    for i in range(8):
                        gl, jb = (half * 8 + i) // 2, (half * 8 + i) % 2
                        last = e.transpose(psb[:, i * 128:(i + 1) * 128],
                                           Ut4[:, gl, jb * 8:(jb + 1) * 8, :].rearrange("p j h -> p (j h)"), identb)
                    return last
                fw.op("pe", tr_u, [BU, Bidb], [PB[bk]])
                dst = UT[:, half * 1024:(half + 1) * 1024]
                act(lambda e, dst=dst, psb=psb: e.activation(out=dst, in_=psb, func=AF.Copy), [PB[bk]], [BUT])
            for pb in range(NGR // 2):
                def mm_toep(e, pb=pb):
                    last = None
                    for k in range(2):
                        gl = 4 * (pb // 2) + (pb % 2) + 2 * k
                        gpl, P0 = gl // 2, 64 * (gl % 2)
                        o = banks[pb][:, k * 256:(k + 1) * 256]
                        e.matmul(o, lhsT=A0r4[P0:P0 + 64, gpl].rearrange("p j h -> p (j h)"),
                                 rhs=Bmr4[P0:P0 + 64, gpl, 0:16].rearrange("p t h -> p (t h)"), start=(k == 0), stop=False, skip_group_check=True)
                        last = e.matmul(o, lhsT=A0i4[P0:P0 + 64, gpl].rearrange("p j h -> p (j h)"),
                                        rhs=Bmi4[P0:P0 + 64, gpl, 0:16].rearrange("p t h -> p (t h)"), start=False, stop=True, skip_group_check=True)
                    return last
                fw.op("pe", mm_toep, [BA0, BBm], [PB[pb]])
            for pb in range(NGR // 2):
                def mm_gm(e, pb=pb):
                    last = None
                    for k in range(2):
                        gl = 4 * (pb // 2) + (pb % 2) + 2 * k
                        gpl, P0 = gl // 2, 64 * (gl % 2)
                        for jb in range(2):
                            l_r = A0r4[P0:P0 + 64, gpl].rearrange("p j h -> p (j h)")
                            l_i = A0i4[P0:P0 + 64, gpl].rearrange("p j h -> p (j h)")
                            d_re = Dg5[P0:P0 + 64, gpl, jb, 0, :]
                            d_im = Dg5[P0:P0 + 64, gpl, jb, 1, :]
                            d_in = Dg5[P0:P0 + 64, gpl, jb, 2, :]
                            c0 = k * 256 + jb * 128
                            o_r = banks[4 + pb][:, c0:c0 + 64]
                            o_i = banks[4 + pb][:, c0 + 64:c0 + 128]
                            first = (jb == 0 and k == 0)
                            e.matmul(o_r, lhsT=l_r, rhs=d_re, start=first, stop=False, skip_group_check=True)
                            e.matmul(o_r, lhsT=l_i, rhs=d_in, start=False, stop=True, skip_group_check=True)
                            e.matmul(o_i, lhsT=l_r, rhs=d_im, start=False, stop=False, skip_group_check=True)
                            last = e.matmul(o_i, lhsT=l_i, rhs=d_re, start=False, stop=True, skip_group_check=True)
                    return last
                fw.op("pe", mm_gm, [BA0, BDg], [PB[4 + pb]])
            for pb in range(NGR // 2):
                ps3 = banks[pb][:, :].rearrange("p (g n) -> p g n", g=2)
                gt_ = (gtmp, gtmp2)[pb % 2]
                Bgt_ = (Bg, Bg2)[pb % 2]
                gt3 = gt_[:, 0:256].rearrange("p (g n) -> p g n", g=2)
                dve(lambda e, ps3=ps3, gt3=gt3: e.tensor_tensor(out=gt3, in0=ps3[:, :, 0:128],
                                                                 in1=C("tmask").unsqueeze(1).to_broadcast([128, 2, 128]), op=ALU.mult),
                    [PB[pb], Bcst], [Bgt_])
                gl0 = 4 * (pb // 2) + (pb % 2)
                for k in range(2):
                    gl = gl0 + 2 * k
                    g = 2 * gp0 + gl
                    dve(lambda e, gl=gl, g=g, k=k, gt3=gt3: e.scalar_tensor_tensor(out=TT3[:, gl, 0:128], in0=C("ident"),
                                                                                   scalar=C("dl")[:, g:g + 1], in1=gt3[:, k, :],
                                                                                   op0=ALU.mult, op1=ALU.add), [Bgt_, Bcst], [BTT])
                act(lambda e, pb=pb, ps3=ps3: e.activation(out=TT3[:, gl0:gl0 + 3:2, 128:256], in_=ps3[:, :, 128:256], func=AF.Copy),
                    [PB[pb]], [BTT])
            for pb in range(NGR // 2):
                gl0 = 4 * (pb // 2) + (pb % 2)
                v = banks[4 + pb][:, :].rearrange("p (g j r n) -> p g j r n", g=2, j=2, r=2)
                act(lambda e, pb=pb, v=v: e.activation(out=GmR4[:, gl0:gl0 + 3:2], in_=v[:, :, :, 0, :], func=AF.Copy), [PB[4 + pb]], [BGm])
                act(lambda e, pb=pb, v=v: e.activation(out=GmI4[:, gl0:gl0 + 3:2], in_=v[:, :, :, 1, :], func=AF.Copy), [PB[4 + pb]], [BGm])

            def mm_G(e):
                last = None
                for ri, (bk, Gm4) in enumerate(((0, GmR4), (1, GmI4))):
                    for gpl in range(NG):
                        for g2 in range(2):
                            gl = 2 * gpl + g2
                            for jb in range(2):
                                last = e.matmul(banks[bk][64 * g2:64 * g2 + 64, gpl * 128:(gpl + 1) * 128],
                                                lhsT=Gm4[:, gl, jb, :], rhs=UT4[:, gl, jb, :],
                                                start=(jb == 0 and gpl == 0), stop=(jb == 1), skip_group_check=True)
                return last
            fw.op("pe", mm_G, [BGm, BUT], [PB[0], PB[1]])
            GRp, GIp = banks[0][:, :], banks[1][:, :]
            dve(lambda e: e.tensor_tensor(out=l2a, in0=GRp, in1=cosT, op=ALU.mult), [PB[0], Brot], [Bl2])
            dve(lambda e: e.tensor_tensor(out=l2b, in0=GIp, in1=sinT, op=ALU.mult), [PB[1], Brot], [Bl2])
            dve(lambda e: e.tensor_tensor(out=l2a, in0=l2a, in1=l2b, op=ALU.add), [Bl2], [Bl2])
            dve(lambda e: e.tensor_tensor(out=l2b, in0=GIp, in1=cosT, op=ALU.mult), [PB[1], Brot], [Bl2])
            dve(lambda e: e.tensor_tensor(out=l2c, in0=GRp, in1=sinT, op=ALU.mult), [PB[0], Brot], [Bl2])
            dve(lambda e: e.tensor_tensor(out=l2b, in0=l2b, in1=l2c, op=ALU.subtract), [Bl2], [Bl2])
            for gpl in range(NG):
                sl = slice(gpl * 128, (gpl + 1) * 128)
                rc = Rcol[:, gp0 + gpl:gp0 + gpl + 1].to_broadcast([128, 128])
                dve(lambda e, sl=sl, rc=rc: e.tensor_tensor_scan(out=l2c[:, sl], data0=rc, data1=l2a[:, sl], initial=0.0,
                                                                 op0=ALU.mult, op1=ALU.add), [Bl2, Bs], [Bl2])
                dve(lambda e, sl=sl, rc=rc: e.tensor_tensor_scan(out=l2d[:, sl], data0=rc, data1=l2b[:, sl], initial=0.0,
                                                                 op0=ALU.mult, op1=ALU.add), [Bl2, Bs], [Bl2])
            dve(lambda e: e.tensor_tensor(out=l2a, in0=l2c, in1=cosT, op=ALU.mult), [Bl2, Brot], [Bl2])
            dve(lambda e: e.tensor_tensor(out=l2b, in0=l2d, in1=sinT, op=ALU.mult), [Bl2, Brot], [Bl2])
            l2a3 = l2a.rearrange("p (g c) -> p g c", g=NG)
            l2b3 = l2b.rearrange("p (g c) -> p g c", g=NG)
            dve(lambda e: e.tensor_tensor(out=SR3[:, :, 1:128], in0=l2a3[:, :, 0:127], in1=l2b3[:, :, 0:127], op=ALU.subtract),
                [Bl2], [BS])
            dve(lambda e: e.tensor_tensor(out=l2a, in0=l2c, in1=sinT, op=ALU.mult), [Bl2, Brot], [Bl2])
            dve(lambda e: e.tensor_tensor(out=l2b, in0=l2d, in1=cosT, op=ALU.mult), [Bl2, Brot], [Bl2])
            dve(lambda e: e.tensor_tensor(out=SI3[:, :, 1:128], in0=l2a3[:, :, 0:127], in1=l2b3[:, :, 0:127], op=ALU.add),
                [Bl2], [BS])
            if qd + 1 < 16 // NG:
                emit_tables(qd + 1)
            for pr in range(NGR // 2):
                bk = 2 + pr % 2
                ps = banks[bk]

                def mm_y(e, pr=pr, ps=ps):
                    last = None
                    for k in range(2):
                        gl = 4 * (pr // 2) + (pr % 2) + 2 * k
                        gpl, g2 = gl // 2, gl % 2
                        P0 = 64 * g2
                        o = ps[:, k * 256:(k + 1) * 256]
                        e.matmul(o, lhsT=UT4[:, gl, 0, :], rhs=TT3[:, gl, 0:256], start=(k == 0), stop=False, skip_group_check=True)
                        e.matmul(o[:, 128:256], lhsT=UT4[:, gl, 1, :], rhs=TT3[:, gl, 0:128], start=False, stop=False, skip_group_check=True)
                        e.matmul(o, lhsT=SR3[P0:P0 + 64, gpl, :], rhs=Bmr4[P0:P0 + 64, gpl, 1:17].rearrange("p t h -> p (t h)"),
                                 start=False, stop=False, skip_group_check=True)
                        last = e.matmul(o, lhsT=SI3[P0:P0 + 64, gpl, :], rhs=Bmi4[P0:P0 + 64, gpl, 1:17].rearrange("p t h -> p (t h)"),
                                        start=False, stop=True, skip_group_check=True)
                    return last
                fw.op("pe", mm_y, [BUT, BTT, BS, BBm], [PB[bk]])
                gl0 = 4 * (pr // 2) + (pr % 2)
                dst = Yt4[:, :, gl0:gl0 + 3:2, :].rearrange("p t g h -> p g t h")
                act(lambda e, ps=ps, dst=dst: e.activation(out=dst, in_=ps[:, :].rearrange("p (g t h) -> p g t h", g=2, t=16),
                                                           func=AF.Gelu_apprx_tanh), [PB[bk]], [BYt])
            for half in range(2):
                bk = 6 + half
                psb = pbf(bk)

                def tr_y(e, half=half, psb=psb):
                    last = None
                    for i in range(8):
                        tau = half * 8 + i
                        last = e.transpose(psb[:, i * 128:(i + 1) * 128], Yt4[:, tau].rearrange("p g h -> p (g h)"), identb)
                    return last
                fw.op("pe", tr_y, [BYt, Bidb], [PB[bk]])
                dst = ygT[:, qd, :].rearrange("p (c t) -> p t c", t=16)[:, half * 8:half * 8 + 8, :]
                src = psb.rearrange("p (t c) -> p t c", t=8)
                act(lambda e, dst=dst, src=src: e.activation(out=dst, in_=src, func=AF.Copy), [PB[bk]], [Byg[qd]])

        hb, Bhb = A(4, F32)
        dve(lambda e: e.tensor_scalar(out=hb, in0=C("bglu"), scalar1=0.5, scalar2=None, op0=ALU.mult), [Bcst], [Bhb])
        gth, Bgth = A(512, BF16)
        for cb in range(4):
            for tg in range(4):
                bk = (cb * 4 + tg) % 2
                ps = banks[bk]

                def mm_glu(e, cb=cb, tg=tg, ps=ps):
                    last = None
                    for kb in range(4):
                        last = e.matmul(ps[:, :], lhsT=wglu[:, kb, cb * 128:(cb + 1) * 128], rhs=ygT[:, kb, tg * 512:(tg + 1) * 512],
                                        start=(kb == 0), stop=(kb == 3))
                    return last
                fw.op("pe", mm_glu, [Bwglu] + Byg, [PB[bk]])
                act(lambda e, cb=cb, ps=ps: e.activation(out=gth, in_=ps[:, :], func=AF.Sigmoid, bias=C("bglu")[:, cb:cb + 1]),
                    [PB[bk], Bcst], [Bgth])
                dve(lambda e, cb=cb, tg=tg: e.tensor_tensor(out=mixT[:, 4 + cb, tg * 512:(tg + 1) * 512], in0=gth,
                                                            in1=ygT[:, cb, tg * 512:(tg + 1) * 512], op=ALU.mult),
                    [Bgth, Byg[cb]], [Bmix[4 + cb][tg]])

        fw.barrier()
        ar.reset(PM_A)
        AT = lambda cols, dt: (ar.alloc_top(cols, dt), Buf())
        lnc, Blnc = AT(4 * D, F32)
        WoG, BWoG = AT(8 * D, BF16)
        WoG = WoG.rearrange("p (k n) -> p k n", k=8)
        Wple, BWple = AT(2 * D, BF16)
        Wple = Wple.rearrange("p (k n) -> p k n", k=2)
        NWUP = 3
        NCH = NPAIR // 2
        wup = []
        for i in range(NWUP):
            t, b = AT(8 * 512, BF16)
            wup.append((t.rearrange("p (k n) -> p k n", k=8), b))

        def emit_wdma(cidx):
            wt_, Bw_ = wup[cidx % NWUP]
            sn = f"wup{cidx % NWUP}"
            fw.dma("pool", sn, wt_[:, :, 0:256], wup_d[:, 256 * cidx:256 * cidx + 256].rearrange("(k p) n -> p k n", p=128), writes=[Bw_])
            fw.dma("pool", sn, wt_[:, :, 256:512],
                   wup_d[:, DFF + 256 * cidx:DFF + 256 * cidx + 256].rearrange("(k p) n -> p k n", p=128), writes=[Bw_])
        fw.dma("sp", "lnc", lnc, lnc_d, writes=[Blnc])
        lnv = lnc.rearrange("p (v n) -> p v n", v=4)
        dve(lambda e: e.tensor_scalar(out=lnv[:, 0:2, :], in0=lnv[:, 0:2, :], scalar1=ALPHA, scalar2=None, op0=ALU.mult), [Blnc], [Blnc])
        fw.dma("pool", "wog", WoG, wo_d.rearrange("(k p) n -> p k n", p=128), writes=[BWoG])
        emit_wdma(0)
        emit_wdma(1)
        fw.dma("pool", "wple", Wple, wple_d.rearrange("(k p) n -> p k n", p=128), writes=[BWple])
        if debug == "ssm":
            dbt, Bdb = A(L, F32)
            for k in range(4):
                dve(lambda e, k=k: e.tensor_copy(out=dbt, in_=mixT[:, 4 + k, :]), Bmix[4 + k], [Bdb])
                fw.dma("sp", "out", dbg_d[k * 128:(k + 1) * 128, :], dbt, reads=[Bdb])
            fw.barrier()
            return nc
        qT = [[A(L, BF16)[0] for _c2 in range(2)] for _ in range(2)]
        kT = [A(L, BF16)[0] for _ in range(2)]
        Vh = [A(16 * 130, BF16)[0].rearrange("p (b e) -> p b e", b=16) for _ in range(2)]
        Bq = [Buf() for _ in range(2)]; Bk = [Buf() for _ in range(2)]; Bv = [Buf() for _ in range(2)]
        NPT = 4
        Pt = [A(512, BF16)[0] for _ in range(NPT)]
        BPt = [Buf() for _ in range(NPT)]
        O1, BO1 = A(512, F32)
        otmp, Bot = A(512, F32)
        osq, _ = A(512, F32)
        oN, BoN = A(512, BF16)
        O1v = O1.rearrange("p (j e) -> p j e", j=4)
        otv = otmp.rearrange("p (j e) -> p j e", j=4)
        osv = osq.rearrange("p (j e) -> p j e", j=4)
        oNv = oN.rearrange("p (j e) -> p j e", j=4)
        lt, Blt = A(64, F32)
        lcol, Blc = A(8, F32)
        rz, Brz = A(4, F32)
        ssum, _ = A(4, F32)
        mhalf, Bmh = A(4, F32)
        for i in range(2):
            dve(lambda e, i=i: e.memset(Vh[i][:, :, 128:130], 1.0), [], [Bv[i]])
            for c2 in range(2):
                dve(lambda e, i=i, c2=c2: e.memset(qT[i][c2], 0.0), [], [Bq[i]])
        dve(lambda e: e.memset(mhalf, -0.5), [], [Bmh])
        lq = C("lamq")
        for i in range(2):
            dve(lambda e, i=i: e.tensor_tensor(out=lt, in0=lq[:, 128 * i:128 * i + 64], in1=lq[:, 128 * i + 64:128 * i + 128], op=ALU.mult),
                [Bcst], [Blt])
            dve(lambda e, i=i: e.tensor_reduce(out=lcol[:, i:i + 1], in_=lt, axis=mybir.AxisListType.X, op=ALU.add), [Blt], [Blc])
        act(lambda e: e.activation(out=lcol[:, 0:2], in_=lcol[:, 0:2], func=AF.Exp), [Blc], [Blc])
        dve(lambda e: e.tensor_tensor(out=lcol[:, 2:3], in0=lcol[:, 1:2], in1=lcol[:, 0:1], op=ALU.subtract), [Blc], [Blc])
        dve(lambda e: e.tensor_scalar(out=lcol[:, 2:3], in0=lcol[:, 2:3], scalar1=-LAM_INIT, scalar2=None, op0=ALU.add), [Blc], [Blc])
        nlam = lcol[:, 2:3]
        dve(lambda e: e.tensor_scalar(out=lcol[:, 3:4], in0=C("gcol"), scalar1=1.0 - LAM_INIT, scalar2=None, op0=ALU.mult), [Bcst, Blc], [Blc])
        gsc = lcol[:, 3:4]

        pending = []
        SC = 1.0 / 8.0

        def proj_groups(h):
            sl = h % 2
            wt, Bw = head_w[h]
            fns = []
            for which, Bd in enumerate((Bq[sl], Bk[sl])):
                for tg in range(4):
                    def grp(bk, which=which, tg=tg, Bd=Bd, wt=wt, Bw=Bw, sl=sl):
                        ps = banks[bk]

                        def mm_p(e):
                            last = None
                            for kb in range(8):
                                last = e.matmul(ps[:, :], lhsT=wt[:, kb, which * 128:(which + 1) * 128], rhs=xT[:, kb, tg * 512:(tg + 1) * 512],
                                                start=(kb == 0), stop=(kb == 7))
                            return last
                        fw.op("pe", mm_p, [Bw, BxT], [PB[bk]])
                        cs = slice(tg * 512, (tg + 1) * 512)
                        if which == 0:
                            act(lambda e: e.activation(out=qT[sl][0][0:64, cs], in_=ps[0:64, :], func=AF.Copy), [PB[bk]], [Bd])
                            dve(lambda e: e.tensor_copy(out=qT[sl][1][64:128, cs], in_=ps[64:128, :]), [PB[bk]], [Bd])
                        else:
                            dve(lambda e: e.tensor_copy(out=kT[sl][:, cs], in_=ps[:, :]), [PB[bk]], [Bd])
                    fns.append(grp)
            for tb4 in range(4):
                def grpv(bk, tb4=tb4, wt=wt, Bw=Bw, sl=sl):
                    ps = banks[bk]

                    def mm_v(e):
                        last = None
                        for ti in range(4):
                            tkb = tb4 * 4 + ti
                            for kb in range(8):
                                last = e.matmul(ps[:, ti * 128:(ti + 1) * 128], lhsT=xT[:, kb, tkb * 128:(tkb + 1) * 128], rhs=wt[:, kb, 256:384],
                                                start=(kb == 0 and ti == 0), stop=(kb == 7), skip_group_check=True)
                        return last
                    fw.op("pe", mm_v, [Bw, BxT], [PB[bk]])
                    src = ps[:, :].rearrange("p (b e) -> p b e", b=4)
                    dst = Vh[sl][:, tb4 * 4:tb4 * 4 + 4, 0:128]
                    dve(lambda e: e.tensor_copy(out=dst, in_=src), [PB[bk]], [Bv[sl]])
                fns.append(grpv)
            return fns

        def finalize(h, G, c, aset, accv):
            nonlocal_pending = pending
            for a in range(2):
                acc = accv[a]
                jsl = slice(2 * a, 2 * a + 2)
                PBa = PB[aset[a]]
                dve(lambda e, acc=acc, a=a: e.reciprocal(out=rz[:, 2 * a:2 * a + 2], in_=acc[:, :, 128]), [PBa], [Brz])
                if c == 0:
                    dve(lambda e, acc=acc, jsl=jsl, a=a: e.tensor_tensor(
                        out=O1v[:, jsl, :], in0=acc[:, :, 0:128],
                        in1=rz[:, 2 * a:2 * a + 2].unsqueeze(2).to_broadcast([128, 2, 128]), op=ALU.mult), [PBa, Brz], [BO1])
                else:
                    dve(lambda e, a=a: e.tensor_scalar(out=rz[:, 2 * a:2 * a + 2], in0=rz[:, 2 * a:2 * a + 2], scalar1=nlam, scalar2=None,
                                                       op0=ALU.mult), [Brz, Blc], [Brz])
                    dve(lambda e, acc=acc, jsl=jsl, a=a: e.tensor_tensor(
                        out=otv[:, jsl, :], in0=acc[:, :, 0:128],
                        in1=rz[:, 2 * a:2 * a + 2].unsqueeze(2).to_broadcast([128, 2, 128]), op=ALU.mult), [PBa, Brz], [Bot])
            if c == 1:
                dve(lambda e: e.tensor_tensor(out=otmp, in0=otmp, in1=O1, op=ALU.add), [Bot, BO1], [Bot])
                dve(lambda e: e.tensor_tensor(out=osq, in0=otmp, in1=otmp, op=ALU.mult), [Bot], [Bot])
                dve(lambda e: e.tensor_reduce(out=ssum, in_=osv, axis=mybir.AxisListType.X, op=ALU.add), [Bot], [Bot])
                dve(lambda e: e.tensor_scalar(out=ssum, in0=ssum, scalar1=1.0 / 128.0, scalar2=LN_EPS, op0=ALU.mult, op1=ALU.add),
                    [Bot], [Bot])
                fw.op("pool", lambda e: e.tensor_tensor(out=ssum, in0=ssum, in1=mhalf, op=ALU.pow), [Bot, Bmh], [Bot])
                dve(lambda e: e.tensor_tensor(out=oNv, in0=otv, in1=ssum.unsqueeze(2).to_broadcast([128, 4, 128]), op=ALU.mult),
                    [Bot], [BoN])

                def do_tr(h=h, G=G):
                    psb = pbf(0)

                    def tr_o(e):
                        last = None
                        for jq in range(4):
                            last = e.transpose(psb[:, jq * 128:(jq + 1) * 128], oNv[:, jq, :], identb)
                        return last
                    fw.op("pe", tr_o, [BoN, Bidb], [PB[0]])
                    act(lambda e: e.activation(out=mixT[:, h, G * 512:(G + 1) * 512], in_=psb[:, 0:512], func=AF.Identity, scale=gsc),
                        [PB[0], Blc], [Bmix[h][G]])
                nonlocal_pending.append(do_tr)

        for gi, g_ in enumerate(proj_groups(0)):
            g_(gi % 4)
        head_w[1] = load_w(0, head_cols(1))
        SB = (1, 2, 3)
        for h in range(4):
            sl = h % 2
            tasks = []
            for G in range(4):
                for c in range(2):
                    for b in range(4 * G + 4):
                        tasks.append((2 * G + c, G, c, b))
            started = {}
            nxt = []
            n_half = None

            def emit_S(n, sl=sl, tasks=tasks):
                ep, G, c, b = tasks[n]
                i = b - 4 * G
                n0 = max(0, i) * 128
                bk = SB[n % 3]
                ps = banks[bk]

                def mm_s(e):
                    last = e.matmul(ps[:, n0:512], lhsT=kT[sl][:, b * 128:(b + 1) * 128],
                                    rhs=qT[sl][c][:, G * 512 + n0:(G + 1) * 512], start=True, stop=(i < 0))
                    if i >= 0:
                        last = e.matmul(ps[:, n0:n0 + 128], lhsT=identb, rhs=maskb, start=False, stop=True)
                    return last
                fw.op("pe", mm_s, [Bq[sl], Bk[sl], Bidb, Bmkb], [PB[bk]])
            for n_ in range(min(2, len(tasks))):
                emit_S(n_)
            for n, (ep, G, c, b) in enumerate(tasks):
                if n + 2 < len(tasks):
                    emit_S(n + 2)
                aset = (4, 5) if ep % 2 == 0 else (6, 7)
                accv = [banks[a][:, 0:258].rearrange("p (j e) -> p j e", j=2) for a in aset]
                st_ = started.setdefault(ep, [False, False])
                i = b - 4 * G
                n0 = max(0, i) * 128
                bk = SB[n % 3]
                pt = Pt[n % NPT]
                act(lambda e, pt=pt, bk=bk, n0=n0: e.activation(out=pt[:, n0:512], in_=banks[bk][:, n0:512], func=AF.Exp, scale=SC),
                    [PB[bk]], [BPt[n % NPT]])

                def mm_pv(e, b=b, i=i, G=G, pt=pt, sl=sl, accv=accv, st_=st_):
                    last = None
                    for jq in range(max(0, i), 4):
                        a = jq // 2
                        last = e.matmul(accv[a][:, jq % 2, 0:129], lhsT=pt[:, jq * 128:(jq + 1) * 128], rhs=Vh[sl][:, b, 0:129],
                                        start=(not st_[a]), stop=(b == 4 * G + jq), skip_group_check=True)
                        st_[a] = True
                    return last
                fw.op("pe", mm_pv, [BPt[n % NPT], Bv[sl]], [PB[aset[0]], PB[aset[1]]], acc=True)
                if ep >= 4 and h + 1 < 4:
                    if n_half is None:
                        n_half = n
                        nxt = proj_groups(h + 1)
                    if (n - n_half) % 4 == 0 and nxt:
                        nxt.pop(0)(0)
                        if not nxt and h + 2 < 4:
                            head_w[h + 2] = load_w((h + 1) % 2, head_cols(h + 2))
                if b == 4 * G + 3:
                    for fn in pending:
                        fn()
                    del pending[:]
                    finalize(h, G, c, aset, accv)
            while nxt:
                nxt.pop(0)(0)
                if not nxt and h + 2 < 4:
                    head_w[h + 2] = load_w((h + 1) % 2, head_cols(h + 2))
        for fn in pending:
            fn()
        del pending[:]

        if debug == "attn":
            fw.barrier()
            ar.reset(PM_A)
            dbt, Bdb = A(L, F32)
            for k in range(4):
                dve(lambda e, k=k: e.tensor_copy(out=dbt, in_=mixT[:, k, :]), Bmix[k], [Bdb])
                fw.dma("sp", "out", dbg_d[k * 128:(k + 1) * 128, :], dbt, reads=[Bdb])
            fw.barrier()
            return nc

        fw.barrier()
        ar.reset(PM)
        Wdn, BWdn = A(NPAIR * D, BF16)
        Wdn = Wdn.rearrange("p (k n) -> p k n", k=NPAIR)
        actT, _ = A(NPAIR * 512, BF16)
        actT = actT.rearrange("p (k t) -> p k t", k=NPAIR)
        Bact = Buf()
        x1a, _ = A(4 * D, F32)
        x1a = x1a.rearrange("p (b n) -> p b n", b=4)
        Bx1a = [Buf() for _ in range(4)]
        x1T, Bx1T = A(8 * 512, BF16)
        x1T = x1T.rearrange("p (k t) -> p k t", k=8)
        pTt, BpT = A(2 * 512, BF16)
        pTt = pTt.rearrange("p (k t) -> p k t", k=2)
        dgs = []
        for i in range(2):
            t, b = A(6 * 128, BF16)
            dgs.append((t.rearrange("p (j n) -> p j n", j=6), b))
        hss = []
        for i in range(2):
            tg_, bg_ = A(514, BF16)
            tv_, bv_ = A(514, BF16)
            hss.append((tg_, bg_, tv_, bv_))
        tails, Btl = A(44 * 2, BF16)
        tails = tails.rearrange("p (f c) -> p f c", f=44)
        sgt, Bsg = A(512, F32)
        W1, BW1 = A(D, F32)
        W2, BW2 = A(D, F32)
        nbf, Bnbf = A(D, BF16)
        nbf2, Bnbf2 = A(D, BF16)
        nbfs = [nbf, nbf2]
        Bnbfs = [Bnbf, Bnbf2]
        stt, Bst = A(2 * 6, F32)
        mv, _ = A(8, F32)
        mhalf2, Bmh2 = A(1, F32)
        pool = lambda fn, r, w: fw.op("pool", fn, reads=r, writes=w)
        lnv = lnc.rearrange("p (v n) -> p v n", v=4)
        dve(lambda e: e.memset(tails, 0.0), [], [Btl])
        dve(lambda e: e.memset(mhalf2, -0.5), [], [Bmh2])

        def layer_norm_stats(src, Bsrc):
            for hh in range(2):
                dve(lambda e, hh=hh: e.bn_stats(out=stt[:, hh * 6:(hh + 1) * 6], in_=src[:, hh * 512:(hh + 1) * 512]), [Bsrc], [Bst])
            dve(lambda e: e.bn_aggr(out=mv[:, 0:2], in_=stt), [Bst], [Bst])
            dve(lambda e: e.tensor_scalar(out=mv[:, 2:3], in0=mv[:, 1:2], scalar1=LN_EPS, scalar2=None, op0=ALU.add), [Bst], [Bst])
            pool(lambda e: e.tensor_tensor(out=mv[:, 2:3], in0=mv[:, 2:3], in1=mhalf2, op=ALU.pow), [Bst, Bmh2], [Bst])
            dve(lambda e: e.scalar_tensor_tensor(out=mv[:, 3:4], in0=mv[:, 0:1], scalar=-1.0, in1=mv[:, 2:3], op0=ALU.mult, op1=ALU.mult),
                [Bst], [Bst])

        stt2, Bst2 = A(2 * 6, F32)
        mv2, _ = A(8, F32)
        sttA = [A(12, F32)[0] for _ in range(4)]; mvA = [A(8, F32)[0] for _ in range(4)]; BstA = [Buf() for _ in range(4)]
        sttB = [A(12, F32)[0] for _ in range(4)]; mvB = [A(8, F32)[0] for _ in range(4)]; BstB = [Buf() for _ in range(4)]

        def ln_stats(src, Bsrc, stt_, mv_, Bs_):
            for hh in range(2):
                dve(lambda e, hh=hh: e.bn_stats(out=stt_[:, hh * 6:(hh + 1) * 6], in_=src[:, hh * 512:(hh + 1) * 512]), [Bsrc], [Bs_])
            dve(lambda e: e.bn_aggr(out=mv_[:, 0:2], in_=stt_), [Bs_], [Bs_])
            dve(lambda e: e.tensor_scalar(out=mv_[:, 2:3], in0=mv_[:, 1:2], scalar1=LN_EPS, scalar2=None, op0=ALU.add), [Bs_], [Bs_])
            pool(lambda e: e.tensor_tensor(out=mv_[:, 2:3], in0=mv_[:, 2:3], in1=mhalf2, op=ALU.pow), [Bs_, Bmh2], [Bs_])
            dve(lambda e: e.scalar_tensor_tensor(out=mv_[:, 3:4], in0=mv_[:, 0:1], scalar=-1.0, in1=mv_[:, 2:3], op0=ALU.mult, op1=ALU.mult),
                [Bs_], [Bs_])

        mixbank = {}

        def emit_mix(tg, tb, mb=6):
            T = 4 * tg + tb
            mixbank[(tg, tb)] = mb

            def mm_mix(e):
                last = None
                for hf in range(2):
                    for k in range(8):
                        last = e.matmul(banks[mb + hf][:, :], lhsT=mixT[:, k, T * 128:(T + 1) * 128], rhs=WoG[:, k, hf * 512:(hf + 1) * 512],
                                        start=(k == 0), stop=(k == 7))
                return last
            fw.op("pe", mm_mix, [BWoG] + [Bmix[k][tg] for k in range(8)], [PB[mb], PB[mb + 1]])

        def emit_xload(tg, tb):
            T = 4 * tg + tb
            fw.dma("sp", f"x{tb}", x1a[:, tb, :], x_d[T * 128:(T + 1) * 128, :], writes=[Bx1a[tb]])

        def ln1_s1(tg, tb):
            xa = x1a[:, tb, :]
            Bxa = Bx1a[tb]
            mb = mixbank[(tg, tb)]
            for hf in range(2):
                dve(lambda e, hf=hf: e.scalar_tensor_tensor(out=xa[:, hf * 512:(hf + 1) * 512], in0=xa[:, hf * 512:(hf + 1) * 512],
                                                            scalar=ALPHA, in1=banks[mb + hf][:, :], op0=ALU.mult, op1=ALU.add),
                    [Bxa, PB[mb + hf]], [Bxa])
            ln_stats(xa, Bxa, sttA[tb], mvA[tb], BstA[tb])

        def ln1_s2(tg, tb):
            xa = x1a[:, tb, :]
            Bxa = Bx1a[tb]
            mv_ = mvA[tb]
            act(lambda e: e.activation(out=nbfs[tb % 2], in_=xa, func=AF.Identity, scale=mv_[:, 2:3], bias=mv_[:, 3:4]), [Bxa, BstA[tb]], [Bnbfs[tb % 2]])
            act(lambda e: e.activation(out=xa, in_=xa, func=AF.Identity, scale=mv_[:, 2:3], bias=mv_[:, 3:4]), [Bxa, BstA[tb]], [Bxa])

        def ln1_s3(tg, tb):
            xa = x1a[:, tb, :]
            Bxa = Bx1a[tb]
            dve(lambda e: e.tensor_tensor(out=xa, in0=xa, in1=lnv[:, 0, :], op=ALU.mult), [Bxa, Blnc], [Bxa])
            dve(lambda e: e.tensor_tensor(out=xa, in0=xa, in1=lnv[:, 1, :], op=ALU.add), [Bxa, Blnc], [Bxa])

        def emit_head(tg):
            ln1_s1(tg, 0)
            emit_mix(tg, 1)
            ln1_s1(tg, 1)
            ln1_s2(tg, 0)
            emit_mix(tg, 2)
            ln1_s1(tg, 2)
            ln1_s2(tg, 1)
            ln1_s3(tg, 0)
            emit_ln1_tr(tg, 0)
            emit_mix(tg, 3)
            ln1_s1(tg, 3)
            ln1_s2(tg, 2)
            ln1_s3(tg, 1)
            emit_ln1_tr(tg, 1)
            ln1_s2(tg, 3)
            ln1_s3(tg, 2)
            emit_ln1_tr(tg, 2)
            ln1_s3(tg, 3)
            emit_ln1_tr(tg, 3)

        def emit_ln1_tr(tg, tb):
            bk = 4 + tb % 2
            psb = pbf(bk)

            def tr_n(e):
                last = None
                for k in range(8):
                    last = e.transpose(psb[:, k * 128:(k + 1) * 128], nbfs[tb % 2][:, k * 128:(k + 1) * 128], identb)
                return last
            fw.op("pe", tr_n, [Bnbfs[tb % 2], Bidb], [PB[bk]])
            for k in range(8):
                act(lambda e, k=k: e.activation(out=x1T[:, k, tb * 128:(tb + 1) * 128], in_=psb[:, k * 128:(k + 1) * 128],
                                                func=AF.Identity, scale=CB("g1c")[:, k:k + 1], bias=CB("b1c")[:, k:k + 1]),
                    [PB[bk], BcstB], [Bx1T])

        def emit_down(tg, tb):
            db = 2 * (tb % 2)

            def mm_dn(e):
                last = None
                for hf in range(2):
                    for k in range(NPAIR):
                        last = e.matmul(banks[db + hf][:, :], lhsT=actT[:, k, tb * 128:(tb + 1) * 128], rhs=Wdn[:, k, hf * 512:(hf + 1) * 512],
                                        start=(k == 0), stop=(k == NPAIR - 1))
                return last
            fw.op("pe", mm_dn, [Bact, BWdn], [PB[db], PB[db + 1]])

        Wout = [W1, W2]
        BWout = [BW1, BW2]

        def tail_A(tg, tb, thS, BthS):
            xa = x1a[:, tb, :]
            Bxa = Bx1a[tb]
            db = 2 * (tb % 2)
            def mm_pp(e):
                last = None
                for hf in range(2):
                    for k in range(2):
                        last = e.matmul(banks[4 + hf][:, :], lhsT=pTt[:, k, tb * 128:(tb + 1) * 128], rhs=Wple[:, k, hf * 512:(hf + 1) * 512],
                                        start=(k == 0), stop=(k == 1))
                return last
            fw.op("pe", mm_pp, [BpT, BWple], [PB[4], PB[5]])
            for hf in range(2):
                hs_ = slice(hf * 512, (hf + 1) * 512)
                dve(lambda e, hf=hf: e.scalar_tensor_tensor(out=sgt, in0=thS[tb][:, hf, :], scalar=1.0, in1=banks[4 + hf][:, :],
                                                            op0=ALU.add, op1=ALU.mult), [BthS[tb][hf], PB[4 + hf]], [Bsg])
                dve(lambda e, hs_=hs_: e.scalar_tensor_tensor(out=xa[:, hs_], in0=sgt, scalar=0.5, in1=xa[:, hs_], op0=ALU.mult, op1=ALU.add),
                    [Bsg, Bxa], [Bxa])
            for hf in range(2):
                hs_ = slice(hf * 512, (hf + 1) * 512)
                dve(lambda e, hf=hf, hs_=hs_: e.tensor_tensor(out=xa[:, hs_], in0=xa[:, hs_], in1=banks[db + hf][:, :], op=ALU.add),
                    [Bxa, PB[db + hf]], [Bxa])
            ln_stats(xa, Bxa, sttB[tb], mvB[tb], BstB[tb])

        def tail_B(tg, tb, nx):
            xa = x1a[:, tb, :]
            mv_ = mvB[tb]
            wo_, Bwo_ = Wout[tb % 2], BWout[tb % 2]
            act(lambda e: e.activation(out=wo_, in_=xa, func=AF.Identity, scale=mv_[:, 2:3], bias=mv_[:, 3:4]), [Bx1a[tb], BstB[tb]], [Bwo_])
            if nx:
                emit_xload(tg + 1, tb)

        def tail_C(tg, tb):
            T = 4 * tg + tb
            wo_, Bwo_ = Wout[tb % 2], BWout[tb % 2]
            dve(lambda e: e.tensor_tensor(out=wo_, in0=wo_, in1=lnv[:, 2, :], op=ALU.mult), [Bwo_, Blnc], [Bwo_])
            dve(lambda e: e.tensor_tensor(out=wo_, in0=wo_, in1=lnv[:, 3, :], op=ALU.add), [Bwo_, Blnc], [Bwo_])
            fw.dma("sp", f"out{tb % 2}", out_d[T * 128:(T + 1) * 128, :], wo_, reads=[Bwo_])

        def emit_tail_head(tg, thS, BthS):
            g1 = tg + 1
            emit_down(tg, 0)
            emit_down(tg, 1)
            tail_A(tg, 0, thS, BthS)
            emit_down(tg, 2)
            tail_A(tg, 1, thS, BthS)
            tail_B(tg, 0, True)
            tail_C(tg, 0)
            emit_down(tg, 3)
            tail_A(tg, 2, thS, BthS)
            tail_B(tg, 1, True)
            tail_C(tg, 1)
            ln1_s1(g1, 0)
            emit_mix(g1, 1)
            tail_A(tg, 3, thS, BthS)
            tail_B(tg, 2, True)
            tail_C(tg, 2)
            ln1_s1(g1, 1)
            ln1_s2(g1, 0)
            emit_mix(g1, 2, 0)
            tail_B(tg, 3, True)
            tail_C(tg, 3)
            ln1_s1(g1, 2)
            ln1_s2(g1, 1)
            ln1_s3(g1, 0)
            emit_ln1_tr(g1, 0)
            emit_mix(g1, 3, 2)
            ln1_s1(g1, 3)
            ln1_s2(g1, 2)
            ln1_s3(g1, 1)
            emit_ln1_tr(g1, 1)
            ln1_s2(g1, 3)
            ln1_s3(g1, 2)
            emit_ln1_tr(g1, 2)
            ln1_s3(g1, 3)
            emit_ln1_tr(g1, 3)

        def emit_tail(tg, thS, BthS, nx):
            emit_down(tg, 0)
            emit_down(tg, 1)
            tail_A(tg, 0, thS, BthS)
            emit_down(tg, 2)
            tail_A(tg, 1, thS, BthS)
            tail_B(tg, 0, nx)
            tail_C(tg, 0)
            emit_down(tg, 3)
            tail_A(tg, 2, thS, BthS)
            tail_B(tg, 1, nx)
            tail_C(tg, 1)
            tail_A(tg, 3, thS, BthS)
            tail_B(tg, 2, nx)
            tail_C(tg, 2)
            tail_B(tg, 3, nx)
            tail_C(tg, 3)

        for tb in range(4):
            emit_xload(0, tb)
        emit_mix(0, 0)
        emit_head(0)

        for tg in range(4):
            t0 = tg * 512
            fw.dma("pool", "pT", pTt, pT_d[:, t0:t0 + 512].rearrange("(k p) t -> p k t", p=128), writes=[BpT])
            thS = [mixT[:, 2 * tb:2 * tb + 2, t0:t0 + 512] for tb in range(4)]
            BthS = [[Bmix[2 * tb][tg], Bmix[2 * tb + 1][tg]] for tb in range(4)]
            fw.dma("pool", "wog", WoG, wg_d.rearrange("(k p) n -> p k n", p=128), writes=[BWoG])
            if tg == 0:
                for hf in range(2):
                    fw.dma("pool", "wdn", Wdn[:, hf * 11:(hf + 1) * 11, :],
                           wdn_d[hf * 1408:(hf + 1) * 1408, :].rearrange("(k p) n -> p k n", p=128), writes=[BWdn])

            def emit_gate():
                for tb in range(4):
                    def mm_gate(e, tb=tb):
                        last = None
                        for hf in range(2):
                            for k in range(8):
                                last = e.matmul(banks[6 + hf][:, :], lhsT=x1T[:, k, tb * 128:(tb + 1) * 128], rhs=WoG[:, k, hf * 512:(hf + 1) * 512],
                                                start=(k == 0), stop=(k == 7))
                        return last
                    fw.op("pe", mm_gate, [Bx1T, BWoG], [PB[6], PB[7]])
                    for hf in range(2):
                        act(lambda e, hf=hf, tb=tb: e.activation(out=thS[tb][:, hf, :], in_=banks[6 + hf][:, :], func=AF.Tanh, scale=0.5),
                            [PB[6 + hf]], [BthS[tb][hf]])

            def emit_up(i):
                cidx, s2 = i // 2, i % 2
                wt_, Bw_ = wup[cidx % NWUP]
                dg_, Bdg_ = dgs[i % 2]
                for j in range(3):
                    for gv in range(2):
                        fb = i + 22 * gv
                        dve(lambda e, j=j, gv=gv, fb=fb, dg_=dg_: e.tensor_scalar(out=dg_[:, gv * 3 + j, :], in0=identb,
                                                                                  scalar1=CB("cw")[:, fb * 3 + j:fb * 3 + j + 1], scalar2=None,
                                                                                  op0=ALU.mult), [Bidb, BcstB], [Bdg_])
                for gv in range(2):
                    bk = 2 * (i % 2) + gv

                    def mm_up(e, gv=gv, bk=bk, wt_=wt_, s2=s2):
                        last = None
                        c0 = gv * 256 + s2 * 128
                        for k in range(8):
                            last = e.matmul(banks[bk][:, :], lhsT=wt_[:, k, c0:c0 + 128], rhs=x1T[:, k, :], start=(k == 0), stop=(k == 7))
                        return last
                    fw.op("pe", mm_up, [Bw_, Bx1T], [PB[bk]])

            def emit_conv(i):
                hg, Bhg, hv, Bhv = hss[i % 2]
                dg_, Bdg_ = dgs[i % 2]
                for gv, (ht, Bh) in enumerate(((hg, Bhg), (hv, Bhv))):
                    fb = i + 22 * gv
                    bk = 2 * (i % 2) + gv
                    act(lambda e, ht=ht, fb=fb: e.activation(out=ht[:, 0:2], in_=tails[:, fb, :], func=AF.Copy), [Btl], [Bh])
                    if gv == 0:
                        act(lambda e, ht=ht, bk=bk: e.activation(out=ht[:, 2:514], in_=banks[bk][:, :], func=AF.Copy), [PB[bk]], [Bh])
                    else:
                        dve(lambda e, ht=ht, bk=bk: e.tensor_copy(out=ht[:, 2:514], in_=banks[bk][:, :]), [PB[bk]], [Bh])
                    dve(lambda e, ht=ht, fb=fb: e.tensor_copy(out=tails[:, fb, :], in_=ht[:, 512:514]), [Bh], [Btl])

                    def mm_cv(e, gv=gv, ht=ht, dg_=dg_):
                        last = None
                        for j in range(3):
                            last = e.matmul(banks[4 + gv][:, :], lhsT=dg_[:, gv * 3 + j, :], rhs=ht[:, j:j + 512], start=(j == 0), stop=(j == 2))
                        return last
                    fw.op("pe", mm_cv, [Bh, Bdg_], [PB[4 + gv]])
                act(lambda e, i=i: e.activation(out=sgt, in_=banks[4][:, :], func=AF.Silu, bias=CB("cb")[:, i:i + 1]), [PB[4], BcstB], [Bsg])
                dve(lambda e, i=i: e.scalar_tensor_tensor(out=actT[:, i, :], in0=banks[5][:, :], scalar=CB("cb")[:, 22 + i:23 + i], in1=sgt,
                                                          op0=ALU.add, op1=ALU.mult), [PB[5], Bsg, BcstB], [Bact])
            emit_up(0)
            for i in range(NPAIR):
                if i % 2 == 0 and i // 2 + 2 < NCH:
                    emit_wdma(i // 2 + 2)
                if i == 8 and tg + 1 < 4:
                    fw.dma("pool", "wog", WoG, wo_d.rearrange("(k p) n -> p k n", p=128), writes=[BWoG])
                if i + 1 < NPAIR:
                    emit_up(i + 1)
                emit_conv(i)
                if i == 5:
                    emit_gate()
            nx = tg + 1 < 4
            if nx:
                emit_wdma(0)
                emit_wdma(1)
                emit_mix(tg + 1, 0)
            if nx:
                emit_tail_head(tg, thS, BthS)
            else:
                emit_tail(tg, thS, BthS, False)
        fw.barrier()
    return nc


def _prep_shared(inp):
    f = lambda a: np.ascontiguousarray(np.asarray(a, dtype=np.float32))
    cst = np.zeros((128, NCST), np.float32)

    def put(n, a):
        a = np.asarray(a, np.float32).reshape(128, -1)
        assert a.shape[1] == _c[n][1] - _c[n][0], (n, a.shape)
        cst[:, _c[n][0]:_c[n][1]] = a
    lamq = np.stack([inp["diff_lambda_q1"][0], inp["diff_lambda_k1"][0], inp["diff_lambda_q2"][0], inp["diff_lambda_k2"][0]], 0)
    put("lamq", np.broadcast_to(lamq.reshape(1, 256), (128, 256)))
    put("gcol", inp["diff_subln_g"][0].reshape(128, 1))
    gl = lambda a: np.asarray(a).reshape(16, 2, *a.shape[1:])
    put("lr", gl(inp["ssm_lambda_re"][0]).transpose(1, 2, 0).reshape(128, 16))
    put("li", gl(inp["ssm_lambda_im"][0]).transpose(1, 2, 0).reshape(128, 16))
    ldt = np.broadcast_to(np.asarray(inp["ssm_log_dt"][0]).reshape(16, 2, 1), (16, 2, 64))
    put("ldt", ldt.transpose(1, 2, 0).reshape(128, 16))
    put("bre", gl(inp["ssm_b_re"][0]).transpose(1, 2, 0, 3).reshape(128, 256))
    put("bim", gl(inp["ssm_b_im"][0]).transpose(1, 2, 0, 3).reshape(128, 256))
    put("cre", gl(inp["ssm_c_re"][0]).transpose(1, 3, 0, 2).reshape(128, 256))
    put("cim", gl(inp["ssm_c_im"][0]).transpose(1, 3, 0, 2).reshape(128, 256))
    dl = np.broadcast_to(np.asarray(inp["ssm_d"][0]).T.reshape(1, 16, 32), (8, 16, 32))
    put("dl", dl.reshape(128, 32))
    put("bglu", np.asarray(inp["ssm_b_glu"][0]).reshape(4, 128).T)
    put("cw", np.asarray(inp["ffn_conv_w"][0]).reshape(3, 44, 128).transpose(2, 1, 0).reshape(128, 132))
    put("cb", np.asarray(inp["ffn_conv_b"][0]).reshape(44, 128).T)
    put("ident", np.eye(128, dtype=np.float32))
    tk = np.arange(128)[:, None]
    tq = np.arange(128)[None, :]
    put("maskb", np.where(tk > tq, -30000.0, 0.0))
    j = (np.arange(128) // 16)
    put("tmask", (j[:, None] <= j[None, :]).astype(np.float32))
    put("kvec", np.broadcast_to(np.asarray(KV, np.float32).reshape(1, 25), (128, 25)))
    put("cidx", np.broadcast_to(np.arange(128, dtype=np.float32).reshape(1, 128), (128, 128)))
    put("i2", np.concatenate([np.eye(64), np.eye(64)], 0))
    put("g1c", np.asarray(inp["ln1_g"][0]).reshape(8, 128).T)
    put("b1c", np.asarray(inp["ln1_b"][0]).reshape(8, 128).T)
    lnc = np.concatenate([np.broadcast_to(np.asarray(inp[k][0]).reshape(1, D), (128, D))
                          for k in ("ln1_g", "ln1_b", "ln2_g", "ln2_b")], 1)
    return {"w_in": f(inp["w_in"][0]), "w_o": f(inp["w_o"][0]), "w_glu": f(inp["ssm_w_glu"][0]),
            "w_up": f(inp["ffn_w_up"][0]), "w_down": f(inp["ffn_w_down"][0]), "w_ple": f(inp["w_ple"][0]),
            "w_gate": f(inp["w_ple_gate"][0]), "cst": cst, "lnc": f(lnc)}


def make_in_maps(inp):
    shared = _prep_shared(inp)
    x = np.asarray(inp["x"], np.float32)
    p = np.asarray(inp["p"], np.float32)
    maps = []
    for b in range(8):
        m = dict(shared)
        m["x"] = np.ascontiguousarray(x[b])
        m["xT"] = np.ascontiguousarray(x[b].T)
        m["pT"] = np.ascontiguousarray(p[0, b].T)
        maps.append(m)
    return maps


def kernel(**inputs):
    nc = build_nc()
    maps = make_in_maps(inputs)
    res = run_bass_kernel_spmd(nc, maps, core_ids=list(range(8)))
    return np.stack([r["out"] for r in res.results], 0).astype(np.float32)
```
